# Optimizing a Trainium2 kernel written in Bass

```python
import math
import jax, jax.numpy as jnp
from jax import lax
import numpy as np

D_MODEL = 1024
BATCH = 8
SEQ = 4096
DEPTH = 2
DEC_BATCH = 8
DEC_SEQ = 16
PAST_LEN = 1024

CHUNK = 64
QBLK = 128
EPS = 1e-6
NEG = -1e30
H_A = 4
DH_A = 64
DV_A = 2 * DH_A
H_B = 8
DH_B = 64
BAND_CHUNKS = 8
REL_CLIP = 128
H_C = 4
DH_C = 128
D_C = H_C * DH_C
CONV_W = 4
N_MEM = 256
H_M = 4
DH_M = D_MODEL // H_M
D_FF = -(-8 * D_MODEL // (3 * 256)) * 256
IN_SIZES = (H_A * DH_A,) * 4 + (H_A * DV_A,) + (H_B * DH_B,) * 3 + (2 * D_C, D_C, D_C, 2 * H_C, 3 * D_MODEL)
N_IN = sum(IN_SIZES)

kernel_name = 'hybrid_streaming_encoder_step'


def rmsnorm(x, g):
    xf = x.astype(jnp.float32)
    y = xf * lax.rsqrt(jnp.mean(xf * xf, axis=-1, keepdims=True) + EPS)
    return (y * g.astype(jnp.float32)).astype(x.dtype)


def alibi_slopes():
    return 2.0 ** (-8.0 * jnp.arange(1, H_A + 1, dtype=jnp.float32) / H_A)


def diff_core(q1, q2, k1, k2, v, qpos, kpos, lam):
    scale = DH_A ** -0.5
    dist = jnp.abs(qpos[:, None] - kpos[None, :]).astype(jnp.float32)
    bias = -alibi_slopes()[:, None, None] * dist[None]
    allowed = (kpos[None, :] // CHUNK) <= (qpos[:, None] // CHUNK)

    def probs(q, k):
        s = jnp.einsum('bqhd,bkhd->bhqk', q, k).astype(jnp.float32) * scale + bias
        return jax.nn.softmax(jnp.where(allowed, s, NEG), axis=-1)

    p = probs(q1, k1) - lam * probs(q2, k2)
    return jnp.einsum('bhqk,bkhe->bqhe', p.astype(v.dtype), v)


def diff_prompt(q1, q2, k1, k2, v, lam):
    B, T = q1.shape[:2]
    nb = T // QBLK
    kpos = jnp.arange(T)
    blocks = lambda a: a.reshape((B, nb, QBLK) + a.shape[2:]).swapaxes(0, 1)

    def one(args):
        i, qa, qb = args
        qpos = i * QBLK + jnp.arange(QBLK)
        return diff_core(qa, qb, k1, k2, v, qpos, kpos, lam)

    o = lax.map(one, (jnp.arange(nb), blocks(q1), blocks(q2)))
    return o.swapaxes(0, 1).reshape((B, T) + o.shape[3:])


def band_core(q, k, v, qpos, kpos, table):
    rel = jnp.clip(qpos[:, None] - kpos[None, :], -REL_CLIP, REL_CLIP) + REL_CLIP
    bias = table[:, rel].astype(jnp.float32)
    qc = qpos[:, None] // CHUNK
    kc = kpos[None, :] // CHUNK
    allowed = (kpos[None, :] >= 0) & (kc <= qc) & (kc >= qc - BAND_CHUNKS)
    s = jnp.einsum('bqhd,bkhd->bhqk', q, k).astype(jnp.float32) * DH_B ** -0.5 + bias
    p = jax.nn.softmax(jnp.where(allowed, s, NEG), axis=-1)
    return jnp.einsum('bhqk,bkhd->bqhd', p.astype(v.dtype), v)


def band_prompt(q, k, v, table):
    B, T, H, d = q.shape
    nc = T // CHUNK
    pad = BAND_CHUNKS * CHUNK
    kp = jnp.pad(k, ((0, 0), (pad, 0), (0, 0), (0, 0)))
    vp = jnp.pad(v, ((0, 0), (pad, 0), (0, 0), (0, 0)))
    qr = q.reshape(B, nc, CHUNK, H, d).swapaxes(0, 1)

    def one(args):
        c, qc = args
        kb = lax.dynamic_slice_in_dim(kp, c * CHUNK, pad + CHUNK, axis=1)
        vb = lax.dynamic_slice_in_dim(vp, c * CHUNK, pad + CHUNK, axis=1)
        qpos = c * CHUNK + jnp.arange(CHUNK)
        kpos = c * CHUNK - pad + jnp.arange(pad + CHUNK)
        return band_core(qc, kb, vb, qpos, kpos, table)

    o = lax.map(one, (jnp.arange(nc), qr))
    return o.swapaxes(0, 1).reshape(B, T, H, d)


def mlstm_chunk(carry, inp):
    Cs, ns, ms = carry
    q, k, v, li, lf = inp
    L = q.shape[1]
    b = jnp.cumsum(lf, axis=1)
    causal = jnp.tril(jnp.ones((L, L), bool))[None, :, :, None]
    dmat = jnp.where(causal, b[:, :, None, :] - b[:, None, :, :] + li[:, None, :, :], NEG)
    inter = b + ms[:, None, :]
    m = jnp.maximum(inter, dmat.max(axis=2))
    w_intra = jnp.exp(dmat - m[:, :, None, :])
    w_inter = jnp.exp(inter - m)
    a = w_intra * jnp.einsum('bthd,bshd->btsh', q, k)
    num = jnp.einsum('btsh,bshe->bthe', a, v) + w_inter[..., None] * jnp.einsum('bhed,bthd->bthe', Cs, q)
    den = a.sum(axis=2) + w_inter * jnp.einsum('bhd,bthd->bth', ns, q)
    h = num / jnp.maximum(jnp.abs(den), jnp.exp(-m))[..., None]
    b_last = b[:, -1]
    g = b_last[:, None, :] - b + li
    m_new = jnp.maximum(b_last + ms, g.max(axis=1))
    ws = jnp.exp(g - m_new[:, None, :])
    decay = jnp.exp(b_last + ms - m_new)
    C_new = decay[..., None, None] * Cs + jnp.einsum('bsh,bshe,bshd->bhed', ws, v, k)
    n_new = decay[..., None] * ns + jnp.einsum('bsh,bshd->bhd', ws, k)
    return (C_new, n_new, m_new), h


def mlstm_branch(c_qk, c_v, c_o, c_if, conv_buf, state, p, prompt):
    B, T, _ = c_qk.shape
    f32 = jnp.float32
    xp = jnp.concatenate([conv_buf.astype(c_qk.dtype), c_qk], axis=1)
    u = p['conv_b'] + sum(xp[:, j:j + T] * p['conv_w'][j] for j in range(CONV_W))
    u = jax.nn.silu(u).astype(f32)
    q, k = jnp.split(u, 2, axis=-1)
    q = q.reshape(B, T, H_C, DH_C)
    k = k.reshape(B, T, H_C, DH_C) * DH_C ** -0.5
    v = c_v.astype(f32).reshape(B, T, H_C, DH_C)
    li = (c_if[..., :H_C] + p['b_i']).astype(f32)
    lf = jax.nn.log_sigmoid((c_if[..., H_C:] + p['b_f']).astype(f32))
    state = tuple(s.astype(f32) for s in state)
    if prompt:
        nc = T // CHUNK
        blk = lambda a: a.reshape((B, nc, CHUNK) + a.shape[2:]).swapaxes(0, 1)
        state, hs = lax.scan(mlstm_chunk, state, (blk(q), blk(k), blk(v), blk(li), blk(lf)))
        hs = hs.swapaxes(0, 1).reshape(B, T, H_C, DH_C)
    else:
        state, hs = mlstm_chunk(state, (q, k, v, li, lf))
    hs = rmsnorm(hs, p['c_head_g']) * jax.nn.sigmoid(c_o.astype(f32)).reshape(B, T, H_C, DH_C)
    return hs.reshape(B, T, D_C).astype(c_qk.dtype), state, xp[:, -(CONV_W - 1):]


def cross_attn(h, mem_k, mem_v, p):
    B, T, _ = h.shape
    q = (h @ p['w_mq']).reshape(B, T, H_M, DH_M)
    s = jnp.einsum('bthd,bmhd->bhtm', q, mem_k).astype(jnp.float32) * DH_M ** -0.5
    a = jax.nn.softmax(s, axis=-1).astype(mem_v.dtype)
    o = jnp.einsum('bhtm,bmhd->bthd', a, mem_v).reshape(B, T, D_MODEL)
    return o @ p['w_mo']


def trunk_layer(x, p, l, mem_k, mem_v, cache):
    B, T, _ = x.shape
    f32 = jnp.float32
    h = rmsnorm(x, p['g_mix'])
    idx = np.cumsum(IN_SIZES)[:-1].tolist()
    (a_q1, a_q2, a_k1, a_k2, a_v, b_q, b_k, b_v, c_qk, c_v, c_o, c_if, g) = jnp.split(h @ p['w_in'], idx, axis=-1)
    a_q1, a_q2, a_k1, a_k2, a_v = (t.reshape(B, T, H_A, -1) for t in (a_q1, a_q2, a_k1, a_k2, a_v))
    b_q, b_k, b_v = (t.reshape(B, T, H_B, DH_B) for t in (b_q, b_k, b_v))
    lam_init = 0.8 - 0.6 * math.exp(-0.3 * l)
    lam = (jnp.exp(jnp.sum(p['lq1'] * p['lk1'])) - jnp.exp(jnp.sum(p['lq2'] * p['lk2'])) + lam_init).astype(f32)
    a_k = jnp.concatenate([a_k1, a_k2], axis=-1)
    if cache is None:
        oa = diff_prompt(a_q1, a_q2, a_k1, a_k2, a_v, lam)
        ob = band_prompt(b_q, b_k, b_v, p['rel'])
        conv_buf = jnp.zeros((B, CONV_W - 1, 2 * D_C), x.dtype)
        st0 = (jnp.zeros((B, H_C, DH_C, DH_C), f32), jnp.zeros((B, H_C, DH_C), f32), jnp.zeros((B, H_C), f32))
        keep = min(BAND_CHUNKS * CHUNK, T)
        new_b = (b_k[:, T - keep:], b_v[:, T - keep:])
    else:
        past = cache['a_k'].shape[1]
        qpos = past + jnp.arange(T)
        ka = jnp.concatenate([cache['a_k'].astype(a_k.dtype), a_k], axis=1)
        va = jnp.concatenate([cache['a_v'].astype(a_v.dtype), a_v], axis=1)
        oa = diff_core(a_q1, a_q2, ka[..., :DH_A], ka[..., DH_A:], va, qpos, jnp.arange(past + T), lam)
        nband = cache['b_k'].shape[1]
        kb = jnp.concatenate([cache['b_k'].astype(b_k.dtype), b_k], axis=1)
        vb = jnp.concatenate([cache['b_v'].astype(b_v.dtype), b_v], axis=1)
        ob = band_core(b_q, kb, vb, qpos, jnp.arange(past - nband, past + T), p['rel'])
        conv_buf = cache['conv']
        st0 = (cache['C'], cache['n'], cache['m'])
        new_b = (b_k, b_v)
    oc, (C, n, m), conv_new = mlstm_branch(c_qk, c_v, c_o, c_if, conv_buf, st0, p, cache is None)
    oa = rmsnorm(oa, p['a_head_g']) * (1.0 - lam_init)
    g_a, g_b, g_c = jnp.split(jax.nn.sigmoid(g), 3, axis=-1)
    mixed = (g_a * (oa.reshape(B, T, -1) @ p['w_up_a'])
             + g_b * (ob.reshape(B, T, -1) @ p['w_up_b'])
             + g_c * (oc @ p['w_up_c']))
    x = x + mixed @ p['w_o']
    x = x + cross_attn(rmsnorm(x, p['g_cross']), mem_k, mem_v, p)
    hf = rmsnorm(x, p['g_ffn'])
    x = x + (jax.nn.silu(hf @ p['w_ff_g']) * (hf @ p['w_ff_u'])) @ p['w_ff_d']
    return x, (a_k, a_v) + new_b + (C, n, m, conv_new)


def setup_inputs(seed: int = 0) -> dict:
    key = jax.random.key(seed)
    ks = list(jax.random.split(key, 64))
    f32 = jnp.float32

    def nrm(shape, scale=1.0):
        return scale * jax.random.normal(ks.pop(), shape, f32)

    def proj(fan_in, fan_out):
        return nrm((DEPTH, fan_in, fan_out), fan_in ** -0.5)

    def gain(*shape):
        return 1.0 + nrm(shape, 0.02)

    band_past = min(BAND_CHUNKS * CHUNK, PAST_LEN)
    D = D_MODEL
    return {
        'x_prompt': nrm((BATCH, SEQ, D)),
        'x_sample': nrm((DEC_BATCH, DEC_SEQ, D)),
        'cache_a_k': nrm((DEPTH, DEC_BATCH, PAST_LEN, H_A, 2 * DH_A)),
        'cache_a_v': nrm((DEPTH, DEC_BATCH, PAST_LEN, H_A, DV_A)),
        'cache_b_k': nrm((DEPTH, DEC_BATCH, band_past, H_B, DH_B)),
        'cache_b_v': nrm((DEPTH, DEC_BATCH, band_past, H_B, DH_B)),
        'state_c_C': nrm((DEPTH, DEC_BATCH, H_C, DH_C, DH_C), 0.5),
        'state_c_n': nrm((DEPTH, DEC_BATCH, H_C, DH_C), 0.5),
        'state_c_m': nrm((DEPTH, DEC_BATCH, H_C)),
        'state_c_conv': nrm((DEPTH, DEC_BATCH, CONV_W - 1, 2 * D_C)),
        'cache_mem_k': nrm((DEPTH, DEC_BATCH, N_MEM, H_M, DH_M)),
        'cache_mem_v': nrm((DEPTH, DEC_BATCH, N_MEM, H_M, DH_M)),
        'mem_prompt': nrm((BATCH, N_MEM, D)),
        'g_mix': gain(DEPTH, D),
        'w_in': proj(D, N_IN),
        'a_lq1': nrm((DEPTH, DH_A), 0.1),
        'a_lk1': nrm((DEPTH, DH_A), 0.1),
        'a_lq2': nrm((DEPTH, DH_A), 0.1),
        'a_lk2': nrm((DEPTH, DH_A), 0.1),
        'a_head_g': gain(DEPTH, DV_A),
        'b_rel': nrm((DEPTH, H_B, 2 * REL_CLIP + 1), 0.5),
        'c_conv_w': nrm((DEPTH, CONV_W, 2 * D_C), CONV_W ** -0.5),
        'c_conv_b': nrm((DEPTH, 2 * D_C), 0.02),
        'c_b_i': nrm((DEPTH, H_C), 0.1),
        'c_b_f': jnp.linspace(3.0, 6.0, H_C, dtype=f32) + nrm((DEPTH, H_C), 0.1),
        'c_head_g': gain(DEPTH, DH_C),
        'w_up_a': proj(H_A * DV_A, D),
        'w_up_b': proj(H_B * DH_B, D),
        'w_up_c': proj(D_C, D),
        'w_o': proj(D, D),
        'g_cross': gain(DEPTH, D),
        'w_mq': proj(D, D),
        'w_mk': proj(D, D),
        'w_mv': proj(D, D),
        'w_mo': proj(D, D),
        'g_ffn': gain(DEPTH, D),
        'w_ff_g': proj(D, D_FF),
        'w_ff_u': proj(D, D_FF),
        'w_ff_d': proj(D_FF, D),
        'g_final': 1.0 + nrm((D,), 0.02),
    }


def reference(x_prompt, x_sample, cache_a_k, cache_a_v, cache_b_k, cache_b_v, state_c_C, state_c_n, state_c_m,
              state_c_conv, cache_mem_k, cache_mem_v, mem_prompt, g_mix, w_in, a_lq1, a_lk1, a_lq2, a_lk2,
              a_head_g, b_rel, c_conv_w, c_conv_b, c_b_i, c_b_f, c_head_g, w_up_a, w_up_b, w_up_c, w_o,
              g_cross, w_mq, w_mk, w_mv, w_mo, g_ffn, w_ff_g, w_ff_u, w_ff_d, g_final):
    xp, xs = x_prompt, x_sample
    Bp = x_prompt.shape[0]
    new_p = [[] for _ in range(10)]
    new_s = [[] for _ in range(8)]
    for l in range(DEPTH):
        p = dict(g_mix=g_mix[l], w_in=w_in[l], lq1=a_lq1[l], lk1=a_lk1[l], lq2=a_lq2[l], lk2=a_lk2[l],
                 a_head_g=a_head_g[l], rel=b_rel[l], conv_w=c_conv_w[l], conv_b=c_conv_b[l], b_i=c_b_i[l],
                 b_f=c_b_f[l], c_head_g=c_head_g[l], w_up_a=w_up_a[l], w_up_b=w_up_b[l], w_up_c=w_up_c[l],
                 w_o=w_o[l], g_cross=g_cross[l], w_mq=w_mq[l], w_mo=w_mo[l], g_ffn=g_ffn[l],
                 w_ff_g=w_ff_g[l], w_ff_u=w_ff_u[l], w_ff_d=w_ff_d[l])
        mk = (mem_prompt @ w_mk[l]).reshape(Bp, N_MEM, H_M, DH_M)
        mv = (mem_prompt @ w_mv[l]).reshape(Bp, N_MEM, H_M, DH_M)
        xp, st_p = trunk_layer(xp, p, l, mk, mv, None)
        cache = dict(a_k=cache_a_k[l], a_v=cache_a_v[l], b_k=cache_b_k[l], b_v=cache_b_v[l],
                     C=state_c_C[l], n=state_c_n[l], m=state_c_m[l], conv=state_c_conv[l])
        xs, st_s = trunk_layer(xs, p, l, cache_mem_k[l], cache_mem_v[l], cache)
        for lst, a in zip(new_p, st_p + (mk, mv)):
            lst.append(a)
        for lst, a in zip(new_s, st_s):
            lst.append(a)
    y_prompt = rmsnorm(xp, g_final)
    y_sample = rmsnorm(xs, g_final)
    (p_a_k, p_a_v, p_b_k, p_b_v, p_c_C, p_c_n, p_c_m, p_c_conv, p_mem_k, p_mem_v) = [jnp.stack(a, 0) for a in new_p]
    (s_a_k, s_a_v, s_b_k, s_b_v, s_c_C, s_c_n, s_c_m, s_c_conv) = [jnp.stack(a, 0) for a in new_s]
    return (y_prompt, y_sample, p_a_k, p_a_v, p_b_k, p_b_v, p_c_C, p_c_n, p_c_m, p_c_conv, p_mem_k, p_mem_v,
            s_a_k, s_a_v, s_b_k, s_b_v, s_c_C, s_c_n, s_c_m, s_c_conv)
```

```python
import math
import numpy as np
import concourse.bass as bass
import concourse.mybir as mybir
from concourse.bass_utils import run_bass_kernel_spmd

F32 = mybir.dt.float32
BF16 = mybir.dt.bfloat16
AF = mybir.ActivationFunctionType
ALU = mybir.AluOpType
AX = mybir.AxisListType

T = 4096
TS = 16
NT = T + TS
D = 1024
NIN = 8200
DFF = 2816
PAST = 1024
NBAND = 512
NMEM = 256
EPS = 1e-6
NEG = -30000.0
BLOCKS = [(i * 512, 512) for i in range(8)] + [(T, TS)]
TILES = [(i * 128, 128) for i in range(32)] + [(T, TS)]
SLOPES = [2.0 ** (-8.0 * (h + 1) / 4) for h in range(4)]


class Buf:
    __slots__ = ("name", "w", "r", "loose")

    def __init__(self, name, loose=False):
        self.name = name
        self.w = None
        self.r = []
        self.loose = loose


class Sched:
    ENG = ("pe", "act", "dve", "pool", "sp")

    def __init__(self, nc, n_dma_sems=16):
        self.nc = nc
        self.e = {"pe": nc.tensor, "act": nc.scalar, "dve": nc.vector, "pool": nc.gpsimd, "sp": nc.sync}
        self.sem = {k: nc.alloc_semaphore(name="sem_" + k) for k in self.ENG}
        self.cnt = {k: 0 for k in self.ENG}
        self.dsem = [nc.alloc_semaphore(name="dsem%d" % i) for i in range(n_dma_sems)]
        self.duse = [0] * n_dma_sems
        self.dnext = 0
        self.seen = {k: {} for k in self.ENG}
        self.ninstr = 0

    def _wait(self, eng, ev):
        if ev is None:
            return
        kind, key, val = ev
        if kind == "c":
            if key == "pe" and eng == "pe":
                return
            k = ("c", key)
            sem = self.sem[key]
        else:
            k = ("d", key)
            sem = self.dsem[key]
        if self.seen[eng].get(k, 0) >= val:
            return
        self.seen[eng][k] = val
        self.e[eng].wait_ge(sem, val)

    def _deps(self, eng, reads, writes):
        for b in reads:
            if not b.loose:
                self._wait(eng, b.w)
        for b in writes:
            if b.loose:
                continue
            self._wait(eng, b.w)
            for ev in b.r:
                self._wait(eng, ev)

    def _commit(self, ev, reads, writes):
        for b in reads:
            if not b.loose:
                b.r.append(ev)
        for b in writes:
            if not b.loose:
                b.w = ev
                b.r = []

    limit = None

    def op(self, eng, fn, reads=(), writes=()):
        if self.limit is not None and self.ninstr >= self.limit:
            self.ninstr += 1
            return None
        self._deps(eng, reads, writes)
        ins = fn()
        self.cnt[eng] += 1
        ins.then_inc(self.sem[eng], 1)
        ev = ("c", eng, self.cnt[eng])
        self._commit(ev, reads, writes)
        self.ninstr += 1
        return ev

    def dma(self, q, out, in_, reads=(), writes=(), **kw):
        if self.limit is not None and self.ninstr >= self.limit:
            self.ninstr += 1
            return None
        slot = self.dnext
        self.dnext = (self.dnext + 1) % len(self.dsem)
        if self.duse[slot] > 0:
            self._wait(q, ("d", slot, 16 * self.duse[slot]))
        self._deps(q, reads, writes)
        ins = self.e[q].dma_start(out=out, in_=in_, **kw)
        self.duse[slot] += 1
        ins.then_inc(self.dsem[slot], 16)
        ev = ("d", slot, 16 * self.duse[slot])
        self._commit(ev, reads, writes)
        self.ninstr += 1
        return ev

    def wait_all(self, bufs, eng="sp"):
        for b in bufs:
            self._wait(eng, b.w)
            for ev in b.r:
                self._wait(eng, ev)


class Rot:
    def __init__(self, items):
        self.items = items
        self.i = 0

    def next(self):
        it = self.items[self.i]
        self.i = (self.i + 1) % len(self.items)
        return it


def make_consts():
    c = {}
    k = np.arange(128)[:, None].astype(np.float64)
    q = np.arange(512)[None, :].astype(np.float64)
    abase = np.zeros((128, 4, 512), np.float32)
    adg = np.zeros((128, 4, 512), np.float32)
    acol = np.zeros((128, 4, 36), np.float32)
    for h in range(4):
        s = SLOPES[h]
        abase[:, h, :] = -s * (q - k)
        adg[:, h, :] = -s * (q - k)
        qq = np.arange(128)[None, :]
        kk = np.arange(128)[:, None]
        dg = np.where((kk // 64) <= (qq // 64), -s * np.abs(qq - kk), NEG)
        adg[:, h, 0:128] = dg
        for Dd in range(-3, 33):
            acol[:, h, Dd + 3] = s * np.arange(128) - s * 128.0 * Dd
    rq = np.zeros((2, 4, 512), np.float32)
    qi = np.arange(512)
    for h in range(4):
        rq[0, h, :] = -8.0 * SLOPES[h] * 256.0 * (qi // 256)
        rq[1, h, :] = -8.0 * SLOPES[h] * (qi % 256)
    c["rq"] = rq.reshape(2, 2048)
    c["abase"] = abase.reshape(128, 2048)
    c["adg"] = adg.reshape(128, 2048)
    c["acol"] = acol.reshape(128, 144)
    kk = np.arange(128)[:, None]
    qq = np.arange(128)[None, :]
    bm = np.zeros((128, 4, 128), np.float32)
    bm[:, 0, :] = np.where((kk < 64) & (qq >= 64), NEG, 0.0)
    bm[:, 1, :] = np.where((kk >= 64) & (qq < 64), NEG, 0.0)
    bm[:, 2, :] = (qq <= kk).astype(np.float32)
    bm[:, 3, :] = 1.0 - bm[:, 2, :]
    c["bmask"] = bm.reshape(128, 512)
    ss = np.arange(64)[:, None]
    tt = np.arange(64)[None, :]
    c["cmask"] = (ss <= tt).astype(np.float32)
    sel = np.zeros((4, 4, 128), np.float32)
    for h in range(4):
        sel[h, h, :] = 1.0
    c["sel"] = sel.reshape(4, 512)
    c["ident"] = np.eye(128, dtype=np.float32)
    return c


CONST_SHAPES = {"rq": [2, 2048], "abase": [128, 2048], "adg": [128, 2048], "acol": [128, 144], "bmask": [128, 512],
                "cmask": [64, 64], "sel": [4, 512], "ident": [128, 128]}

IN_SHAPES = {
    "x_p": [T, D], "x_s": [TS, D], "cak": [2, PAST, 512], "cav": [2, PAST, 512], "cbk": [2, NBAND, 512],
    "cbv": [2, NBAND, 512], "sC": [2, 4, 128, 128], "sn": [2, 4, 128], "sm": [2, 4], "sconv": [2, 3, D],
    "cmk": [2, NMEM, D], "cmv": [2, NMEM, D], "memp": [NMEM, D],
    "g_mix": [2, D], "w_in": [2, D, NIN], "a_lq1": [2, 64], "a_lk1": [2, 64], "a_lq2": [2, 64], "a_lk2": [2, 64],
    "a_head_g": [2, 128], "b_rel": [2, 8, 257], "c_conv_w": [2, 4, D], "c_conv_b": [2, D], "c_b_i": [2, 4],
    "c_b_f": [2, 4], "c_head_g": [2, 128], "w_up_a": [2, 512, D], "w_up_b": [2, 512, D], "w_up_c": [2, 512, D],
    "w_o": [2, D, D], "g_cross": [2, D], "w_mq": [2, D, D], "w_mk": [2, D, D], "w_mv": [2, D, D], "w_mo": [2, D, D],
    "g_ffn": [2, D], "w_ff_g": [2, D, DFF], "w_ff_u": [2, D, DFF], "w_ff_d": [2, DFF, D], "g_final": [D],
}
OUT_SHAPES = {
    "y_p": [T, D], "y_s": [TS, D], "p_a_k": [2, T, 512], "p_a_v": [2, T, 512], "p_b_k": [2, 512, 512],
    "p_b_v": [2, 512, 512], "p_c_C": [2, 4, 128, 128], "p_c_n": [2, 4, 128], "p_c_m": [2, 4], "p_c_conv": [2, 3, D],
    "p_mem_k": [2, NMEM, D], "p_mem_v": [2, NMEM, D], "s_a_k": [2, TS, 512], "s_a_v": [2, TS, 512],
    "s_b_k": [2, TS, 512], "s_b_v": [2, TS, 512], "s_c_C": [2, 4, 128, 128], "s_c_n": [2, 4, 128], "s_c_m": [2, 4],
    "s_c_conv": [2, 3, D],
}
OUT_ORDER = ["y_p", "y_s", "p_a_k", "p_a_v", "p_b_k", "p_b_v", "p_c_C", "p_c_n", "p_c_m", "p_c_conv", "p_mem_k",
             "p_mem_v", "s_a_k", "s_a_v", "s_b_k", "s_b_v", "s_c_C", "s_c_n", "s_c_m", "s_c_conv"]


def build_program(n_layers=2, stop_after=None, debug=False):
    nc = bass.Bass("TRN2", target_bir_lowering=False)
    S = Sched(nc)
    import os as _os
    if _os.environ.get("KLIMIT"):
        S.limit = int(_os.environ["KLIMIT"])
    I = {k: nc.dram_tensor(k, v, F32, kind="ExternalInput").ap() for k, v in IN_SHAPES.items()}
    CI = {k: nc.dram_tensor("c_" + k, v, F32, kind="ExternalInput").ap() for k, v in CONST_SHAPES.items()}
    O = {k: nc.dram_tensor(k, v, F32, kind="ExternalOutput").ap() for k, v in OUT_SHAPES.items()}
    OB = {k: Buf("o_" + k, loose=True) for k in OUT_SHAPES}
    skind = "ExternalOutput" if debug else "Internal"

    def scratch(name, shape, dt, loose=True):
        return nc.dram_tensor(name, shape, dt, kind=skind).ap(), Buf(name, loose=loose)

    XA, bXA = scratch("XA", [NT, D], F32)
    XB, bXB = scratch("XB", [NT, D], F32)
    OAT, bOAT = scratch("OAT", [4, 128, NT], BF16)
    OBT, bOBT = scratch("OBT", [4, 128, NT], BF16)
    OCT, bOCT = scratch("OCT", [4, 128, NT], BF16)
    MIXT, bMIXT = scratch("MIXT", [33, 128, 8, 128], BF16)
    CROT, bCROT = scratch("CROT", [33, 128, 8, 128], BF16)
    ACTT, bACTT = scratch("ACTT", [33, 128, 22, 128], BF16)
    ZB, bZB = scratch("ZB", [8, 128, 384], F32, loose=False)

    def sb(name, shape, dt=F32):
        return nc.alloc_sbuf_tensor(name, shape, dt), Buf(name)

    HT, bHT = sb("HT", [128, 8, NT], BF16)
    identf, bidf = sb("identf", [128, 128], F32)
    identb, bidb = sb("identb", [128, 128], BF16)
    onesb, bones = sb("onesb", [128, 128], BF16)
    GT, bGT = sb("GT", [128, D], F32)
    stg = Rot([sb("stg%d" % i, [128, 2048], F32) for i in range(2)])
    xin = Rot([sb("xin%d" % i, [128, D], F32) for i in range(2)])
    ybf = Rot([sb("ybf%d" % i, [128, D], BF16) for i in range(4)])
    junk, bjunk = sb("junk", [128, D], BF16)
    col = Rot([sb("col%d" % i, [128, 8], F32) for i in range(24)])
    epsc, bepsc = sb("epsc", [128, 1], F32)
    PS = [nc.alloc_psum_tensor("ps%d" % i, [128, 512], F32) for i in range(8)]
    PB = [Buf("ps%d" % i) for i in range(8)]
    ARENA, bAR = sb("arena", [128, 24576], F32)

    class Arena:
        def __init__(self):
            self.off = 0

        def reset(self):
            self.off = 0

        def get(self, name, shape, dt=F32):
            n = int(np.prod(shape[1:]))
            words = n if dt == F32 else (n + 1) // 2
            words = (words + 7) // 8 * 8
            assert self.off + words <= 24576, (name, self.off, words)
            v = ARENA[:, self.off:self.off + words]
            self.off += words
            if dt != F32:
                v = v.bitcast(dt)
            v = v[:, 0:n]
            if len(shape) == 3:
                v = v.rearrange("p (a b) -> p a b", a=shape[1])
            elif len(shape) == 4:
                v = v.rearrange("p (a b c) -> p a b c", a=shape[1], b=shape[2])
            return v[0:shape[0]], Buf(name)

    AR = Arena()
    engs = ("pe", "act", "dve", "pool", "sp")

    def barrier():
        evs = [("c", e, S.cnt[e]) for e in engs if S.cnt[e] > 0]
        evs += [("d", s, 16 * S.duse[s]) for s in range(len(S.dsem)) if S.duse[s] > 0]
        for e in engs:
            for ev in evs:
                S._wait(e, ev)

    def PE(out, lhsT, rhs, start=True, stop=True, r=(), w=()):
        return S.op("pe", lambda: nc.tensor.matmul(out, lhsT=lhsT, rhs=rhs, start=start, stop=stop), reads=r, writes=w)

    def TRP(out, in_, ident, r=(), w=()):
        return S.op("pe", lambda: nc.tensor.transpose(out, in_, ident), reads=r, writes=w)

    def ACT(out, in_, func, r=(), w=(), **kw):
        return S.op("act", lambda: nc.scalar.activation(out=out, in_=in_, func=func, **kw), reads=r, writes=w)

    def TS_(eng, out, in0, s1, s2, op0, op1=None, r=(), w=()):
        e = nc.vector if eng == "dve" else nc.gpsimd
        if op1 is None:
            return S.op(eng, lambda: e.tensor_scalar(out=out, in0=in0, scalar1=s1, scalar2=None, op0=op0), reads=r, writes=w)
        return S.op(eng, lambda: e.tensor_scalar(out=out, in0=in0, scalar1=s1, scalar2=s2, op0=op0, op1=op1), reads=r, writes=w)

    def STT(out, in0, scalar, in1, op0, op1, r=(), w=()):
        return S.op("dve", lambda: nc.vector.scalar_tensor_tensor(out=out, in0=in0, scalar=scalar, in1=in1, op0=op0, op1=op1),
                    reads=r, writes=w)

    def TT(eng, out, in0, in1, op, r=(), w=()):
        e = nc.vector if eng == "dve" else nc.gpsimd
        return S.op(eng, lambda: e.tensor_tensor(out=out, in0=in0, in1=in1, op=op), reads=r, writes=w)

    def CP(eng, out, in_, r=(), w=()):
        if eng == "act":
            return S.op("act", lambda: nc.scalar.copy(out=out, in_=in_), reads=r, writes=w)
        e = nc.vector if eng == "dve" else nc.gpsimd
        return S.op(eng, lambda: e.tensor_copy(out=out, in_=in_), reads=r, writes=w)

    def MSET(eng, ap, val, w=()):
        e = nc.vector if eng == "dve" else nc.gpsimd
        return S.op(eng, lambda: e.memset(ap, val), writes=w)

    dmaq = Rot(["sp", "act"])

    def RSQ(cl, bcl, n, i_src, i_tmp, i_dst, invn):
        ACT(cl[:n, i_tmp:i_tmp + 1], cl[:n, i_src:i_src + 1], AF.Ln, r=[bcl, bepsc], w=[bcl], bias=epsc[:n, 0:1], scale=invn)
        ACT(cl[:n, i_dst:i_dst + 1], cl[:n, i_tmp:i_tmp + 1], AF.Exp, r=[bcl], w=[bcl], scale=-0.5)

    def DMA(out, in_, r=(), w=(), q=None, **kw):
        return S.dma(q or "sp", out, in_, reads=r, writes=w, **kw)


    def put_fm(dst, kc_idx, c0, n, src2d, bsrc, bdst, q="act"):
        if c0 < T:
            t0 = c0 // 128
            nt_ = n // 128
            DMA(dst[t0:t0 + nt_, :, kc_idx, :].rearrange("t p x -> p t x"), src2d.rearrange("p (t x) -> p t x", x=128), r=[bsrc], w=[bdst], q=q)
        else:
            DMA(dst[32, :, kc_idx, 0:n], src2d, r=[bsrc], w=[bdst], q=q)

    def bcast_rows(dram_ap_1d_tensor, offset, n, parts=128):
        return bass.AP(tensor=dram_ap_1d_tensor, offset=offset, ap=[[0, parts], [1, n]])

    def load_w(dst, bdst, wl, pieces, KC, cast_eng="pool"):
        mx = 2048 // KC
        sub = []
        for (c0, n) in pieces:
            for a in range(0, n, mx):
                sub.append((c0 + a, min(mx, n - a)))
        groups, cur, tot = [], [], 0
        for (c0, n) in sub:
            if tot + n > mx:
                groups.append(cur)
                cur, tot = [], 0
            cur.append((c0, n))
            tot += n
        groups.append(cur)
        doff = 0
        for grp in groups:
            ntot = sum(n for _, n in grp)
            st, bst = stg.next()
            sv = st[:, 0:KC * ntot].rearrange("p (k n) -> p k n", k=KC)
            off = 0
            for (c0, n) in grp:
                DMA(sv[:, :, off:off + n], wl[:, c0:c0 + n].rearrange("(k p) n -> p k n", p=128), w=[bst])
                off += n
            CP(cast_eng, dst[:, :, doff:doff + ntot], sv, r=[bst], w=[bdst])
            doff += ntot

    def load_gamma(vec_ap):
        DMA(GT[:], bcast_rows(vec_ap.tensor, vec_ap.offset, D), w=[bGT])

    def xrows(xsel, c0, n):
        if xsel == "in":
            return (I["x_p"][c0:c0 + n, :], None) if c0 < T else (I["x_s"][0:n, :], None)
        if xsel == "A":
            return XA[c0:c0 + n, :], bXA
        return XB[c0:c0 + n, :], bXB

    psb7 = PS[7][:].bitcast(BF16)

    def norm_tile(xt, bxt, n, c0, to_out=None, defer=False):
        cl, bcl = col.next()
        ACT(junk[:n], xt[:n], AF.Square, r=[bxt], w=[bjunk, bcl], accum_out=cl[:n, 0:1])
        RSQ(cl, bcl, n, 0, 1, 2, 1.0 / D)
        if to_out is not None:
            yo, byo = xin.next()
            STT(yo[:n], xt[:n], cl[:n, 2:3], GT[:n], ALU.mult, ALU.mult, r=[bxt, bcl, bGT], w=[byo])
            DMA(to_out[0], yo[:n], r=[byo], w=[to_out[1]])
            return None
        yb, byb = ybf.next()
        STT(yb[:n], xt[:n], cl[:n, 2:3], GT[:n], ALU.mult, ALU.mult, r=[bxt, bcl, bGT], w=[byb])

        def fin():
            for kc in range(8):
                TRP(psb7[:, kc * 128:kc * 128 + n], yb[:n, kc * 128:(kc + 1) * 128], identb[:n, :n], r=[byb, bidb], w=[PB[7]])
            CP("act", HT[:, :, c0:c0 + n], psb7.rearrange("p (k t) -> p k t", k=8)[:, :, 0:n], r=[PB[7]], w=[bHT])
        if defer:
            return fin
        fin()
        return None

    DMA(identf[:], CI["ident"], w=[bidf])
    CP("dve", identb[:], identf[:], r=[bidf], w=[bidb])
    MSET("dve", onesb[:], 1.0, w=[bones])
    MSET("dve", epsc[:], EPS, w=[bepsc])
    cmask, bcm = sb("cmask", [64, 64], F32)
    DMA(cmask[:], CI["cmask"], w=[bcm])
    sel, bsel = sb("sel", [4, 512], F32)
    DMA(sel[:], CI["sel"], w=[bsel])
    lp, blp = sb("lp", [128, 640], F32)

    dbg = {}

    def phase_norm1(l, xsel):
        load_gamma(I["g_mix"][l])
        for (c0, n) in TILES:
            xt, bxt = xin.next()
            src, bsrc = xrows(xsel, c0, n)
            DMA(xt[:n], src, r=[bsrc] if bsrc else [], w=[bxt])
            norm_tile(xt, bxt, n, c0)

    def phase_A(l):
        AR.reset()
        wl = I["w_in"][l]
        qqT, bqq = AR.get("qqT", [128, 2, NT], BF16)
        kkT, bkk = AR.get("kkT", [128, 2, T + PAST + TS], BF16)
        Vaug, bV = AR.get("Vaug", [128, 41, 130], BF16)
        oaT, boa = AR.get("oaT", [128, NT], BF16)
        Wq, bWq = AR.get("Wq", [128, 8, 128], BF16)
        Wkv, bWkv = AR.get("Wkv", [128, 8, 256], BF16)
        kvf = Rot([AR.get("kvf%d" % i, [128, 256], F32) for i in range(2)])
        sbr = Rot([AR.get("sbA%d" % i, [128, 512], F32) for i in range(2)])
        ptr = Rot([AR.get("ptA%d" % i, [128, 512], BF16) for i in range(6)])
        o1, bo1 = AR.get("o1", [128, 4, 128], F32)
        ot = Rot([AR.get("otA%d" % i, [128, 128], F32) for i in range(8)])
        ob_ = Rot([AR.get("obA%d" % i, [128, 128], BF16) for i in range(4)])
        ckb, bckb = AR.get("ckb", [128, 8, 128], BF16)
        ahg, bahg = AR.get("ahg", [128, 128], F32)
        lam, blam = AR.get("lam", [128, 8], F32)
        cst, bcst = AR.get("cstA", [128, 2192], F32)
        DMA(cst[:, 0:2048], CI["adg"], w=[bcst])
        DMA(cst[:, 2048:2192], CI["acol"], w=[bcst])
        adg = cst[:, 0:2048].rearrange("p (h q) -> p h q", h=4)
        abase = adg
        acol = cst[:, 2048:2192].rearrange("p (h d) -> p h d", h=4)
        rqb, brqb = AR.get("rqb", [2, 4, 512], BF16)
        st_rq, bst_rq = stg.next()
        DMA(st_rq[0:2, 0:2048], CI["rq"], w=[bst_rq])
        CP("dve", rqb[:], st_rq[0:2, 0:2048].rearrange("p (h q) -> p h q", h=4), r=[bst_rq], w=[brqb])
        MSET("pool", Vaug[:, :, 128:129], 1.0, w=[bV])
        MSET("pool", kkT[64:66, :, :], 1.0, w=[bkk])
        lam_init = 0.8 - 0.6 * math.exp(-0.3 * l)
        for i, nm in enumerate(["a_lq1", "a_lk1", "a_lq2", "a_lk2"]):
            DMA(lp[:, i * 64:(i + 1) * 64], bcast_rows(I[nm].tensor, I[nm][l].offset, 64), w=[blp])
        TT("dve", lp[:, 256:320], lp[:, 0:64], lp[:, 64:128], ALU.mult, r=[blp], w=[blp])
        TT("dve", lp[:, 320:384], lp[:, 128:192], lp[:, 192:256], ALU.mult, r=[blp], w=[blp])
        ACT(junk[:, 0:64], lp[:, 256:320], AF.Identity, r=[blp], w=[bjunk, blam], accum_out=lam[:, 0:1])
        ACT(junk[:, 0:64], lp[:, 320:384], AF.Identity, r=[blp], w=[bjunk, blam], accum_out=lam[:, 1:2])
        ACT(lam[:, 4:6], lam[:, 0:2], AF.Exp, r=[blam], w=[blam])
        TT("dve", lam[:, 2:3], lam[:, 5:6], lam[:, 4:5], ALU.subtract, r=[blam], w=[blam])
        TS_("dve", lam[:, 3:4], lam[:, 2:3], -lam_init, None, ALU.add, r=[blam], w=[blam])
        DMA(ahg[:], bcast_rows(I["a_head_g"].tensor, I["a_head_g"][l].offset, 128), w=[bahg])
        TS_("dve", ahg[:], ahg[:], 1.0 - lam_init, None, ALU.mult, r=[bahg], w=[bahg])
        prj = Rot([(PS[i], PB[i]) for i in (4, 5, 6)])
        if debug:
            print("A pre-heads ninstr", S.ninstr)
        for h in range(4):
            load_w(Wq, bWq, wl, [(h * 64, 64), (256 + h * 64, 64)], 8)
            load_w(Wkv, bWkv, wl, [(512 + h * 64, 64), (768 + h * 64, 64), (1024 + h * 128, 128)], 8)
            for (c0, n) in BLOCKS:
                kc0 = c0 if c0 < T else T + PAST
                for m in range(2):
                    ps, bp = prj.next()
                    for kc in range(8):
                        PE(ps[0:64, 0:n], Wq[:, kc, m * 64:(m + 1) * 64], HT[:, kc, c0:c0 + n], kc == 0, kc == 7, r=[bWq, bHT], w=[bp])
                    CP("act", qqT[0:64, m, c0:c0 + n], ps[0:64, 0:n], r=[bp], w=[bqq])
                    ps, bp = prj.next()
                    for kc in range(8):
                        PE(ps[0:64, 0:n], Wkv[:, kc, m * 64:(m + 1) * 64], HT[:, kc, c0:c0 + n], kc == 0, kc == 7, r=[bWkv, bHT], w=[bp])
                    CP("dve", kkT[0:64, m, kc0:kc0 + n], ps[0:64, 0:n], r=[bp], w=[bkk])
            for m in range(2):
                for (c0, n) in BLOCKS:
                    DMA(qqT[64:66, m, c0:c0 + n], rqb[0:2, h, 0:n], r=[brqb], w=[bqq])
            if debug:
                print("A head", h, "pre-tokmajor ninstr", S.ninstr)
            for ti, (c0, n) in enumerate(TILES):
                ps, bp = prj.next()
                for kc in range(8):
                    PE(ps[:n, 0:256], HT[:, kc, c0:c0 + n], Wkv[:, kc, :], kc == 0, kc == 7, r=[bWkv, bHT], w=[bp])
                kf, bkf = kvf.next()
                CP("act", kf[:n], ps[:n, 0:256], r=[bp], w=[bkf])
                CP("pool", Vaug[:n, ti, 0:128], kf[:n, 128:256], r=[bkf], w=[bV])
                if c0 < T:
                    DMA(O["p_a_k"][l, c0:c0 + n, h * 128:(h + 1) * 128], kf[:n, 0:128], r=[bkf], w=[OB["p_a_k"]])
                    DMA(O["p_a_v"][l, c0:c0 + n, h * 128:(h + 1) * 128], kf[:n, 128:256], r=[bkf], w=[OB["p_a_v"]])
                else:
                    DMA(O["s_a_k"][l, 0:n, h * 128:(h + 1) * 128], kf[:n, 0:128], r=[bkf], w=[OB["s_a_k"]])
                    DMA(O["s_a_v"][l, 0:n, h * 128:(h + 1) * 128], kf[:n, 128:256], r=[bkf], w=[OB["s_a_v"]])
            st, bst = stg.next()
            sv = st[:, 0:1024].rearrange("p (t c) -> p t c", t=8)
            DMA(sv, I["cak"][l, :, h * 128:(h + 1) * 128].rearrange("(t p) c -> p t c", p=128), w=[bst])
            CP("pool", ckb[:], sv, r=[bst], w=[bckb])
            for m in range(2):
                for t8 in range(8):
                    TRP(psb7[0:64, t8 * 128:(t8 + 1) * 128], ckb[:, t8, m * 64:(m + 1) * 64], identb[:], r=[bckb, bidb], w=[PB[7]])
                CP("act", kkT[0:64, m, T:T + PAST], psb7[0:64, 0:1024], r=[PB[7]], w=[bkk])
            st, bst = stg.next()
            sv = st[:, 0:1024].rearrange("p (t c) -> p t c", t=8)
            DMA(sv, I["cav"][l, :, h * 128:(h + 1) * 128].rearrange("(t p) c -> p t c", p=128), w=[bst])
            CP("pool", Vaug[:, 33:41, 0:128], sv, r=[bst], w=[bV])
            if debug:
                print("A head", h, "pre-attn ninstr", S.ninstr)
            items = []
            prjA = Rot([(PS[i], PB[i]) for i in (2, 3, 4, 5, 6, 7)])
            for qb, (q0, qn) in enumerate(BLOCKS):
                sample = q0 >= T
                nsub = 1 if sample else 4
                if not sample:
                    kbl = []
                    for kb in range(4 * qb + 4):
                        j = kb - 4 * qb
                        if j < 0:
                            kbl.append(dict(kc=kb * 128, nk=128, vt=kb, lo=0, tile=abase, Dd=(q0 - kb * 128) // 128, last_sub=None))
                        else:
                            kbl.append(dict(kc=kb * 128, nk=128, vt=kb, lo=128 * j, tile=adg, Dd=0, last_sub=j))
                else:
                    kbl = [dict(kc=T + kb * 128, nk=128, vt=33 + kb, lo=0, tile=abase, Dd=8 - kb, last_sub=None) for kb in range(8)]
                    kbl.append(dict(kc=T + PAST, nk=TS, vt=32, lo=0, tile=adg, Dd=0, last_sub=0))
                for m in range(2):
                    for bi, kb in enumerate(kbl):
                        it = dict(kb)
                        it.update(q0=q0, qn=qn, m=m, bi=bi, nb=len(kbl), sample=sample, nsub=nsub)
                        items.append(it)

            def stage1(it):
                lo, nk, qn, q0, m = it["lo"], it["nk"], it["qn"], it["q0"], it["m"]
                ps, bp = prjA.next()
                offd = it["last_sub"] is None
                p_, bp_ = ptr.next()
                if offd:
                    PE(ps[:nk, lo:qn], kkT[0:66, m, it["kc"]:it["kc"] + nk], qqT[0:66, m, q0 + lo:q0 + qn], True, True,
                       r=[bkk, bqq], w=[bp])
                    ACT(p_[:nk, lo:qn], ps[:nk, lo:qn], AF.Exp, r=[bp, bcst], w=[bp_],
                        bias=acol[:nk, h, it["Dd"] + 3:it["Dd"] + 4], scale=0.125)
                else:
                    PE(ps[:nk, lo:qn], kkT[0:64, m, it["kc"]:it["kc"] + nk], qqT[0:64, m, q0 + lo:q0 + qn], True, True,
                       r=[bkk, bqq], w=[bp])
                    s_, bs_ = sbr.next()
                    STT(s_[:nk, lo:qn], ps[:nk, lo:qn], 0.125, it["tile"][:nk, h, 0:qn - lo], ALU.mult, ALU.add, r=[bp, bcst], w=[bs_])
                    ACT(p_[:nk, lo:qn], s_[:nk, lo:qn], AF.Exp, r=[bs_], w=[bp_])
                it["p"] = (p_, bp_)

            def stage2(it):
                lo, nk, qn, q0, m = it["lo"], it["nk"], it["qn"], it["q0"], it["m"]
                p_, bp_ = it["p"]
                for sub in range(lo // 128, it["nsub"]):
                    sn_ = min(128, qn - sub * 128)
                    if it["last_sub"] is None:
                        last = it["sample"] and (it["bi"] == it["nb"] - 1)
                    else:
                        last = it["last_sub"] == sub
                    bk_, co_ = sub // 2, (sub % 2) * 256
                    PE(PS[bk_][:sn_, co_:co_ + 129], p_[:nk, sub * 128:sub * 128 + sn_], Vaug[:nk, it["vt"], 0:129],
                       it["bi"] == 0 and sub % 2 == 0, last and (sub % 2 == 1 or it["nsub"] == 1), r=[bp_, bV], w=[PB[bk_]])
                if it["bi"] != it["nb"] - 1:
                    return
                ns_ = it["nsub"]
                snb = min(128, qn)
                clA, bclA = col.next()
                clB, bclB = col.next()
                outs = []
                for sub in range(ns_):
                    sn_ = min(128, qn - sub * 128)
                    cl, bcl = col.next()
                    bk_, co_ = sub // 2, (sub % 2) * 256
                    acc_ = PS[bk_][:sn_, co_:co_ + 129]
                    S.op("dve", lambda: nc.vector.reciprocal(out=cl[:sn_, 0:1], in_=acc_[:, 128:129]), reads=[PB[bk_]], writes=[bcl])
                    if m == 0:
                        TS_("dve", o1[:sn_, sub, :], acc_[:, 0:128], cl[:sn_, 0:1], None, ALU.mult, r=[PB[bk_], bcl], w=[bo1])
                    else:
                        TT("dve", cl[:sn_, 1:2], cl[:sn_, 0:1], lam[:sn_, 3:4], ALU.mult, r=[bcl, blam], w=[bcl])
                        o_, bo_ = ot.next()
                        STT(o_[:sn_], acc_[:, 0:128], cl[:sn_, 1:2], o1[:sn_, sub, :], ALU.mult, ALU.add,
                            r=[PB[bk_], bcl, bo1], w=[bo_])
                        ACT(junk[:sn_, 0:128], o_[:sn_], AF.Square, r=[bo_], w=[bjunk, bclA], accum_out=clA[:sn_, sub:sub + 1])
                        outs.append((sub, sn_, o_, bo_))
                if m == 1:
                    ACT(clB[:snb, 0:ns_], clA[:snb, 0:ns_], AF.Ln, r=[bclA, bepsc], w=[bclB], bias=epsc[:snb, 0:1], scale=1.0 / 128)
                    ACT(clB[:snb, 4:4 + ns_], clB[:snb, 0:ns_], AF.Exp, r=[bclB], w=[bclB], scale=-0.5)
                    it["tr"] = (outs, clB, bclB)

            def stage3(it):
                outs, clB, bclB = it["tr"]
                q0 = it["q0"]
                for (sub, sn_, o_, bo_) in outs:
                    ob1, bob1 = ob_.next()
                    STT(ob1[:sn_], o_[:sn_], clB[:sn_, 4 + sub:5 + sub], ahg[:sn_], ALU.mult, ALU.mult, r=[bo_, bclB, bahg], w=[bob1])
                    pt_, bpt_ = prjA.next()
                    ptb = pt_[:].bitcast(BF16)
                    TRP(ptb[:, 0:sn_], ob1[:sn_, :], identb[:sn_, :sn_], r=[bob1, bidb], w=[bpt_])
                    CP("act", oaT[:, q0 + sub * 128:q0 + sub * 128 + sn_], ptb[:, 0:sn_], r=[bpt_], w=[boa])

            LOOK = 5
            for i in range(min(LOOK, len(items))):
                stage1(items[i])
            pend = None
            for i, it in enumerate(items):
                stage2(it)
                if i + LOOK < len(items):
                    stage1(items[i + LOOK])
                if pend is not None:
                    stage3(pend)
                    pend = None
                if "tr" in it:
                    pend = it
            if pend is not None:
                stage3(pend)
            DMA(OAT[h], oaT[:], r=[boa], w=[bOAT])
        if debug:
            dbg["OAT"] = OAT

    def phase_B(l):
        AR.reset()
        wl = I["w_in"][l]
        qT, bq = AR.get("qTb", [128, NT], BF16)
        kT, bk = AR.get("kTb", [128, T + NBAND + TS], BF16)
        Vb, bV = AR.get("Vb", [128, 37, 2, 66], BF16)
        obT, bobT = AR.get("obT", [128, NT], BF16)
        Wq, bWq = AR.get("WqB", [128, 8, 128], BF16)
        Wkv, bWkv = AR.get("WkvB", [128, 8, 256], BF16)
        kvf = Rot([AR.get("kvfB%d" % i, [128, 256], F32) for i in range(2)])
        sbr = Rot([AR.get("sbB%d" % i, [128, 128], F32) for i in range(10)])
        ptr = Rot([AR.get("ptB%d" % i, [128, 128], BF16) for i in range(11)])
        obt = Rot([AR.get("obtB%d" % i, [128, 128], BF16) for i in range(6)])
        ckb, bckb = AR.get("ckbB", [128, 4, 128], BF16)
        T34, bT34 = AR.get("T34", [128, 8, 2, 128], F32)
        c256, bc256 = AR.get("c256", [128, 8], F32)
        tmp, btmp = AR.get("tmpB", [128, 128], F32)
        zt, bzt = AR.get("ztB", [128, 384], F32)
        cst, bcst = AR.get("cstB", [128, 512], F32)
        DMA(cst[:], CI["bmask"], w=[bcst])
        bmask = cst[:].rearrange("p (m q) -> p m q", m=4)
        MSET("pool", Vb[:, :, :, 64:65], 1.0, w=[bV])
        MSET("dve", zt[:], 0.0, w=[bzt])
        for hh in range(8):
            DMA(ZB[hh], zt[:], r=[bzt], w=[bZB])
        rel = I["b_rel"]
        for hh in range(8):
            DMA(ZB[hh, :, 0:257], bcast_rows(rel.tensor, rel[l, hh].offset, 257), r=[], w=[bZB])
        DMA(c256[:], bass.AP(tensor=rel.tensor, offset=rel[l, 0].offset + 256, ap=[[0, 128], [257, 8]]), w=[bc256],
            allow_slow_non_contiguous=True)
        for hh in range(8):
            zb = ZB[hh]
            DMA(T34[:, hh, 1, :], bass.AP(tensor=zb.tensor, offset=zb.offset + 128, ap=[[383, 128], [1, 128]]), r=[bZB], w=[bT34])
            DMA(T34[:, hh, 0, :], bass.AP(tensor=zb.tensor, offset=zb.offset + 256, ap=[[383, 128], [1, 128]]), r=[bZB], w=[bT34])
            TT("dve", T34[:, hh, 1, :], T34[:, hh, 1, :], bmask[:, 1, :], ALU.add, r=[bT34, bcst], w=[bT34])
            TT("dve", tmp[:], T34[:, hh, 0, :], bmask[:, 2, :], ALU.mult, r=[bT34, bcst], w=[btmp])
            STT(T34[:, hh, 0, :], bmask[:, 3, :], c256[:, hh:hh + 1], tmp[:], ALU.mult, ALU.add, r=[btmp, bcst, bc256], w=[bT34])
        prj = Rot([(PS[i], PB[i]) for i in (4, 5, 6)])
        acc = Rot([(PS[i], PB[i]) for i in (0, 1)])
        for hp in range(4):
            load_w(Wq, bWq, wl, [(1536 + hp * 128, 128)], 8)
            load_w(Wkv, bWkv, wl, [(2048 + hp * 128, 128), (2560 + hp * 128, 128)], 8)
            for (c0, n) in BLOCKS:
                ps, bp = prj.next()
                for kc in range(8):
                    PE(ps[:, 0:n], Wq[:, kc, :], HT[:, kc, c0:c0 + n], kc == 0, kc == 7, r=[bWq, bHT], w=[bp])
                CP("act", qT[:, c0:c0 + n], ps[:, 0:n], r=[bp], w=[bq])
                ps, bp = prj.next()
                for kc in range(8):
                    PE(ps[:, 0:n], Wkv[:, kc, 0:128], HT[:, kc, c0:c0 + n], kc == 0, kc == 7, r=[bWkv, bHT], w=[bp])
                kc0 = c0 if c0 < T else T + NBAND
                CP("dve", kT[:, kc0:kc0 + n], ps[:, 0:n], r=[bp], w=[bk])
            for ti, (c0, n) in enumerate(TILES):
                ps, bp = prj.next()
                for kc in range(8):
                    PE(ps[:n, 0:256], HT[:, kc, c0:c0 + n], Wkv[:, kc, :], kc == 0, kc == 7, r=[bWkv, bHT], w=[bp])
                kf, bkf = kvf.next()
                CP("act", kf[:n], ps[:n, 0:256], r=[bp], w=[bkf])
                CP("pool", Vb[:n, ti, :, 0:64], kf[:n, 128:256].rearrange("p (h d) -> p h d", h=2), r=[bkf], w=[bV])
                if c0 >= T - 512:
                    if c0 < T:
                        r_ = c0 - (T - 512)
                        DMA(O["p_b_k"][l, r_:r_ + n, hp * 128:(hp + 1) * 128], kf[:n, 0:128], r=[bkf], w=[OB["p_b_k"]])
                        DMA(O["p_b_v"][l, r_:r_ + n, hp * 128:(hp + 1) * 128], kf[:n, 128:256], r=[bkf], w=[OB["p_b_v"]])
                    else:
                        DMA(O["s_b_k"][l, 0:n, hp * 128:(hp + 1) * 128], kf[:n, 0:128], r=[bkf], w=[OB["s_b_k"]])
                        DMA(O["s_b_v"][l, 0:n, hp * 128:(hp + 1) * 128], kf[:n, 128:256], r=[bkf], w=[OB["s_b_v"]])
            st, bst = stg.next()
            sv = st[:, 0:512].rearrange("p (t c) -> p t c", t=4)
            DMA(sv, I["cbk"][l, :, hp * 128:(hp + 1) * 128].rearrange("(t p) c -> p t c", p=128), w=[bst])
            CP("pool", ckb[:], sv, r=[bst], w=[bckb])
            for t4 in range(4):
                TRP(psb7[:, t4 * 128:(t4 + 1) * 128], ckb[:, t4, :], identb[:], r=[bckb, bidb], w=[PB[7]])
            CP("act", kT[:, T:T + NBAND], psb7[:, 0:512], r=[PB[7]], w=[bk])
            st, bst = stg.next()
            sv = st[:, 0:512].rearrange("p (t h d) -> p t h d", t=4, h=2)
            DMA(st[:, 0:512].rearrange("p (t c) -> p t c", t=4),
                I["cbv"][l, :, hp * 128:(hp + 1) * 128].rearrange("(t p) c -> p t c", p=128), w=[bst])
            for t4 in range(4):
                CP("pool", Vb[:, 33 + t4, :, 0:64], sv[:, t4], r=[bst], w=[bV])
            items = []
            for qt, (q0, qn) in enumerate(TILES):
                sample = q0 >= T
                for hh in range(2):
                    if not sample:
                        kbl = []
                        for i in range(5):
                            kbi = qt - 4 + i
                            if kbi >= 0:
                                kbl.append(dict(kc=kbi * 128, nk=128, vt=kbi, kind=i))
                    else:
                        kbl = [dict(kc=T + j * 128, nk=128, vt=33 + j, kind=(3 if j == 3 else 1)) for j in range(4)]
                        kbl.append(dict(kc=T + NBAND, nk=TS, vt=32, kind=4))
                    for bi, kb in enumerate(kbl):
                        it = dict(kb)
                        it.update(q0=q0, qn=qn, hh=hh, bi=bi, nb=len(kbl), qt=qt)
                        items.append(it)
            state = {}
            sprj = Rot([(PS[b_], PB[b_]) for b_ in (2, 3, 4, 5, 6)])

            def stage1(it):
                nk, kind, qn, q0, hh = it["nk"], it["kind"], it["qn"], it["q0"], it["hh"]
                H = 2 * hp + hh
                r0 = 64 * hh
                ps, bp = sprj.next()
                PE(ps[:nk, 0:qn], kT[r0:r0 + 64, it["kc"]:it["kc"] + nk], qT[r0:r0 + 64, q0:q0 + qn], True, True, r=[bk, bq], w=[bp])
                p_, bp_ = ptr.next()
                if kind in (1, 2):
                    ACT(p_[:nk, 0:qn], ps[:nk, 0:qn], AF.Exp, r=[bp, bc256], w=[bp_], bias=c256[:nk, H:H + 1], scale=0.125)
                else:
                    s_, bs_ = sbr.next()
                    tile = bmask[:nk, 0, 0:qn] if kind == 0 else (T34[:nk, H, 0, 0:qn] if kind == 3 else T34[:nk, H, 1, 0:qn])
                    STT(s_[:nk, 0:qn], ps[:nk, 0:qn], 0.125, tile, ALU.mult, ALU.add, r=[bp, bcst, bT34], w=[bs_])
                    if kind == 0:
                        ACT(p_[:nk, 0:qn], s_[:nk, 0:qn], AF.Exp, r=[bs_, bc256], w=[bp_], bias=c256[:nk, H:H + 1], scale=1.0)
                    else:
                        ACT(p_[:nk, 0:qn], s_[:nk, 0:qn], AF.Exp, r=[bs_], w=[bp_])
                it["p"] = (p_, bp_)

            def stage2(it):
                nk, qn, hh = it["nk"], it["qn"], it["hh"]
                r0 = 64 * hh
                if it["bi"] == 0:
                    state["ac"] = acc.next()
                    if hh == 0:
                        state["ot"] = obt.next()
                ac, bac = state["ac"]
                o_t, bo_t = state["ot"]
                p_, bp_ = it["p"]
                PE(ac[:qn, 0:65], p_[:nk, 0:qn], Vb[:nk, it["vt"], hh, 0:65], it["bi"] == 0, it["bi"] == it["nb"] - 1, r=[bp_, bV], w=[bac])
                if it["bi"] != it["nb"] - 1:
                    return
                cl, bcl = col.next()
                S.op("dve", lambda: nc.vector.reciprocal(out=cl[:qn, 0:1], in_=ac[:qn, 64:65]), reads=[bac], writes=[bcl])
                TS_("dve", o_t[:qn, r0:r0 + 64], ac[:qn, 0:64], cl[:qn, 0:1], None, ALU.mult, r=[bac, bcl], w=[bo_t])
                if hh == 1:
                    it["tr"] = (o_t, bo_t)

            def stage3(it):
                o_t, bo_t = it["tr"]
                q0, qn = it["q0"], it["qn"]
                TRP(psb7[:, 0:qn], o_t[:qn, :], identb[:qn, :qn], r=[bo_t, bidb], w=[PB[7]])
                CP("act", obT[:, q0:q0 + qn], psb7[:, 0:qn], r=[PB[7]], w=[bobT])

            LOOK = 4
            for i in range(min(LOOK, len(items))):
                stage1(items[i])
            pend = None
            for i, it in enumerate(items):
                stage2(it)
                if i + LOOK < len(items):
                    stage1(items[i + LOOK])
                if pend is not None:
                    stage3(pend)
                    pend = None
                if "tr" in it:
                    pend = it
            if pend is not None:
                stage3(pend)
            DMA(OBT[hp], obT[:], r=[bobT], w=[bOBT])
        if debug:
            dbg["OBT"] = OBT

    def phase_C(l):
        AR.reset()
        wl = I["w_in"][l]
        WST, bWST = AR.get("WST", [64, 65, 8], F32)
        DECB, bDECB = AR.get("DECB", [128, 4, 65], F32)
        chg, bchg = AR.get("chg", [64, 128], F32)
        cw, bcw = AR.get("cw", [128, 8, 4], F32)
        cb, bcb = AR.get("cb", [128, 8], F32)
        gp, bgp = AR.get("gp", [4, 8], F32)
        scv, bscv = AR.get("scv", [4, D], F32)
        mark = AR.off
        prj = Rot([(PS[i], PB[i]) for i in (4, 5, 6)])
        DMA(gp[:, 0:1], bass.AP(tensor=I["c_b_i"].tensor, offset=I["c_b_i"][l].offset, ap=[[1, 4], [1, 1]]), w=[bgp])
        DMA(gp[:, 1:2], bass.AP(tensor=I["c_b_f"].tensor, offset=I["c_b_f"][l].offset, ap=[[1, 4], [1, 1]]), w=[bgp])
        DMA(gp[:, 4:5], bass.AP(tensor=I["sm"].tensor, offset=I["sm"][l].offset, ap=[[1, 4], [1, 1]]), w=[bgp])
        TS_("dve", gp[:, 1:2], gp[:, 1:2], -1.0, None, ALU.mult, r=[bgp], w=[bgp])
        MSET("dve", gp[:, 2:3], 1.0, w=[bgp])
        MSET("dve", gp[:, 3:4], math.log(128.0 ** -0.5), w=[bgp])
        DMA(chg[:], bcast_rows(I["c_head_g"].tensor, I["c_head_g"][l].offset, 128, parts=64), w=[bchg])
        cwl = I["c_conv_w"]
        for j in range(4):
            DMA(cw[:, :, j], bass.AP(tensor=cwl.tensor, offset=cwl[l, j].offset, ap=[[1, 128], [128, 8]]), w=[bcw],
                allow_slow_non_contiguous=True)
        cbl = I["c_conv_b"]
        DMA(cb[:], bass.AP(tensor=cbl.tensor, offset=cbl[l].offset, ap=[[1, 128], [128, 8]]), w=[bcb], allow_slow_non_contiguous=True)
        DMA(scv[0:3, :], I["sconv"][l], w=[bscv])
        A1, bA1 = AR.get("A1", [4, NT], F32)
        A2, bA2 = AR.get("A2", [4, NT], F32)
        A3, bA3 = AR.get("A3", [4, NT], F32)
        ONE, bONE = AR.get("ONE", [4, NT], BF16)
        CM, bCM = AR.get("CM", [4, 65], F32)
        MI, bMI = AR.get("MI", [4, 65], F32)
        ME, bME = AR.get("ME", [4, 65], F32)
        DEC, bDEC = AR.get("DEC", [4, 65], F32)
        Wif, bWif = AR.get("Wif", [128, 8, 8], BF16)
        load_w(Wif, bWif, wl, [(5120, 8)], 8)
        MSET("pool", ONE[:], 1.0, w=[bONE])
        for (c0, n) in BLOCKS:
            ps, bp = prj.next()
            for kc in range(8):
                PE(ps[0:4, 0:n], Wif[:, kc, 0:4], HT[:, kc, c0:c0 + n], kc == 0, kc == 7, r=[bWif, bHT], w=[bp])
            ACT(A1[:, c0:c0 + n], ps[0:4, 0:n], AF.Identity, r=[bp, bgp], w=[bA1], bias=gp[:, 0:1], scale=1.0)
            ps, bp = prj.next()
            for kc in range(8):
                PE(ps[0:4, 0:n], Wif[:, kc, 4:8], HT[:, kc, c0:c0 + n], kc == 0, kc == 7, r=[bWif, bHT], w=[bp])
            ACT(A2[:, c0:c0 + n], ps[0:4, 0:n], AF.Exp, r=[bp, bgp], w=[bA2], bias=gp[:, 1:2], scale=-1.0)
        ACT(A2[:], A2[:], AF.Ln, r=[bA2, bgp], w=[bA2], bias=gp[:, 2:3], scale=1.0)
        for (a, b_) in ((0, T), (T, NT)):
            S.op("dve", lambda: nc.vector.tensor_tensor_scan(out=A3[:, a:b_], data0=ONE[:, a:b_], data1=A2[:, a:b_], initial=0.0,
                                                             op0=ALU.mult, op1=ALU.add), reads=[bONE, bA2], writes=[bA3])
        TT("dve", A1[:], A1[:], A3[:], ALU.add, r=[bA1, bA3], w=[bA1])
        S.op("dve", lambda: nc.vector.tensor_reduce(out=CM[:, 0:64], in_=A1[:, 0:T].rearrange("p (c s) -> p c s", s=64), axis=AX.X, op=ALU.max),
             reads=[bA1], writes=[bCM])
        S.op("dve", lambda: nc.vector.tensor_reduce(out=CM[:, 64:65], in_=A1[:, T:NT], axis=AX.X, op=ALU.max), reads=[bA1], writes=[bCM])
        S.op("dve", lambda: nc.vector.tensor_tensor_scan(out=MI[:, 0:64], data0=ONE[:, 0:64], data1=CM[:, 0:64], initial=0.0,
                                                         op0=ALU.mult, op1=ALU.max), reads=[bONE, bCM], writes=[bMI])
        TT("dve", MI[:, 64:65], CM[:, 64:65], gp[:, 4:5], ALU.max, r=[bCM, bgp], w=[bMI])
        MSET("dve", ME[:, 0:1], 0.0, w=[bME])
        CP("dve", ME[:, 1:64], MI[:, 0:63], r=[bMI], w=[bME])
        CP("dve", ME[:, 64:65], gp[:, 4:5], r=[bgp], w=[bME])
        TT("dve", DEC[:], ME[:], MI[:], ALU.subtract, r=[bME, bMI], w=[bDEC])
        ACT(DEC[:], DEC[:], AF.Exp, r=[bDEC], w=[bDEC])
        TT("dve", gp[:, 5:6], MI[:, 63:64], A3[:, T - 1:T], ALU.subtract, r=[bMI, bA3], w=[bgp])
        TT("dve", gp[:, 6:7], MI[:, 64:65], A3[:, NT - 1:NT], ALU.subtract, r=[bMI, bA3], w=[bgp])
        mi_b = MI[:, 0:64].unsqueeze(2).to_broadcast([4, 64, 64])
        a1v = A1[:, 0:T].rearrange("p (c s) -> p c s", s=64)
        a3v = A3[:, 0:T].rearrange("p (c s) -> p c s", s=64)
        TT("dve", a1v, a1v, mi_b, ALU.subtract, r=[bA1, bMI], w=[bA1])
        TS_("dve", A1[:, T:NT], A1[:, T:NT], MI[:, 64:65], None, ALU.subtract, r=[bA1, bMI], w=[bA1])
        ACT(A1[:], A1[:], AF.Exp, r=[bA1, bgp], w=[bA1], bias=gp[:, 3:4], scale=1.0)
        TT("dve", a3v, a3v, mi_b, ALU.subtract, r=[bA3, bMI], w=[bA3])
        TS_("dve", A3[:, T:NT], A3[:, T:NT], MI[:, 64:65], None, ALU.subtract, r=[bA3, bMI], w=[bA3])
        ACT(A3[:], A3[:], AF.Exp, r=[bA3], w=[bA3])
        DMA(bass.AP(tensor=O["p_c_m"].tensor, offset=O["p_c_m"][l].offset, ap=[[1, 4], [1, 1]]), gp[:, 5:6], r=[bgp], w=[OB["p_c_m"]])
        DMA(bass.AP(tensor=O["s_c_m"].tensor, offset=O["s_c_m"][l].offset, ap=[[1, 4], [1, 1]]), gp[:, 6:7], r=[bgp], w=[OB["s_c_m"]])
        for c in range(65):
            L = 64 if c < 64 else TS
            a = c * 64
            if c < 64:
                o1_, o2_, bb = PS[0][:L, c * 8:c * 8 + 4], PS[0][:L, c * 8 + 4:c * 8 + 8], PB[0]
            else:
                o1_, o2_, bb = PS[1][:L, 0:4], PS[1][:L, 4:8], PB[1]
            PE(o1_, A1[:, a:a + L], identf[0:4, 0:4], True, True, r=[bA1, bidf], w=[bb])
            PE(o2_, A3[:, a:a + L], identf[0:4, 0:4], True, True, r=[bA3, bidf], w=[bb])
        CP("dve", WST[:, 0:64, :], PS[0][0:64, :].rearrange("p (c e) -> p c e", e=8), r=[PB[0]], w=[bWST])
        CP("dve", WST[0:TS, 64, :], PS[1][0:TS, 0:8], r=[PB[1]], w=[bWST])
        for h in range(4):
            PE(PS[2][:, h * 65:(h + 1) * 65], sel[0:4, h * 128:(h + 1) * 128], DEC[:, :], True, True, r=[bsel, bDEC], w=[PB[2]])
        CP("dve", DECB[:], PS[2][:, 0:260].rearrange("p (h c) -> p h c", h=4), r=[PB[2]], w=[bDECB])
        barrier()
        for hp2 in range(2):
            AR.off = mark
            QK, bQK = AR.get("QK", [128, 4, NT], BF16)
            Wc, bWc = AR.get("Wc", [128, 8, 128], BF16)
            Wv, bWv = AR.get("WvC", [128, 8, 256], BF16)
            Wo, bWo = AR.get("WoC", [128, 8, 256], BF16)
            CS, bCS = AR.get("CS", [128, 2, 130], F32)
            CSs, bCSs = AR.get("CSs", [128, 2, 130], F32)
            CDB, bCDB = AR.get("CDB", [128, 2, 130], BF16)
            CDBs, bCDBs = AR.get("CDBs", [128, 2, 130], BF16)
            cvr = Rot([AR.get("cvr%d" % i, [4, 128], F32) for i in range(4)])
            VA = Rot([AR.get("VA%d" % i, [64, 2, 130], BF16) for i in range(5)])
            ATr = Rot([AR.get("AT%d" % i, [64, 64], BF16) for i in range(10)])
            KPr = Rot([AR.get("KP%d" % i, [64, 128], BF16) for i in range(10)])
            OCB = Rot([AR.get("OCB%d" % i, [128, 2, 512], BF16) for i in range(2)])
            cf, bcf = AR.get("cf", [128, 2, 128], F32)
            conv_mark = AR.off
            xp, bxp = AR.get("xp", [128, T + 3], F32)
            u, bu = AR.get("u", [128, 2048], F32)
            xps, bxps = AR.get("xps", [128, TS + 3], F32)
            MSET("dve", xp[:, 0:3], 0.0, w=[bxp])
            for slot in range(4):
                cc = (2 * hp2 + slot) if slot < 2 else (4 + 2 * hp2 + slot - 2)
                load_w(Wc, bWc, wl, [(3072 + cc * 128, 128)], 8)
                for (c0, n) in BLOCKS:
                    ps, bp = prj.next()
                    for kc in range(8):
                        PE(ps[:, 0:n], Wc[:, kc, :], HT[:, kc, c0:c0 + n], kc == 0, kc == 7, r=[bWc, bHT], w=[bp])
                    if c0 < T:
                        CP("act", xp[:, 3 + c0:3 + c0 + n], ps[:, 0:n], r=[bp], w=[bxp])
                    else:
                        CP("act", xps[:, 3:3 + n], ps[:, 0:n], r=[bp], w=[bxps])
                for (t0, onm) in ((T - 3, "p_c_conv"), (NT - 3, "s_c_conv")):
                    ps, bp = prj.next()
                    for kc in range(8):
                        PE(ps[0:3, 0:128], HT[:, kc, t0:t0 + 3], Wc[:, kc, :], kc == 0, kc == 7, r=[bWc, bHT], w=[bp])
                    cv_, bcv_ = cvr.next()
                    CP("dve", cv_[0:3, :], ps[0:3, 0:128], r=[bp], w=[bcv_])
                    DMA(O[onm][l, :, cc * 128:(cc + 1) * 128], cv_[0:3, :], r=[bcv_], w=[OB[onm]])
                PE(PS[3][:, 0:3], scv[0:3, cc * 128:(cc + 1) * 128], identf[0:3, 0:3], True, True, r=[bscv, bidf], w=[PB[3]])
                CP("dve", xps[:, 0:3], PS[3][:, 0:3], r=[PB[3]], w=[bxps])
                for (xx, bxx, n, dcol) in ((xp, bxp, T, 0), (xps, bxps, TS, T)):
                    for hs in range(0, n, 2048):
                        hn = min(2048, n - hs)
                        TS_("dve", u[:, 0:hn], xx[:, hs:hs + hn], cw[:, cc, 0:1], cb[:, cc:cc + 1], ALU.mult, ALU.add,
                            r=[bxx, bcw, bcb], w=[bu])
                        for j in range(1, 4):
                            STT(u[:, 0:hn], xx[:, hs + j:hs + j + hn], cw[:, cc, j:j + 1], u[:, 0:hn], ALU.mult, ALU.add,
                                r=[bxx, bcw, bu], w=[bu])
                        ACT(QK[:, slot, dcol + hs:dcol + hs + hn], u[:, 0:hn], AF.Silu, r=[bu], w=[bQK])
            MSET("dve", CS[:], 0.0, w=[bCS])
            MSET("pool", CDB[:], 0.0, w=[bCDB])
            DMA(cf[:], I["sC"][l, 2 * hp2:2 * hp2 + 2].rearrange("h e d -> e h d"), w=[bcf])
            for hh in range(2):
                TRP(PS[3][:, 0:128], cf[:, hh, :], identf[:], r=[bcf, bidf], w=[PB[3]])
                CP("dve", CSs[:, hh, 0:128], PS[3][:, 0:128], r=[PB[3]], w=[bCSs])
            snl = I["sn"]
            DMA(CSs[:, :, 128], bass.AP(tensor=snl.tensor, offset=snl[l, 2 * hp2].offset, ap=[[1, 128], [128, 2]]), w=[bCSs],
                allow_slow_non_contiguous=True)
            for hh in range(2):
                H = 2 * hp2 + hh
                TS_("dve", CDBs[:, hh, 0:129], CSs[:, hh, 0:129], DECB[:, H, 64:65], None, ALU.mult, r=[bCSs, bDECB], w=[bCDBs])
            load_w(Wv, bWv, wl, [(4096 + hp2 * 256, 256)], 8)
            load_w(Wo, bWo, wl, [(4608 + hp2 * 256, 256)], 8)
            for it in VA.items:
                MSET("pool", it[0][:, :, 128:129], 1.0, w=[it[1]])
            barrier()
            AR.off = conv_mark
            HG2 = [AR.get("HG%d" % i, [64, 8, 256], F32) for i in range(2)]
            SQ, bSQ = AR.get("SQ", [64, 8, 256], BF16)
            OGG2 = [AR.get("OGG%d" % i, [64, 8, 256], BF16) for i in range(2)]
            OKG, bOKG = SQ, bSQ
            post = []
            ssr, bssr = AR.get("ssr", [64, 48], F32)
            if debug and hp2 == 0:
                print("C2 arena words used", AR.off, "of 24576; conv_mark", conv_mark)
            prj = Rot([(PS[i], PB[i]) for i in range(7)])
            ocbs = {"cur": OCB.next()}
            ctxs = {}

            def stageX(c):
                smp = c == 64
                L = TS if smp else 64
                a = c * 64
                j8 = c % 8
                OGG, bOGG = OGG2[(c // 8) % 2]
                va, bva = VA.next()
                ps, bp = prj.next()
                for kc in range(8):
                    PE(ps[:L, 0:256], HT[:, kc, a:a + L], Wv[:, kc, :], kc == 0, kc == 7, r=[bWv, bHT], w=[bp])
                CP("act", va[:L, :, 0:128], ps[:L, 0:256].rearrange("p (h e) -> p h e", h=2), r=[bp], w=[bva])
                ps, bp = prj.next()
                for kc in range(8):
                    PE(ps[:L, 0:256], HT[:, kc, a:a + L], Wo[:, kc, :], kc == 0, kc == 7, r=[bWo, bHT], w=[bp])
                ACT(OGG[:L, j8, :], ps[:L, 0:256], AF.Sigmoid, r=[bp], w=[bOGG])
                ats, kps = [], []
                for hh in range(2):
                    H = 2 * hp2 + hh
                    qs = QK[:, hh, a:a + L]
                    ks = QK[:, 2 + hh, a:a + L]
                    ps, bp = prj.next()
                    PE(ps[:L, 0:L], ks, qs, True, True, r=[bQK], w=[bp])
                    at, bat = ATr.next()
                    STT(at[:L, :L], ps[:L, 0:L], WST[:L, c, H:H + 1], cmask[:L, :L], ALU.mult, ALU.mult, r=[bp, bWST, bcm], w=[bat])
                    pk, bpk = prj.next()
                    pkb = pk[:].bitcast(BF16)
                    TRP(pkb[:L, 0:128], ks, identb[:], r=[bQK, bidb], w=[bpk])
                    kp, bkp = KPr.next()
                    ACT(kp[:L, :], pkb[:L, 0:128], AF.Identity, r=[bpk, bWST], w=[bkp], scale=WST[:L, c, H:H + 1])
                    ats.append((at, bat))
                    kps.append((kp, bkp))
                ctxs[c] = dict(va=(va, bva), ats=ats, kps=kps)

            def stageY(c):
                smp = c == 64
                L = TS if smp else 64
                a = c * 64
                j8 = c % 8
                OGG, bOGG = OGG2[(c // 8) % 2]
                ctx = ctxs.pop(c)
                HG, bHG = HG2[(c // 8) % 2]
                va, bva = ctx["va"]
                st_, bst_ = (CSs, bCSs) if smp else (CS, bCS)
                cd_, bcd_ = (CDBs, bCDBs) if smp else (CDB, bCDB)
                pn, bpn = prj.next()
                pnv = pn[:, 0:260].rearrange("p (h e) -> p h e", h=2)
                for hh in range(2):
                    qs = QK[:, hh, a:a + L]
                    at, bat = ctx["ats"][hh]
                    PE(pnv[:L, hh, 0:129], at[:L, :L], va[:L, hh, 0:129], True, False, r=[bat, bva], w=[bpn])
                    PE(pnv[:L, hh, 0:129], qs, cd_[:, hh, 0:129], False, True, r=[bQK, bcd_], w=[bpn])
                for hh in range(2):
                    H = 2 * hp2 + hh
                    kp, bkp = ctx["kps"][hh]
                    pst, bpst = prj.next()
                    PE(pst[:, 0:129], kp[:L, :], va[:L, hh, 0:129], True, True, r=[bkp, bva], w=[bpst])
                    STT(st_[:, hh, 0:129], st_[:, hh, 0:129], DECB[:, H, c:c + 1], pst[:, 0:129], ALU.mult, ALU.add, r=[bst_, bDECB, bpst], w=[bst_])
                if c < 63:
                    TT("pool", cd_[:, :, 0:129], st_[:, :, 0:129],
                       DECB[:, 2 * hp2:2 * hp2 + 2, c + 1:c + 2].to_broadcast([128, 2, 129]), ALU.mult, r=[bst_, bDECB], w=[bcd_])
                cl, bcl = col.next()
                CP("dve", cl[:L, 0:2], pnv[:L, :, 128], r=[bpn], w=[bcl])
                TS_("dve", cl[:L, 2:4], cl[:L, 0:2], -1.0, None, ALU.mult, r=[bcl], w=[bcl])
                TT("dve", cl[:L, 2:4], cl[:L, 2:4], cl[:L, 0:2], ALU.max, r=[bcl], w=[bcl])
                TT("dve", cl[:L, 2:4], cl[:L, 2:4], WST[:L, c, 4 + 2 * hp2:6 + 2 * hp2], ALU.max, r=[bcl, bWST], w=[bcl])
                S.op("dve", lambda: nc.vector.reciprocal(out=cl[:L, 4:6], in_=cl[:L, 2:4]), reads=[bcl], writes=[bcl])
                TT("dve", HG[:L, j8, :].rearrange("p (h e) -> p h e", h=2), pnv[:L, :, 0:128],
                   cl[:L, 4:6].unsqueeze(2).to_broadcast([L, 2, 128]), ALU.mult, r=[bpn, bcl], w=[bHG])
                if j8 == 7 or smp:
                    ng = 1 if smp else 8
                    n2 = ng * 2
                    hgv = HG[:L, 0:ng, :].rearrange("p j (h e) -> p (j h) e", h=2)
                    sqv = SQ[:L, 0:ng, :].rearrange("p j (h e) -> p (j h) e", h=2)
                    TT("pool", SQ[:L, 0:ng, :], HG[:L, 0:ng, :], HG[:L, 0:ng, :], ALU.mult, r=[bHG], w=[bSQ])
                    S.op("dve", lambda: nc.vector.tensor_reduce(out=ssr[:L, 0:n2], in_=sqv, axis=AX.X, op=ALU.add), reads=[bSQ], writes=[bssr])
                    TT("pool", hgv, hgv, chg[:L, :].unsqueeze(1).to_broadcast([L, n2, 128]), ALU.mult, r=[bHG, bchg], w=[bHG])
                    ACT(ssr[:L, 16:16 + n2], ssr[:L, 0:n2], AF.Ln, r=[bssr, bepsc], w=[bssr], bias=epsc[:L, 0:1], scale=1.0 / 128)
                    ACT(ssr[:L, 32:32 + n2], ssr[:L, 16:16 + n2], AF.Exp, r=[bssr], w=[bssr], scale=-0.5)

                    def second(c=c, smp=smp, L=L, ng=ng, n2=n2, HG=HG, bHG=bHG, OGG=OGG, bOGG=bOGG, hgv=hgv):
                        ocb, bocb = ocbs["cur"]
                        TT("dve", hgv, hgv, ssr[:L, 32:32 + n2].unsqueeze(2).to_broadcast([L, n2, 128]), ALU.mult, r=[bHG, bssr], w=[bHG])
                        TT("dve", OKG[:L, 0:ng, :], HG[:L, 0:ng, :], OGG[:L, 0:ng, :], ALU.mult, r=[bHG, bOGG], w=[bOKG])
                        for jj in range(ng):
                            for hh in range(2):
                                TRP(psb7[:, hh * 512 + jj * 64:hh * 512 + jj * 64 + L], OKG[:L, jj, hh * 128:(hh + 1) * 128], identb[:L, :L],
                                    r=[bOKG, bidb], w=[PB[7]])
                        nn = ng * 64 if not smp else TS
                        CP("act", ocb[:, :, 0:nn], psb7.rearrange("p (h t) -> p h t", h=2)[:, :, 0:nn], r=[PB[7]], w=[bocb])
                        c0 = (c // 8) * 512
                        DMA(OCT[2 * hp2:2 * hp2 + 2, :, c0:c0 + nn].rearrange("h p t -> p h t"), ocb[:, :, 0:nn], r=[bocb], w=[bOCT])
                        ocbs["cur"] = OCB.next()
                    post.append(second)
                if c == 63 or smp:
                    oC, oN = ("s_c_C", "s_c_n") if smp else ("p_c_C", "p_c_n")
                    for hh in range(2):
                        pt3, bpt3 = prj.next()
                        TRP(pt3[:, 0:128], st_[:, hh, 0:128], identf[:], r=[bst_, bidf], w=[bpt3])
                        CP("dve", cf[:, hh, :], pt3[:, 0:128], r=[bpt3], w=[bcf])
                    DMA(O[oC][l, 2 * hp2:2 * hp2 + 2].rearrange("h e d -> e h d"), cf[:], r=[bcf], w=[OB[oC]])
                    on = O[oN]
                    DMA(bass.AP(tensor=on.tensor, offset=on[l, 2 * hp2].offset, ap=[[1, 128], [128, 2]]), st_[:, :, 128], r=[bst_],
                        w=[OB[oN]], allow_slow_non_contiguous=True)

            CLOOK = 3
            for c in range(min(CLOOK, 65)):
                stageX(c)
            for c in range(65):
                if post and (c % 8 == 7 or c == 64):
                    post.pop(0)()
                had = len(post)
                stageY(c)
                if c + CLOOK < 65:
                    stageX(c + CLOOK)
                if had:
                    post.pop(0)()
            while post:
                post.pop(0)()
            barrier()
        if debug:
            dbg["OCT"] = OCT

    def phase_MIX(l):
        AR.reset()
        wl = I["w_in"][l]
        Wg = [[AR.get("Wg%d_%d" % (fi, m), [128, 8, 128], BF16) for m in range(3)] for fi in range(4)]
        Wu = [[AR.get("Wu%d_%d" % (fi, m), [128, 4, 128], BF16) for m in range(3)] for fi in range(4)]
        ob3 = Rot([[AR.get("oblk%d_%d" % (i, m), [128, 4, 512], BF16) for m in range(3)] for i in range(2)])
        sg = Rot([AR.get("sg%d" % i, [128, 512], F32) for i in range(3)])
        mx = Rot([AR.get("mx%d" % i, [128, 512], F32) for i in range(2)])
        mo = Rot([AR.get("mo%d" % i, [128, 512], BF16) for i in range(2)])
        ups = [I["w_up_a"][l], I["w_up_b"][l], I["w_up_c"][l]]
        srcs = [(OAT, bOAT), (OBT, bOBT), (OCT, bOCT)]
        prj = Rot([(PS[i], PB[i]) for i in range(7)])
        for fh in range(2):
            for fi in range(4):
                f = fh * 4 + fi
                for m in range(3):
                    load_w(Wg[fi][m][0], Wg[fi][m][1], wl, [(5128 + m * 1024 + f * 128, 128)], 8)
                    load_w(Wu[fi][m][0], Wu[fi][m][1], ups[m], [(f * 128, 128)], 4)
            for (c0, n) in BLOCKS:
                blk = ob3.next()
                for m in range(3):
                    DMA(blk[m][0][:, :, 0:n], srcs[m][0][:, :, c0:c0 + n].rearrange("h p t -> p h t"), r=[srcs[m][1]], w=[blk[m][1]])
                for fi in range(4):
                    f = fh * 4 + fi
                    mx_, bmx = mx.next()
                    for m in range(3):
                        pg, bpg = prj.next()
                        for kc in range(8):
                            PE(pg[:, 0:n], Wg[fi][m][0][:, kc, :], HT[:, kc, c0:c0 + n], kc == 0, kc == 7, r=[Wg[fi][m][1], bHT], w=[bpg])
                        s_, bs_ = sg.next()
                        ACT(s_[:, 0:n], pg[:, 0:n], AF.Sigmoid, r=[bpg], w=[bs_])
                        pu, bpu = prj.next()
                        for kc in range(4):
                            PE(pu[:, 0:n], Wu[fi][m][0][:, kc, :], blk[m][0][:, kc, 0:n], kc == 0, kc == 3, r=[Wu[fi][m][1], blk[m][1]], w=[bpu])
                        if m == 0:
                            TT("dve", mx_[:, 0:n], pu[:, 0:n], s_[:, 0:n], ALU.mult, r=[bpu, bs_], w=[bmx])
                        else:
                            TT("dve", s_[:, 0:n], pu[:, 0:n], s_[:, 0:n], ALU.mult, r=[bpu, bs_], w=[bs_])
                            if m == 1:
                                TT("pool", mx_[:, 0:n], mx_[:, 0:n], s_[:, 0:n], ALU.add, r=[bmx, bs_], w=[bmx])
                            else:
                                mo_, bmo = mo.next()
                                TT("pool", mo_[:, 0:n], mx_[:, 0:n], s_[:, 0:n], ALU.add, r=[bmx, bs_], w=[bmo])
                                put_fm(MIXT, f, c0, n, mo_[:, 0:n], bmo, bMIXT)
        if debug:
            dbg["MIXT"] = MIXT

    def wr_pieces(w_dram, KC, Wr, bWr):
        out = []
        for kc0 in range(0, KC, 4):
            kk = min(4, KC - kc0)
            for half in range(2):
                def piece(kc0=kc0, kk=kk, half=half):
                    st, bst = stg.next()
                    sv = st[:, 0:kk * 512].rearrange("p (k n) -> p k n", k=kk)
                    DMA(sv, w_dram[kc0 * 128:(kc0 + kk) * 128, half * 512:(half + 1) * 512].rearrange("(k p) n -> p k n", p=128), w=[bst])
                    CP("pool", Wr[:, kc0:kc0 + kk, half * 512:(half + 1) * 512], sv, r=[bst], w=[bWr])
                out.append(piece)
        return out

    def phase_resid(l, w_dram, KC, srcT, bsrcT, xin_sel, xout_sel, gamma, final=False, tapname=None, preloaded=False):
        AR.reset()
        Wr, bWr = AR.get("Wr", [128, KC, D], BF16)
        for kc0 in (range(0, KC, 4) if not preloaded else []):
            kk = min(4, KC - kc0)
            for half in range(2):
                st, bst = stg.next()
                sv = st[:, 0:kk * 512].rearrange("p (k n) -> p k n", k=kk)
                DMA(sv, w_dram[kc0 * 128:(kc0 + kk) * 128, half * 512:(half + 1) * 512].rearrange("(k p) n -> p k n", p=128), w=[bst])
                CP("pool", Wr[:, kc0:kc0 + kk, half * 512:(half + 1) * 512], sv, r=[bst], w=[bWr])
        load_gamma(gamma)
        at = Rot([AR.get("at%d" % i, [128, KC, 128], BF16) for i in range(2)])
        xo = Rot([AR.get("xo%d" % i, [128, D], F32) for i in range(4)])
        prj = Rot([(PS[i], PB[i]) for i in range(6)])
        pend_fin = [None]
        for (c0, n) in TILES:
            a_, ba_ = at.next()
            ti_ = c0 // 128
            DMA(a_[:, :, 0:n], srcT[ti_, :, :, 0:n], r=[bsrcT], w=[ba_])
            xt, bxt = xin.next()
            src, bsrc = xrows(xin_sel, c0, n)
            DMA(xt[:n], src, r=[bsrc] if bsrc else [], w=[bxt], q="act")
            xo_, bxo = xo.next()
            for half in range(2):
                ps, bp = prj.next()
                for kc in range(KC):
                    PE(ps[:n, :], a_[:, kc, 0:n], Wr[:, kc, half * 512:(half + 1) * 512], kc == 0, kc == KC - 1, r=[ba_, bWr], w=[bp])
                TT("dve", xo_[:n, half * 512:(half + 1) * 512], ps[:n, :], xt[:n, half * 512:(half + 1) * 512], ALU.add,
                   r=[bp, bxt], w=[bxo])
            if len(pend_fin) > 2:
                pend_fin.pop(1)()
            if not final:
                dst, bdst = xrows(xout_sel, c0, n)
                DMA(dst, xo_[:n], r=[bxo], w=[bdst], q="act")
                pend_fin.append(norm_tile(xo_, bxo, n, c0, defer=True))
            else:
                if c0 < T:
                    norm_tile(xo_, bxo, n, c0, to_out=(O["y_p"][c0:c0 + n, :], OB["y_p"]))
                else:
                    norm_tile(xo_, bxo, n, c0, to_out=(O["y_s"][0:n, :], OB["y_s"]))
        for fn_ in pend_fin[1:]:
            fn_()

    def phase_CROSS(l):
        AR.reset()
        memT, bmemT = AR.get("memT", [128, 8, NMEM], BF16)
        mb, bmb = AR.get("mb", [128, 2, D], BF16)
        mkT = [AR.get("mkT%d" % g, [128, 8, NMEM], BF16) for g in range(2)]
        mvt = [AR.get("mvt%d" % g, [128, 2, D], BF16) for g in range(2)]
        Wm, bWm = AR.get("Wm", [128, 8, 512], BF16)
        qh, bqh = AR.get("qh", [128, 2, NT], BF16)
        oh, boh = AR.get("oh", [128, 2, NT], BF16)
        ptr = Rot([AR.get("ptC%d" % i, [128, 512], BF16) for i in range(4)])
        rcp = Rot([AR.get("rcp%d" % i, [128, 512], F32) for i in range(2)])
        mf = Rot([AR.get("mf%d" % i, [128, 512], F32) for i in range(2)])
        prj = Rot([(PS[i], PB[i]) for i in range(7)])
        for mt in range(2):
            xt, bxt = xin.next()
            DMA(xt[:], I["memp"][mt * 128:(mt + 1) * 128, :], w=[bxt])
            CP("dve", mb[:, mt, :], xt[:], r=[bxt], w=[bmb])
            for kc in range(8):
                TRP(psb7[:, kc * 128:(kc + 1) * 128], mb[:, mt, kc * 128:(kc + 1) * 128], identb[:], r=[bmb, bidb], w=[PB[7]])
            CP("act", memT[:, :, mt * 128:(mt + 1) * 128], psb7.rearrange("p (k t) -> p k t", k=8), r=[PB[7]], w=[bmemT])
        for which, wd, onm in ((0, I["w_mk"][l], "p_mem_k"), (1, I["w_mv"][l], "p_mem_v")):
            for half in range(2):
                load_w(Wm, bWm, wd, [(half * 512, 512)], 8)
                for mt in range(2):
                    ps, bp = prj.next()
                    for kc in range(8):
                        PE(ps[:, :], memT[:, kc, mt * 128:(mt + 1) * 128], Wm[:, kc, :], kc == 0, kc == 7, r=[bmemT, bWm], w=[bp])
                    m_, bm_ = mf.next()
                    CP("act", m_[:], ps[:, :], r=[bp], w=[bm_])
                    DMA(O[onm][l, mt * 128:(mt + 1) * 128, half * 512:(half + 1) * 512], m_[:], r=[bm_], w=[OB[onm]])
                    if which == 1:
                        CP("pool", mvt[0][0][:, mt, half * 512:(half + 1) * 512], m_[:], r=[bm_], w=[mvt[0][1]])
                if which == 0:
                    for j in range(4):
                        ps, bp = prj.next()
                        for kc in range(8):
                            PE(ps[:, 0:NMEM], Wm[:, kc, j * 128:(j + 1) * 128], memT[:, kc, :], kc == 0, kc == 7, r=[bmemT, bWm], w=[bp])
                        CP("dve", mkT[0][0][:, half * 4 + j, :], ps[:, 0:NMEM], r=[bp], w=[mkT[0][1]])
        for mt in range(2):
            xt, bxt = xin.next()
            DMA(xt[:], I["cmk"][l, mt * 128:(mt + 1) * 128, :], w=[bxt])
            CP("dve", mb[:, mt, :], xt[:], r=[bxt], w=[bmb])
            for kc in range(8):
                TRP(psb7[:, kc * 128:(kc + 1) * 128], mb[:, mt, kc * 128:(kc + 1) * 128], identb[:], r=[bmb, bidb], w=[PB[7]])
            CP("act", mkT[1][0][:, :, mt * 128:(mt + 1) * 128], psb7.rearrange("p (k t) -> p k t", k=8), r=[PB[7]], w=[mkT[1][1]])
            xt, bxt = xin.next()
            DMA(xt[:], I["cmv"][l, mt * 128:(mt + 1) * 128, :], w=[bxt])
            CP("dve", mvt[1][0][:, mt, :], xt[:], r=[bxt], w=[mvt[1][1]])
        for h in range(4):
            for half2 in range(1):
                load_w(Wm[:, :, 0:256], bWm, I["w_mq"][l], [(h * 256, 256)], 8)
            for (c0, n) in BLOCKS:
                for dc in range(2):
                    ps, bp = prj.next()
                    for kc in range(8):
                        PE(ps[:, 0:n], Wm[:, kc, dc * 128:(dc + 1) * 128], HT[:, kc, c0:c0 + n], kc == 0, kc == 7, r=[bWm, bHT], w=[bp])
                    CP("act", qh[:, dc, c0:c0 + n], ps[:, 0:n], r=[bp], w=[bqh])
            def c_stage1(c0, n):
                g = 0 if c0 < T else 1
                pts = []
                for mt in range(2):
                    ps, bp = prj.next()
                    for dc in range(2):
                        PE(ps[:, 0:n], mkT[g][0][:, h * 2 + dc, mt * 128:(mt + 1) * 128], qh[:, dc, c0:c0 + n], dc == 0, dc == 1,
                           r=[mkT[g][1], bqh], w=[bp])
                    p_, bp_ = ptr.next()
                    ACT(p_[:, 0:n], ps[:, 0:n], AF.Exp, r=[bp], w=[bp_], scale=1.0 / 16.0)
                    pts.append((p_, bp_))
                return pts

            def c_stage2(c0, n, pts):
                g = 0 if c0 < T else 1
                psu, bpsu = prj.next()
                for mt in range(2):
                    PE(psu[:, 0:n], onesb[:, :], pts[mt][0][:, 0:n], mt == 0, mt == 1, r=[bones, pts[mt][1]], w=[bpsu])
                rc, brc = rcp.next()
                S.op("dve", lambda: nc.vector.reciprocal(out=rc[:, 0:n], in_=psu[:, 0:n]), reads=[bpsu], writes=[brc])
                for ec in range(2):
                    po, bpo = prj.next()
                    for mt in range(2):
                        PE(po[:, 0:n], mvt[g][0][:, mt, h * 256 + ec * 128:h * 256 + (ec + 1) * 128], pts[mt][0][:, 0:n], mt == 0, mt == 1,
                           r=[mvt[g][1], pts[mt][1]], w=[bpo])
                    TT("dve", oh[:, ec, c0:c0 + n], po[:, 0:n], rc[:, 0:n], ALU.mult, r=[bpo, brc], w=[boh])

            nxt = c_stage1(*BLOCKS[0])
            for bi_, (c0, n) in enumerate(BLOCKS):
                cur_pts = nxt
                if bi_ + 1 < len(BLOCKS):
                    nxt = c_stage1(*BLOCKS[bi_ + 1])
                c_stage2(c0, n, cur_pts)
            for ec in range(2):
                put_fm(CROT, 2 * h + ec, 0, T, oh[:, ec, 0:T], boh, bCROT, q="sp")
                put_fm(CROT, 2 * h + ec, T, TS, oh[:, ec, T:NT], boh, bCROT, q="sp")
        if debug:
            dbg["CROT"] = CROT

    def phase_FFNU(l):
        AR.reset()
        Wr, bWr = AR.get("Wr", [128, 22, D], BF16)
        pre = wr_pieces(I["w_ff_d"][l], 22, Wr, bWr)
        Wg, bWg = AR.get("Wfg", [128, 8, 128], BF16)
        Wu, bWu = AR.get("Wfu", [128, 8, 128], BF16)
        sg = Rot([AR.get("fsg%d" % i, [128, 512], F32) for i in range(3)])
        ao = Rot([AR.get("fao%d" % i, [128, 512], BF16) for i in range(3)])
        prj = Rot([(PS[i], PB[i]) for i in range(7)])
        for f in range(22):
            load_w(Wg, bWg, I["w_ff_g"][l], [(f * 128, 128)], 8)
            load_w(Wu, bWu, I["w_ff_u"][l], [(f * 128, 128)], 8)
            if f >= 2 and f % 2 == 0 and pre:
                pre.pop(0)()
                if f >= 12 and pre:
                    pre.pop(0)()
            for (c0, n) in BLOCKS:
                pg, bpg = prj.next()
                for kc in range(8):
                    PE(pg[:, 0:n], Wg[:, kc, :], HT[:, kc, c0:c0 + n], kc == 0, kc == 7, r=[bWg, bHT], w=[bpg])
                s_, bs_ = sg.next()
                ACT(s_[:, 0:n], pg[:, 0:n], AF.Silu, r=[bpg], w=[bs_])
                pu, bpu = prj.next()
                for kc in range(8):
                    PE(pu[:, 0:n], Wu[:, kc, :], HT[:, kc, c0:c0 + n], kc == 0, kc == 7, r=[bWu, bHT], w=[bpu])
                a_, ba_ = ao.next()
                TT("dve", a_[:, 0:n], pu[:, 0:n], s_[:, 0:n], ALU.mult, r=[bpu, bs_], w=[ba_])
                put_fm(ACTT, f, c0, n, a_[:, 0:n], ba_, bACTT)
        while pre:
            pre.pop(0)()

    phases = []
    cur = "in"
    other = {"in": "A", "A": "B", "B": "A"}
    for l in range(n_layers):
        if l == 0:
            phases.append(("norm1", lambda l=l: phase_norm1(l, "in")))
        phases.append(("A%d" % l, lambda l=l: phase_A(l)))
        phases.append(("B%d" % l, lambda l=l: phase_B(l)))
        phases.append(("C%d" % l, lambda l=l: phase_C(l)))
        phases.append(("MIX%d" % l, lambda l=l: phase_MIX(l)))
        xi, xo = cur, other[cur]
        phases.append(("WO%d" % l, lambda l=l, xi=xi, xo=xo: phase_resid(l, I["w_o"][l], 8, MIXT, bMIXT, xi, xo, I["g_cross"][l])))
        cur = xo
        phases.append(("CROSS%d" % l, lambda l=l: phase_CROSS(l)))
        xi, xo = cur, other[cur]
        phases.append(("WMO%d" % l, lambda l=l, xi=xi, xo=xo: phase_resid(l, I["w_mo"][l], 8, CROT, bCROT, xi, xo, I["g_ffn"][l])))
        cur = xo
        phases.append(("FFNU%d" % l, lambda l=l: phase_FFNU(l)))
        xi, xo = cur, other[cur]
        if l < n_layers - 1:
            phases.append(("FFND%d" % l, lambda l=l, xi=xi, xo=xo: phase_resid(l, I["w_ff_d"][l], 22, ACTT, bACTT, xi, xo, I["g_mix"][l + 1], preloaded=True)))
        else:
            phases.append(("FFND%d" % l, lambda l=l, xi=xi, xo=xo: phase_resid(l, I["w_ff_d"][l], 22, ACTT, bACTT, xi, xo, I["g_final"], final=True, preloaded=True)))
        cur = xo
    for name, fn in phases:
        fn()
        barrier()
        if stop_after == name:
            break
    barrier()
    return nc, S, dbg


_PROG = {}


def _prep_inputs(inputs):
    f = lambda a: np.ascontiguousarray(a, dtype=np.float32)
    consts = make_consts()
    maps = []
    shared = {}
    for k in ("g_mix", "w_in", "a_lq1", "a_lk1", "a_lq2", "a_lk2", "a_head_g", "b_rel", "c_conv_w", "c_conv_b", "c_b_i", "c_b_f",
              "c_head_g", "w_up_a", "w_up_b", "w_up_c", "w_o", "g_cross", "w_mq", "w_mk", "w_mv", "w_mo", "g_ffn", "w_ff_g",
              "w_ff_u", "w_ff_d", "g_final"):
        shared[k] = f(inputs[k])
    for k, v in consts.items():
        shared["c_" + k] = f(v)
    for b in range(8):
        m = dict(shared)
        m["x_p"] = f(inputs["x_prompt"][b])
        m["x_s"] = f(inputs["x_sample"][b])
        m["cak"] = f(inputs["cache_a_k"][:, b].reshape(2, PAST, 512))
        m["cav"] = f(inputs["cache_a_v"][:, b].reshape(2, PAST, 512))
        m["cbk"] = f(inputs["cache_b_k"][:, b].reshape(2, NBAND, 512))
        m["cbv"] = f(inputs["cache_b_v"][:, b].reshape(2, NBAND, 512))
        m["sC"] = f(inputs["state_c_C"][:, b])
        m["sn"] = f(inputs["state_c_n"][:, b])
        m["sm"] = f(inputs["state_c_m"][:, b])
        m["sconv"] = f(inputs["state_c_conv"][:, b])
        m["cmk"] = f(inputs["cache_mem_k"][:, b].reshape(2, NMEM, D))
        m["cmv"] = f(inputs["cache_mem_v"][:, b].reshape(2, NMEM, D))
        m["memp"] = f(inputs["mem_prompt"][b])
        maps.append(m)
    return maps


def kernel(**inputs):
    if "nc" not in _PROG:
        _PROG["nc"] = build_program()[0]
    nc = _PROG["nc"]
    maps = _prep_inputs(inputs)
    res = run_bass_kernel_spmd(nc, maps, core_ids=list(range(8)))
    R = res.results
    st = lambda k: np.stack([np.asarray(R[b][k]) for b in range(8)], axis=0)
    st1 = lambda k: np.stack([np.asarray(R[b][k]) for b in range(8)], axis=1)
    out = {}
    out["y_p"] = st("y_p")
    out["y_s"] = st("y_s")
    out["p_a_k"] = st1("p_a_k").reshape(2, 8, T, 4, 128)
    out["p_a_v"] = st1("p_a_v").reshape(2, 8, T, 4, 128)
    out["p_b_k"] = st1("p_b_k").reshape(2, 8, 512, 8, 64)
    out["p_b_v"] = st1("p_b_v").reshape(2, 8, 512, 8, 64)
    out["p_c_C"] = st1("p_c_C")
    out["p_c_n"] = st1("p_c_n")
    out["p_c_m"] = st1("p_c_m")
    out["p_c_conv"] = st1("p_c_conv")
    out["p_mem_k"] = st1("p_mem_k").reshape(2, 8, NMEM, 4, 256)
    out["p_mem_v"] = st1("p_mem_v").reshape(2, 8, NMEM, 4, 256)
    out["s_a_k"] = st1("s_a_k").reshape(2, 8, TS, 4, 128)
    out["s_a_v"] = st1("s_a_v").reshape(2, 8, TS, 4, 128)
    out["s_b_k"] = st1("s_b_k").reshape(2, 8, TS, 8, 64)
    out["s_b_v"] = st1("s_b_v").reshape(2, 8, TS, 8, 64)
    out["s_c_C"] = st1("s_c_C")
    out["s_c_n"] = st1("s_c_n")
    out["s_c_m"] = st1("s_c_m")
    out["s_c_conv"] = st1("s_c_conv")
    return tuple(np.ascontiguousarray(out[k], dtype=np.float32) for k in OUT_ORDER)
```

```python
import math
import numpy as np
import concourse.bass as bass
import concourse.mybir as mybir
from concourse.bass_utils import run_bass_kernel_spmd

F32 = mybir.dt.float32
BF16 = mybir.dt.bfloat16
AF = mybir.ActivationFunctionType
ALU = mybir.AluOpType
AX = mybir.AxisListType

T = 4096
TS = 16
NT = T + TS
D = 1024
NIN = 8200
DFF = 2816
PAST = 1024
NBAND = 512
NMEM = 256
EPS = 1e-6
NEG = -30000.0
BLOCKS = [(i * 512, 512) for i in range(8)] + [(T, TS)]
TILES = [(i * 128, 128) for i in range(32)] + [(T, TS)]
SLOPES = [2.0 ** (-8.0 * (h + 1) / 4) for h in range(4)]


class Buf:
    __slots__ = ("name", "w", "r", "loose")

    def __init__(self, name, loose=False):
        self.name = name
        self.w = None
        self.r = []
        self.loose = loose


class Sched:
    ENG = ("pe", "act", "dve", "pool", "sp")

    def __init__(self, nc, n_dma_sems=16):
        self.nc = nc
        self.e = {"pe": nc.tensor, "act": nc.scalar, "dve": nc.vector, "pool": nc.gpsimd, "sp": nc.sync}
        self.sem = {k: nc.alloc_semaphore(name="sem_" + k) for k in self.ENG}
        self.cnt = {k: 0 for k in self.ENG}
        self.dsem = [nc.alloc_semaphore(name="dsem%d" % i) for i in range(n_dma_sems)]
        self.duse = [0] * n_dma_sems
        self.dnext = 0
        self.seen = {k: {} for k in self.ENG}
        self.ninstr = 0

    def _wait(self, eng, ev):
        if ev is None:
            return
        kind, key, val = ev
        if kind == "c":
            if key == "pe" and eng == "pe":
                return
            k = ("c", key)
            sem = self.sem[key]
        else:
            k = ("d", key)
            sem = self.dsem[key]
        if self.seen[eng].get(k, 0) >= val:
            return
        self.seen[eng][k] = val
        self.e[eng].wait_ge(sem, val)

    def _deps(self, eng, reads, writes):
        for b in reads:
            if not b.loose:
                self._wait(eng, b.w)
        for b in writes:
            if b.loose:
                continue
            self._wait(eng, b.w)
            for ev in b.r:
                self._wait(eng, ev)

    def _commit(self, ev, reads, writes):
        for b in reads:
            if not b.loose:
                b.r.append(ev)
        for b in writes:
            if not b.loose:
                b.w = ev
                b.r = []

    limit = None

    def op(self, eng, fn, reads=(), writes=()):
        if self.limit is not None and self.ninstr >= self.limit:
            self.ninstr += 1
            return None
        self._deps(eng, reads, writes)
        ins = fn()
        self.cnt[eng] += 1
        ins.then_inc(self.sem[eng], 1)
        ev = ("c", eng, self.cnt[eng])
        self._commit(ev, reads, writes)
        self.ninstr += 1
        return ev

    def dma(self, q, out, in_, reads=(), writes=(), **kw):
        if self.limit is not None and self.ninstr >= self.limit:
            self.ninstr += 1
            return None
        slot = self.dnext
        self.dnext = (self.dnext + 1) % len(self.dsem)
        if self.duse[slot] > 0:
            self._wait(q, ("d", slot, 16 * self.duse[slot]))
        self._deps(q, reads, writes)
        ins = self.e[q].dma_start(out=out, in_=in_, **kw)
        self.duse[slot] += 1
        ins.then_inc(self.dsem[slot], 16)
        ev = ("d", slot, 16 * self.duse[slot])
        self._commit(ev, reads, writes)
        self.ninstr += 1
        return ev

    def wait_all(self, bufs, eng="sp"):
        for b in bufs:
            self._wait(eng, b.w)
            for ev in b.r:
                self._wait(eng, ev)


class Rot:
    def __init__(self, items):
        self.items = items
        self.i = 0

    def next(self):
        it = self.items[self.i]
        self.i = (self.i + 1) % len(self.items)
        return it


def make_consts():
    c = {}
    k = np.arange(128)[:, None].astype(np.float64)
    q = np.arange(512)[None, :].astype(np.float64)
    abase = np.zeros((128, 4, 512), np.float32)
    adg = np.zeros((128, 4, 512), np.float32)
    acol = np.zeros((128, 4, 36), np.float32)
    for h in range(4):
        s = SLOPES[h]
        abase[:, h, :] = -s * (q - k)
        adg[:, h, :] = -s * (q - k)
        qq = np.arange(128)[None, :]
        kk = np.arange(128)[:, None]
        dg = np.where((kk // 64) <= (qq // 64), -s * np.abs(qq - kk), NEG)
        adg[:, h, 0:128] = dg
        for Dd in range(-3, 33):
            acol[:, h, Dd + 3] = s * np.arange(128) - s * 128.0 * Dd
    rq = np.zeros((2, 4, 512), np.float32)
    qi = np.arange(512)
    for h in range(4):
        rq[0, h, :] = -8.0 * SLOPES[h] * 256.0 * (qi // 256)
        rq[1, h, :] = -8.0 * SLOPES[h] * (qi % 256)
    c["rq"] = rq.reshape(2, 2048)
    c["abase"] = abase.reshape(128, 2048)
    c["adg"] = adg.reshape(128, 2048)
    c["acol"] = acol.reshape(128, 144)
    kk = np.arange(128)[:, None]
    qq = np.arange(128)[None, :]
    bm = np.zeros((128, 4, 128), np.float32)
    bm[:, 0, :] = np.where((kk < 64) & (qq >= 64), NEG, 0.0)
    bm[:, 1, :] = np.where((kk >= 64) & (qq < 64), NEG, 0.0)
    bm[:, 2, :] = (qq <= kk).astype(np.float32)
    bm[:, 3, :] = 1.0 - bm[:, 2, :]
    c["bmask"] = bm.reshape(128, 512)
    ss = np.arange(64)[:, None]
    tt = np.arange(64)[None, :]
    c["cmask"] = (ss <= tt).astype(np.float32)
    sel = np.zeros((4, 4, 128), np.float32)
    for h in range(4):
        sel[h, h, :] = 1.0
    c["sel"] = sel.reshape(4, 512)
    c["ident"] = np.eye(128, dtype=np.float32)
    return c


CONST_SHAPES = {"rq": [2, 2048], "abase": [128, 2048], "adg": [128, 2048], "acol": [128, 144], "bmask": [128, 512],
                "cmask": [64, 64], "sel": [4, 512], "ident": [128, 128]}

IN_SHAPES = {
    "x_p": [T, D], "x_s": [TS, D], "cak": [2, PAST, 512], "cav": [2, PAST, 512], "cbk": [2, NBAND, 512],
    "cbv": [2, NBAND, 512], "sC": [2, 4, 128, 128], "sn": [2, 4, 128], "sm": [2, 4], "sconv": [2, 3, D],
    "cmk": [2, NMEM, D], "cmv": [2, NMEM, D], "memp": [NMEM, D],
    "g_mix": [2, D], "w_in": [2, D, NIN], "a_lq1": [2, 64], "a_lk1": [2, 64], "a_lq2": [2, 64], "a_lk2": [2, 64],
    "a_head_g": [2, 128], "b_rel": [2, 8, 257], "c_conv_w": [2, 4, D], "c_conv_b": [2, D], "c_b_i": [2, 4],
    "c_b_f": [2, 4], "c_head_g": [2, 128], "w_up_a": [2, 512, D], "w_up_b": [2, 512, D], "w_up_c": [2, 512, D],
    "w_o": [2, D, D], "g_cross": [2, D], "w_mq": [2, D, D], "w_mk": [2, D, D], "w_mv": [2, D, D], "w_mo": [2, D, D],
    "g_ffn": [2, D], "w_ff_g": [2, D, DFF], "w_ff_u": [2, D, DFF], "w_ff_d": [2, DFF, D], "g_final": [D],
}
OUT_SHAPES = {
    "y_p": [T, D], "y_s": [TS, D], "p_a_k": [2, T, 512], "p_a_v": [2, T, 512], "p_b_k": [2, 512, 512],
    "p_b_v": [2, 512, 512], "p_c_C": [2, 4, 128, 128], "p_c_n": [2, 4, 128], "p_c_m": [2, 4], "p_c_conv": [2, 3, D],
    "p_mem_k": [2, NMEM, D], "p_mem_v": [2, NMEM, D], "s_a_k": [2, TS, 512], "s_a_v": [2, TS, 512],
    "s_b_k": [2, TS, 512], "s_b_v": [2, TS, 512], "s_c_C": [2, 4, 128, 128], "s_c_n": [2, 4, 128], "s_c_m": [2, 4],
    "s_c_conv": [2, 3, D],
}
OUT_ORDER = ["y_p", "y_s", "p_a_k", "p_a_v", "p_b_k", "p_b_v", "p_c_C", "p_c_n", "p_c_m", "p_c_conv", "p_mem_k",
             "p_mem_v", "s_a_k", "s_a_v", "s_b_k", "s_b_v", "s_c_C", "s_c_n", "s_c_m", "s_c_conv"]


def build_program(n_layers=2, stop_after=None, debug=False):
    nc = bass.Bass("TRN2", target_bir_lowering=False)
    S = Sched(nc)
    import os as _os
    if _os.environ.get("KLIMIT"):
        S.limit = int(_os.environ["KLIMIT"])
    I = {k: nc.dram_tensor(k, v, F32, kind="ExternalInput").ap() for k, v in IN_SHAPES.items()}
    CI = {k: nc.dram_tensor("c_" + k, v, F32, kind="ExternalInput").ap() for k, v in CONST_SHAPES.items()}
    O = {k: nc.dram_tensor(k, v, F32, kind="ExternalOutput").ap() for k, v in OUT_SHAPES.items()}
    OB = {k: Buf("o_" + k, loose=True) for k in OUT_SHAPES}
    skind = "ExternalOutput" if debug else "Internal"

    def scratch(name, shape, dt, loose=True):
        return nc.dram_tensor(name, shape, dt, kind=skind).ap(), Buf(name, loose=loose)

    XA, bXA = scratch("XA", [NT, D], F32)
    XB, bXB = scratch("XB", [NT, D], F32)
    OAT, bOAT = scratch("OAT", [4, 128, NT], BF16)
    OBT, bOBT = scratch("OBT", [4, 128, NT], BF16)
    OCT, bOCT = scratch("OCT", [4, 128, NT], BF16)
    MIXT, bMIXT = scratch("MIXT", [33, 128, 8, 128], BF16)
    CROT, bCROT = scratch("CROT", [33, 128, 8, 128], BF16)
    ACTT, bACTT = scratch("ACTT", [33, 128, 22, 128], BF16)
    ZB, bZB = scratch("ZB", [8, 128, 384], F32, loose=False)

    def sb(name, shape, dt=F32):
        return nc.alloc_sbuf_tensor(name, shape, dt), Buf(name)

    HT, bHT = sb("HT", [128, 8, NT], BF16)
    identf, bidf = sb("identf", [128, 128], F32)
    identb, bidb = sb("identb", [128, 128], BF16)
    onesb, bones = sb("onesb", [128, 128], BF16)
    GT, bGT = sb("GT", [128, D], F32)
    stg = Rot([sb("stg%d" % i, [128, 2048], F32) for i in range(2)])
    xin = Rot([sb("xin%d" % i, [128, D], F32) for i in range(2)])
    ybf = Rot([sb("ybf%d" % i, [128, D], BF16) for i in range(4)])
    junk, bjunk = sb("junk", [128, D], BF16)
    col = Rot([sb("col%d" % i, [128, 8], F32) for i in range(24)])
    epsc, bepsc = sb("epsc", [128, 1], F32)
    PS = [nc.alloc_psum_tensor("ps%d" % i, [128, 512], F32) for i in range(8)]
    PB = [Buf("ps%d" % i) for i in range(8)]
    ARENA, bAR = sb("arena", [128, 24576], F32)

    class Arena:
        def __init__(self):
            self.off = 0

        def reset(self):
            self.off = 0

        def get(self, name, shape, dt=F32):
            n = int(np.prod(shape[1:]))
            words = n if dt == F32 else (n + 1) // 2
            words = (words + 7) // 8 * 8
            assert self.off + words <= 24576, (name, self.off, words)
            v = ARENA[:, self.off:self.off + words]
            self.off += words
            if dt != F32:
                v = v.bitcast(dt)
            v = v[:, 0:n]
            if len(shape) == 3:
                v = v.rearrange("p (a b) -> p a b", a=shape[1])
            elif len(shape) == 4:
                v = v.rearrange("p (a b c) -> p a b c", a=shape[1], b=shape[2])
            return v[0:shape[0]], Buf(name)

    AR = Arena()
    engs = ("pe", "act", "dve", "pool", "sp")

    def barrier():
        evs = [("c", e, S.cnt[e]) for e in engs if S.cnt[e] > 0]
        evs += [("d", s, 16 * S.duse[s]) for s in range(len(S.dsem)) if S.duse[s] > 0]
        for e in engs:
            for ev in evs:
                S._wait(e, ev)

    def PE(out, lhsT, rhs, start=True, stop=True, r=(), w=()):
        return S.op("pe", lambda: nc.tensor.matmul(out, lhsT=lhsT, rhs=rhs, start=start, stop=stop), reads=r, writes=w)

    def TRP(out, in_, ident, r=(), w=()):
        return S.op("pe", lambda: nc.tensor.transpose(out, in_, ident), reads=r, writes=w)

    def ACT(out, in_, func, r=(), w=(), **kw):
        return S.op("act", lambda: nc.scalar.activation(out=out, in_=in_, func=func, **kw), reads=r, writes=w)

    def TS_(eng, out, in0, s1, s2, op0, op1=None, r=(), w=()):
        e = nc.vector if eng == "dve" else nc.gpsimd
        if op1 is None:
            return S.op(eng, lambda: e.tensor_scalar(out=out, in0=in0, scalar1=s1, scalar2=None, op0=op0), reads=r, writes=w)
        return S.op(eng, lambda: e.tensor_scalar(out=out, in0=in0, scalar1=s1, scalar2=s2, op0=op0, op1=op1), reads=r, writes=w)

    def STT(out, in0, scalar, in1, op0, op1, r=(), w=()):
        return S.op("dve", lambda: nc.vector.scalar_tensor_tensor(out=out, in0=in0, scalar=scalar, in1=in1, op0=op0, op1=op1),
                    reads=r, writes=w)

    def TT(eng, out, in0, in1, op, r=(), w=()):
        e = nc.vector if eng == "dve" else nc.gpsimd
        return S.op(eng, lambda: e.tensor_tensor(out=out, in0=in0, in1=in1, op=op), reads=r, writes=w)

    def CP(eng, out, in_, r=(), w=()):
        if eng == "act":
            return S.op("act", lambda: nc.scalar.copy(out=out, in_=in_), reads=r, writes=w)
        e = nc.vector if eng == "dve" else nc.gpsimd
        return S.op(eng, lambda: e.tensor_copy(out=out, in_=in_), reads=r, writes=w)

    def MSET(eng, ap, val, w=()):
        e = nc.vector if eng == "dve" else nc.gpsimd
        return S.op(eng, lambda: e.memset(ap, val), writes=w)

    dmaq = Rot(["sp", "act"])

    def RSQ(cl, bcl, n, i_src, i_tmp, i_dst, invn):
        ACT(cl[:n, i_tmp:i_tmp + 1], cl[:n, i_src:i_src + 1], AF.Ln, r=[bcl, bepsc], w=[bcl], bias=epsc[:n, 0:1], scale=invn)
        ACT(cl[:n, i_dst:i_dst + 1], cl[:n, i_tmp:i_tmp + 1], AF.Exp, r=[bcl], w=[bcl], scale=-0.5)

    def DMA(out, in_, r=(), w=(), q=None, **kw):
        return S.dma(q or "sp", out, in_, reads=r, writes=w, **kw)


    def put_fm(dst, kc_idx, c0, n, src2d, bsrc, bdst, q="act"):
        if c0 < T:
            t0 = c0 // 128
            nt_ = n // 128
            DMA(dst[t0:t0 + nt_, :, kc_idx, :].rearrange("t p x -> p t x"), src2d.rearrange("p (t x) -> p t x", x=128), r=[bsrc], w=[bdst], q=q)
        else:
            DMA(dst[32, :, kc_idx, 0:n], src2d, r=[bsrc], w=[bdst], q=q)

    def bcast_rows(dram_ap_1d_tensor, offset, n, parts=128):
        return bass.AP(tensor=dram_ap_1d_tensor, offset=offset, ap=[[0, parts], [1, n]])

    def load_w(dst, bdst, wl, pieces, KC, cast_eng="pool"):
        mx = 2048 // KC
        sub = []
        for (c0, n) in pieces:
            for a in range(0, n, mx):
                sub.append((c0 + a, min(mx, n - a)))
        groups, cur, tot = [], [], 0
        for (c0, n) in sub:
            if tot + n > mx:
                groups.append(cur)
                cur, tot = [], 0
            cur.append((c0, n))
            tot += n
        groups.append(cur)
        doff = 0
        for grp in groups:
            ntot = sum(n for _, n in grp)
            st, bst = stg.next()
            sv = st[:, 0:KC * ntot].rearrange("p (k n) -> p k n", k=KC)
            off = 0
            for (c0, n) in grp:
                DMA(sv[:, :, off:off + n], wl[:, c0:c0 + n].rearrange("(k p) n -> p k n", p=128), w=[bst])
                off += n
            CP(cast_eng, dst[:, :, doff:doff + ntot], sv, r=[bst], w=[bdst])
            doff += ntot

    def load_gamma(vec_ap):
        DMA(GT[:], bcast_rows(vec_ap.tensor, vec_ap.offset, D), w=[bGT])

    def xrows(xsel, c0, n):
        if xsel == "in":
            return (I["x_p"][c0:c0 + n, :], None) if c0 < T else (I["x_s"][0:n, :], None)
        if xsel == "A":
            return XA[c0:c0 + n, :], bXA
        return XB[c0:c0 + n, :], bXB

    psb7 = PS[7][:].bitcast(BF16)

    def norm_tile(xt, bxt, n, c0, to_out=None, defer=False):
        cl, bcl = col.next()
        ACT(junk[:n], xt[:n], AF.Square, r=[bxt], w=[bjunk, bcl], accum_out=cl[:n, 0:1])
        RSQ(cl, bcl, n, 0, 1, 2, 1.0 / D)
        if to_out is not None:
            yo, byo = xin.next()
            STT(yo[:n], xt[:n], cl[:n, 2:3], GT[:n], ALU.mult, ALU.mult, r=[bxt, bcl, bGT], w=[byo])
            DMA(to_out[0], yo[:n], r=[byo], w=[to_out[1]])
            return None
        yb, byb = ybf.next()
        STT(yb[:n], xt[:n], cl[:n, 2:3], GT[:n], ALU.mult, ALU.mult, r=[bxt, bcl, bGT], w=[byb])

        def fin():
            for kc in range(8):
                TRP(psb7[:, kc * 128:kc * 128 + n], yb[:n, kc * 128:(kc + 1) * 128], identb[:n, :n], r=[byb, bidb], w=[PB[7]])
            CP("act", HT[:, :, c0:c0 + n], psb7.rearrange("p (k t) -> p k t", k=8)[:, :, 0:n], r=[PB[7]], w=[bHT])
        if defer:
            return fin
        fin()
        return None

    DMA(identf[:], CI["ident"], w=[bidf])
    CP("dve", identb[:], identf[:], r=[bidf], w=[bidb])
    MSET("dve", onesb[:], 1.0, w=[bones])
    MSET("dve", epsc[:], EPS, w=[bepsc])
    cmask, bcm = sb("cmask", [64, 64], F32)
    DMA(cmask[:], CI["cmask"], w=[bcm])
    sel, bsel = sb("sel", [4, 512], F32)
    DMA(sel[:], CI["sel"], w=[bsel])
    lp, blp = sb("lp", [128, 640], F32)

    dbg = {}

    def phase_norm1(l, xsel):
        load_gamma(I["g_mix"][l])
        for (c0, n) in TILES:
            xt, bxt = xin.next()
            src, bsrc = xrows(xsel, c0, n)
            DMA(xt[:n], src, r=[bsrc] if bsrc else [], w=[bxt])
            norm_tile(xt, bxt, n, c0)

    def phase_A(l):
        AR.reset()
        wl = I["w_in"][l]
        qqT, bqq = AR.get("qqT", [128, 2, NT], BF16)
        kkT, bkk = AR.get("kkT", [128, 2, T + PAST + TS], BF16)
        Vaug, bV = AR.get("Vaug", [128, 41, 130], BF16)
        oaT, boa = AR.get("oaT", [128, NT], BF16)
        Wq, bWq = AR.get("Wq", [128, 8, 128], BF16)
        Wkv, bWkv = AR.get("Wkv", [128, 8, 256], BF16)
        kvf = Rot([AR.get("kvf%d" % i, [128, 256], F32) for i in range(4)])
        sbr = Rot([AR.get("sbA%d" % i, [128, 512], F32) for i in range(2)])
        ptr = Rot([AR.get("ptA%d" % i, [128, 512], BF16) for i in range(6)])
        o1, bo1 = AR.get("o1", [128, 4, 128], F32)
        ot = Rot([AR.get("otA%d" % i, [128, 128], F32) for i in range(6)])
        ob_ = Rot([AR.get("obA%d" % i, [128, 128], BF16) for i in range(4)])
        ckb, bckb = AR.get("ckb", [128, 8, 128], BF16)
        ahg, bahg = AR.get("ahg", [128, 128], F32)
        lam, blam = AR.get("lam", [128, 8], F32)
        cst, bcst = AR.get("cstA", [128, 2192], F32)
        DMA(cst[:, 0:2048], CI["adg"], w=[bcst])
        DMA(cst[:, 2048:2192], CI["acol"], w=[bcst])
        adg = cst[:, 0:2048].rearrange("p (h q) -> p h q", h=4)
        abase = adg
        acol = cst[:, 2048:2192].rearrange("p (h d) -> p h d", h=4)
        rqb, brqb = AR.get("rqb", [2, 4, 512], BF16)
        st_rq, bst_rq = stg.next()
        DMA(st_rq[0:2, 0:2048], CI["rq"], w=[bst_rq])
        CP("dve", rqb[:], st_rq[0:2, 0:2048].rearrange("p (h q) -> p h q", h=4), r=[bst_rq], w=[brqb])
        MSET("pool", Vaug[:, :, 128:129], 1.0, w=[bV])
        MSET("pool", kkT[64:66, :, :], 1.0, w=[bkk])
        lam_init = 0.8 - 0.6 * math.exp(-0.3 * l)
        for i, nm in enumerate(["a_lq1", "a_lk1", "a_lq2", "a_lk2"]):
            DMA(lp[:, i * 64:(i + 1) * 64], bcast_rows(I[nm].tensor, I[nm][l].offset, 64), w=[blp])
        TT("dve", lp[:, 256:320], lp[:, 0:64], lp[:, 64:128], ALU.mult, r=[blp], w=[blp])
        TT("dve", lp[:, 320:384], lp[:, 128:192], lp[:, 192:256], ALU.mult, r=[blp], w=[blp])
        ACT(junk[:, 0:64], lp[:, 256:320], AF.Identity, r=[blp], w=[bjunk, blam], accum_out=lam[:, 0:1])
        ACT(junk[:, 0:64], lp[:, 320:384], AF.Identity, r=[blp], w=[bjunk, blam], accum_out=lam[:, 1:2])
        ACT(lam[:, 4:6], lam[:, 0:2], AF.Exp, r=[blam], w=[blam])
        TT("dve", lam[:, 2:3], lam[:, 5:6], lam[:, 4:5], ALU.subtract, r=[blam], w=[blam])
        TS_("dve", lam[:, 3:4], lam[:, 2:3], -lam_init, None, ALU.add, r=[blam], w=[blam])
        DMA(ahg[:], bcast_rows(I["a_head_g"].tensor, I["a_head_g"][l].offset, 128), w=[bahg])
        TS_("dve", ahg[:], ahg[:], 1.0 - lam_init, None, ALU.mult, r=[bahg], w=[bahg])
        prj = Rot([(PS[i], PB[i]) for i in (4, 5, 6)])
        if debug:
            print("A pre-heads ninstr", S.ninstr)
        for h in range(4):
            load_w(Wq, bWq, wl, [(h * 64, 64), (256 + h * 64, 64)], 8)
            load_w(Wkv, bWkv, wl, [(512 + h * 64, 64), (768 + h * 64, 64), (1024 + h * 128, 128)], 8)
            for (c0, n) in BLOCKS:
                kc0 = c0 if c0 < T else T + PAST
                for m in range(2):
                    ps, bp = prj.next()
                    for kc in range(8):
                        PE(ps[0:64, 0:n], Wq[:, kc, m * 64:(m + 1) * 64], HT[:, kc, c0:c0 + n], kc == 0, kc == 7, r=[bWq, bHT], w=[bp])
                    CP("act", qqT[0:64, m, c0:c0 + n], ps[0:64, 0:n], r=[bp], w=[bqq])
                    ps, bp = prj.next()
                    for kc in range(8):
                        PE(ps[0:64, 0:n], Wkv[:, kc, m * 64:(m + 1) * 64], HT[:, kc, c0:c0 + n], kc == 0, kc == 7, r=[bWkv, bHT], w=[bp])
                    CP("dve", kkT[0:64, m, kc0:kc0 + n], ps[0:64, 0:n], r=[bp], w=[bkk])
            for m in range(2):
                for (c0, n) in BLOCKS:
                    DMA(qqT[64:66, m, c0:c0 + n], rqb[0:2, h, 0:n], r=[brqb], w=[bqq])
            if debug:
                print("A head", h, "pre-tokmajor ninstr", S.ninstr)
            for ti, (c0, n) in enumerate(TILES):
                ps, bp = prj.next()
                for kc in range(8):
                    PE(ps[:n, 0:256], HT[:, kc, c0:c0 + n], Wkv[:, kc, :], kc == 0, kc == 7, r=[bWkv, bHT], w=[bp])
                kf, bkf = kvf.next()
                CP("act", kf[:n], ps[:n, 0:256], r=[bp], w=[bkf])
                CP("pool", Vaug[:n, ti, 0:128], kf[:n, 128:256], r=[bkf], w=[bV])
                if c0 < T:
                    DMA(O["p_a_k"][l, c0:c0 + n, h * 128:(h + 1) * 128], kf[:n, 0:128], r=[bkf], w=[OB["p_a_k"]])
                    DMA(O["p_a_v"][l, c0:c0 + n, h * 128:(h + 1) * 128], kf[:n, 128:256], r=[bkf], w=[OB["p_a_v"]])
                else:
                    DMA(O["s_a_k"][l, 0:n, h * 128:(h + 1) * 128], kf[:n, 0:128], r=[bkf], w=[OB["s_a_k"]])
                    DMA(O["s_a_v"][l, 0:n, h * 128:(h + 1) * 128], kf[:n, 128:256], r=[bkf], w=[OB["s_a_v"]])
            st, bst = stg.next()
            sv = st[:, 0:1024].rearrange("p (t c) -> p t c", t=8)
            DMA(sv, I["cak"][l, :, h * 128:(h + 1) * 128].rearrange("(t p) c -> p t c", p=128), w=[bst])
            CP("pool", ckb[:], sv, r=[bst], w=[bckb])
            for m in range(2):
                for t8 in range(8):
                    TRP(psb7[0:64, t8 * 128:(t8 + 1) * 128], ckb[:, t8, m * 64:(m + 1) * 64], identb[:], r=[bckb, bidb], w=[PB[7]])
                CP("act", kkT[0:64, m, T:T + PAST], psb7[0:64, 0:1024], r=[PB[7]], w=[bkk])
            st, bst = stg.next()
            sv = st[:, 0:1024].rearrange("p (t c) -> p t c", t=8)
            DMA(sv, I["cav"][l, :, h * 128:(h + 1) * 128].rearrange("(t p) c -> p t c", p=128), w=[bst])
            CP("pool", Vaug[:, 33:41, 0:128], sv, r=[bst], w=[bV])
            if debug:
                print("A head", h, "pre-attn ninstr", S.ninstr)
            items = []
            prjA = Rot([(PS[i], PB[i]) for i in (2, 3, 4, 5, 6, 7)])
            for qb, (q0, qn) in enumerate(BLOCKS):
                sample = q0 >= T
                nsub = 1 if sample else 4
                if not sample:
                    kbl = []
                    for kb in range(4 * qb + 4):
                        j = kb - 4 * qb
                        if j < 0:
                            kbl.append(dict(kc=kb * 128, nk=128, vt=kb, lo=0, tile=abase, Dd=(q0 - kb * 128) // 128, last_sub=None))
                        else:
                            kbl.append(dict(kc=kb * 128, nk=128, vt=kb, lo=128 * j, tile=adg, Dd=0, last_sub=j))
                else:
                    kbl = [dict(kc=T + kb * 128, nk=128, vt=33 + kb, lo=0, tile=abase, Dd=8 - kb, last_sub=None) for kb in range(8)]
                    kbl.append(dict(kc=T + PAST, nk=TS, vt=32, lo=0, tile=adg, Dd=0, last_sub=0))
                for m in range(2):
                    for bi, kb in enumerate(kbl):
                        it = dict(kb)
                        it.update(q0=q0, qn=qn, m=m, bi=bi, nb=len(kbl), sample=sample, nsub=nsub)
                        items.append(it)

            def stage1(it):
                lo, nk, qn, q0, m = it["lo"], it["nk"], it["qn"], it["q0"], it["m"]
                ps, bp = prjA.next()
                offd = it["last_sub"] is None
                p_, bp_ = ptr.next()
                if offd:
                    PE(ps[:nk, lo:qn], kkT[0:66, m, it["kc"]:it["kc"] + nk], qqT[0:66, m, q0 + lo:q0 + qn], True, True,
                       r=[bkk, bqq], w=[bp])
                    ACT(p_[:nk, lo:qn], ps[:nk, lo:qn], AF.Exp, r=[bp, bcst], w=[bp_],
                        bias=acol[:nk, h, it["Dd"] + 3:it["Dd"] + 4], scale=0.125)
                else:
                    PE(ps[:nk, lo:qn], kkT[0:64, m, it["kc"]:it["kc"] + nk], qqT[0:64, m, q0 + lo:q0 + qn], True, True,
                       r=[bkk, bqq], w=[bp])
                    s_, bs_ = sbr.next()
                    STT(s_[:nk, lo:qn], ps[:nk, lo:qn], 0.125, it["tile"][:nk, h, 0:qn - lo], ALU.mult, ALU.add, r=[bp, bcst], w=[bs_])
                    ACT(p_[:nk, lo:qn], s_[:nk, lo:qn], AF.Exp, r=[bs_], w=[bp_])
                it["p"] = (p_, bp_)

            def stage2(it):
                lo, nk, qn, q0, m = it["lo"], it["nk"], it["qn"], it["q0"], it["m"]
                p_, bp_ = it["p"]
                for sub in range(lo // 128, it["nsub"]):
                    sn_ = min(128, qn - sub * 128)
                    if it["last_sub"] is None:
                        last = it["sample"] and (it["bi"] == it["nb"] - 1)
                    else:
                        last = it["last_sub"] == sub
                    bk_, co_ = sub // 2, (sub % 2) * 256
                    PE(PS[bk_][:sn_, co_:co_ + 129], p_[:nk, sub * 128:sub * 128 + sn_], Vaug[:nk, it["vt"], 0:129],
                       it["bi"] == 0 and sub % 2 == 0, last and (sub % 2 == 1 or it["nsub"] == 1), r=[bp_, bV], w=[PB[bk_]])
                if it["bi"] != it["nb"] - 1:
                    return
                ns_ = it["nsub"]
                snb = min(128, qn)
                clA, bclA = col.next()
                clB, bclB = col.next()
                outs = []
                for sub in range(ns_):
                    sn_ = min(128, qn - sub * 128)
                    cl, bcl = col.next()
                    bk_, co_ = sub // 2, (sub % 2) * 256
                    acc_ = PS[bk_][:sn_, co_:co_ + 129]
                    S.op("dve", lambda: nc.vector.reciprocal(out=cl[:sn_, 0:1], in_=acc_[:, 128:129]), reads=[PB[bk_]], writes=[bcl])
                    if m == 0:
                        TS_("dve", o1[:sn_, sub, :], acc_[:, 0:128], cl[:sn_, 0:1], None, ALU.mult, r=[PB[bk_], bcl], w=[bo1])
                    else:
                        TT("dve", cl[:sn_, 1:2], cl[:sn_, 0:1], lam[:sn_, 3:4], ALU.mult, r=[bcl, blam], w=[bcl])
                        o_, bo_ = ot.next()
                        STT(o_[:sn_], acc_[:, 0:128], cl[:sn_, 1:2], o1[:sn_, sub, :], ALU.mult, ALU.add,
                            r=[PB[bk_], bcl, bo1], w=[bo_])
                        ACT(junk[:sn_, 0:128], o_[:sn_], AF.Square, r=[bo_], w=[bjunk, bclA], accum_out=clA[:sn_, sub:sub + 1])
                        outs.append((sub, sn_, o_, bo_))
                if m == 1:
                    ACT(clB[:snb, 0:ns_], clA[:snb, 0:ns_], AF.Ln, r=[bclA, bepsc], w=[bclB], bias=epsc[:snb, 0:1], scale=1.0 / 128)
                    ACT(clB[:snb, 4:4 + ns_], clB[:snb, 0:ns_], AF.Exp, r=[bclB], w=[bclB], scale=-0.5)
                    it["tr"] = (outs, clB, bclB)

            def stage3(it):
                outs, clB, bclB = it["tr"]
                q0 = it["q0"]
                for (sub, sn_, o_, bo_) in outs:
                    ob1, bob1 = ob_.next()
                    STT(ob1[:sn_], o_[:sn_], clB[:sn_, 4 + sub:5 + sub], ahg[:sn_], ALU.mult, ALU.mult, r=[bo_, bclB, bahg], w=[bob1])
                    pt_, bpt_ = prjA.next()
                    ptb = pt_[:].bitcast(BF16)
                    TRP(ptb[:, 0:sn_], ob1[:sn_, :], identb[:sn_, :sn_], r=[bob1, bidb], w=[bpt_])
                    CP("act", oaT[:, q0 + sub * 128:q0 + sub * 128 + sn_], ptb[:, 0:sn_], r=[bpt_], w=[boa])

            LOOK = 5
            for i in range(min(LOOK, len(items))):
                stage1(items[i])
            pend = None
            for i, it in enumerate(items):
                stage2(it)
                if i + LOOK < len(items):
                    stage1(items[i + LOOK])
                if pend is not None:
                    stage3(pend)
                    pend = None
                if "tr" in it:
                    pend = it
            if pend is not None:
                stage3(pend)
            DMA(OAT[h], oaT[:], r=[boa], w=[bOAT])
        if debug:
            dbg["OAT"] = OAT

    def phase_B(l):
        AR.reset()
        wl = I["w_in"][l]
        qT, bq = AR.get("qTb", [128, NT], BF16)
        kT, bk = AR.get("kTb", [128, T + NBAND + TS], BF16)
        Vb, bV = AR.get("Vb", [128, 37, 2, 66], BF16)
        obT, bobT = AR.get("obT", [128, NT], BF16)
        Wq, bWq = AR.get("WqB", [128, 8, 128], BF16)
        Wkv, bWkv = AR.get("WkvB", [128, 8, 256], BF16)
        kvf = Rot([AR.get("kvfB%d" % i, [128, 256], F32) for i in range(4)])
        sbr = Rot([AR.get("sbB%d" % i, [128, 128], F32) for i in range(10)])
        ptr = Rot([AR.get("ptB%d" % i, [128, 128], BF16) for i in range(11)])
        obt = Rot([AR.get("obtB%d" % i, [128, 128], BF16) for i in range(6)])
        ckb, bckb = AR.get("ckbB", [128, 4, 128], BF16)
        T34, bT34 = AR.get("T34", [128, 8, 2, 128], F32)
        c256, bc256 = AR.get("c256", [128, 8], F32)
        tmp, btmp = AR.get("tmpB", [128, 128], F32)
        zt, bzt = AR.get("ztB", [128, 384], F32)
        cst, bcst = AR.get("cstB", [128, 512], F32)
        DMA(cst[:], CI["bmask"], w=[bcst])
        bmask = cst[:].rearrange("p (m q) -> p m q", m=4)
        MSET("pool", Vb[:, :, :, 64:65], 1.0, w=[bV])
        MSET("dve", zt[:], 0.0, w=[bzt])
        for hh in range(8):
            DMA(ZB[hh], zt[:], r=[bzt], w=[bZB])
        rel = I["b_rel"]
        for hh in range(8):
            DMA(ZB[hh, :, 0:257], bcast_rows(rel.tensor, rel[l, hh].offset, 257), r=[], w=[bZB])
        DMA(c256[:], bass.AP(tensor=rel.tensor, offset=rel[l, 0].offset + 256, ap=[[0, 128], [257, 8]]), w=[bc256],
            allow_slow_non_contiguous=True)
        for hh in range(8):
            zb = ZB[hh]
            DMA(T34[:, hh, 1, :], bass.AP(tensor=zb.tensor, offset=zb.offset + 128, ap=[[383, 128], [1, 128]]), r=[bZB], w=[bT34])
            DMA(T34[:, hh, 0, :], bass.AP(tensor=zb.tensor, offset=zb.offset + 256, ap=[[383, 128], [1, 128]]), r=[bZB], w=[bT34])
            TT("dve", T34[:, hh, 1, :], T34[:, hh, 1, :], bmask[:, 1, :], ALU.add, r=[bT34, bcst], w=[bT34])
            TT("dve", tmp[:], T34[:, hh, 0, :], bmask[:, 2, :], ALU.mult, r=[bT34, bcst], w=[btmp])
            STT(T34[:, hh, 0, :], bmask[:, 3, :], c256[:, hh:hh + 1], tmp[:], ALU.mult, ALU.add, r=[btmp, bcst, bc256], w=[bT34])
        prj = Rot([(PS[i], PB[i]) for i in (4, 5, 6)])
        acc = Rot([(PS[i], PB[i]) for i in (0, 1)])
        for hp in range(4):
            load_w(Wq, bWq, wl, [(1536 + hp * 128, 128)], 8)
            load_w(Wkv, bWkv, wl, [(2048 + hp * 128, 128), (2560 + hp * 128, 128)], 8)
            for (c0, n) in BLOCKS:
                ps, bp = prj.next()
                for kc in range(8):
                    PE(ps[:, 0:n], Wq[:, kc, :], HT[:, kc, c0:c0 + n], kc == 0, kc == 7, r=[bWq, bHT], w=[bp])
                CP("act", qT[:, c0:c0 + n], ps[:, 0:n], r=[bp], w=[bq])
                ps, bp = prj.next()
                for kc in range(8):
                    PE(ps[:, 0:n], Wkv[:, kc, 0:128], HT[:, kc, c0:c0 + n], kc == 0, kc == 7, r=[bWkv, bHT], w=[bp])
                kc0 = c0 if c0 < T else T + NBAND
                CP("dve", kT[:, kc0:kc0 + n], ps[:, 0:n], r=[bp], w=[bk])
            for ti, (c0, n) in enumerate(TILES):
                ps, bp = prj.next()
                for kc in range(8):
                    PE(ps[:n, 0:256], HT[:, kc, c0:c0 + n], Wkv[:, kc, :], kc == 0, kc == 7, r=[bWkv, bHT], w=[bp])
                kf, bkf = kvf.next()
                CP("act", kf[:n], ps[:n, 0:256], r=[bp], w=[bkf])
                CP("pool", Vb[:n, ti, :, 0:64], kf[:n, 128:256].rearrange("p (h d) -> p h d", h=2), r=[bkf], w=[bV])
                if c0 >= T - 512:
                    if c0 < T:
                        r_ = c0 - (T - 512)
                        DMA(O["p_b_k"][l, r_:r_ + n, hp * 128:(hp + 1) * 128], kf[:n, 0:128], r=[bkf], w=[OB["p_b_k"]])
                        DMA(O["p_b_v"][l, r_:r_ + n, hp * 128:(hp + 1) * 128], kf[:n, 128:256], r=[bkf], w=[OB["p_b_v"]])
                    else:
                        DMA(O["s_b_k"][l, 0:n, hp * 128:(hp + 1) * 128], kf[:n, 0:128], r=[bkf], w=[OB["s_b_k"]])
                        DMA(O["s_b_v"][l, 0:n, hp * 128:(hp + 1) * 128], kf[:n, 128:256], r=[bkf], w=[OB["s_b_v"]])
            st, bst = stg.next()
            sv = st[:, 0:512].rearrange("p (t c) -> p t c", t=4)
            DMA(sv, I["cbk"][l, :, hp * 128:(hp + 1) * 128].rearrange("(t p) c -> p t c", p=128), w=[bst])
            CP("pool", ckb[:], sv, r=[bst], w=[bckb])
            for t4 in range(4):
                TRP(psb7[:, t4 * 128:(t4 + 1) * 128], ckb[:, t4, :], identb[:], r=[bckb, bidb], w=[PB[7]])
            CP("act", kT[:, T:T + NBAND], psb7[:, 0:512], r=[PB[7]], w=[bk])
            st, bst = stg.next()
            sv = st[:, 0:512].rearrange("p (t h d) -> p t h d", t=4, h=2)
            DMA(st[:, 0:512].rearrange("p (t c) -> p t c", t=4),
                I["cbv"][l, :, hp * 128:(hp + 1) * 128].rearrange("(t p) c -> p t c", p=128), w=[bst])
            for t4 in range(4):
                CP("pool", Vb[:, 33 + t4, :, 0:64], sv[:, t4], r=[bst], w=[bV])
            items = []
            for qt, (q0, qn) in enumerate(TILES):
                sample = q0 >= T
                for hh in range(2):
                    if not sample:
                        kbl = []
                        for i in range(5):
                            kbi = qt - 4 + i
                            if kbi >= 0:
                                kbl.append(dict(kc=kbi * 128, nk=128, vt=kbi, kind=i))
                    else:
                        kbl = [dict(kc=T + j * 128, nk=128, vt=33 + j, kind=(3 if j == 3 else 1)) for j in range(4)]
                        kbl.append(dict(kc=T + NBAND, nk=TS, vt=32, kind=4))
                    for bi, kb in enumerate(kbl):
                        it = dict(kb)
                        it.update(q0=q0, qn=qn, hh=hh, bi=bi, nb=len(kbl), qt=qt)
                        items.append(it)
            state = {}
            sprj = Rot([(PS[b_], PB[b_]) for b_ in (2, 3, 4, 5, 6)])

            def stage1(it):
                nk, kind, qn, q0, hh = it["nk"], it["kind"], it["qn"], it["q0"], it["hh"]
                H = 2 * hp + hh
                r0 = 64 * hh
                ps, bp = sprj.next()
                PE(ps[:nk, 0:qn], kT[r0:r0 + 64, it["kc"]:it["kc"] + nk], qT[r0:r0 + 64, q0:q0 + qn], True, True, r=[bk, bq], w=[bp])
                p_, bp_ = ptr.next()
                if kind in (1, 2):
                    ACT(p_[:nk, 0:qn], ps[:nk, 0:qn], AF.Exp, r=[bp, bc256], w=[bp_], bias=c256[:nk, H:H + 1], scale=0.125)
                else:
                    s_, bs_ = sbr.next()
                    tile = bmask[:nk, 0, 0:qn] if kind == 0 else (T34[:nk, H, 0, 0:qn] if kind == 3 else T34[:nk, H, 1, 0:qn])
                    STT(s_[:nk, 0:qn], ps[:nk, 0:qn], 0.125, tile, ALU.mult, ALU.add, r=[bp, bcst, bT34], w=[bs_])
                    if kind == 0:
                        ACT(p_[:nk, 0:qn], s_[:nk, 0:qn], AF.Exp, r=[bs_, bc256], w=[bp_], bias=c256[:nk, H:H + 1], scale=1.0)
                    else:
                        ACT(p_[:nk, 0:qn], s_[:nk, 0:qn], AF.Exp, r=[bs_], w=[bp_])
                it["p"] = (p_, bp_)

            def stage2(it):
                nk, qn, hh = it["nk"], it["qn"], it["hh"]
                r0 = 64 * hh
                if it["bi"] == 0:
                    state["ac"] = acc.next()
                    if hh == 0:
                        state["ot"] = obt.next()
                ac, bac = state["ac"]
                o_t, bo_t = state["ot"]
                p_, bp_ = it["p"]
                PE(ac[:qn, 0:65], p_[:nk, 0:qn], Vb[:nk, it["vt"], hh, 0:65], it["bi"] == 0, it["bi"] == it["nb"] - 1, r=[bp_, bV], w=[bac])
                if it["bi"] != it["nb"] - 1:
                    return
                cl, bcl = col.next()
                S.op("dve", lambda: nc.vector.reciprocal(out=cl[:qn, 0:1], in_=ac[:qn, 64:65]), reads=[bac], writes=[bcl])
                TS_("dve", o_t[:qn, r0:r0 + 64], ac[:qn, 0:64], cl[:qn, 0:1], None, ALU.mult, r=[bac, bcl], w=[bo_t])
                if hh == 1:
                    it["tr"] = (o_t, bo_t)

            def stage3(it):
                o_t, bo_t = it["tr"]
                q0, qn = it["q0"], it["qn"]
                TRP(psb7[:, 0:qn], o_t[:qn, :], identb[:qn, :qn], r=[bo_t, bidb], w=[PB[7]])
                CP("act", obT[:, q0:q0 + qn], psb7[:, 0:qn], r=[PB[7]], w=[bobT])

            LOOK = 4
            for i in range(min(LOOK, len(items))):
                stage1(items[i])
            pend = None
            for i, it in enumerate(items):
                stage2(it)
                if i + LOOK < len(items):
                    stage1(items[i + LOOK])
                if pend is not None:
                    stage3(pend)
                    pend = None
                if "tr" in it:
                    pend = it
            if pend is not None:
                stage3(pend)
            DMA(OBT[hp], obT[:], r=[bobT], w=[bOBT])
        if debug:
            dbg["OBT"] = OBT

    def phase_C(l):
        AR.reset()
        wl = I["w_in"][l]
        WST, bWST = AR.get("WST", [64, 65, 8], F32)
        DECB, bDECB = AR.get("DECB", [128, 4, 65], F32)
        chg, bchg = AR.get("chg", [64, 128], F32)
        cw, bcw = AR.get("cw", [128, 8, 4], F32)
        cb, bcb = AR.get("cb", [128, 8], F32)
        gp, bgp = AR.get("gp", [4, 8], F32)
        scv, bscv = AR.get("scv", [4, D], F32)
        mark = AR.off
        prj = Rot([(PS[i], PB[i]) for i in (4, 5, 6)])
        DMA(gp[:, 0:1], bass.AP(tensor=I["c_b_i"].tensor, offset=I["c_b_i"][l].offset, ap=[[1, 4], [1, 1]]), w=[bgp])
        DMA(gp[:, 1:2], bass.AP(tensor=I["c_b_f"].tensor, offset=I["c_b_f"][l].offset, ap=[[1, 4], [1, 1]]), w=[bgp])
        DMA(gp[:, 4:5], bass.AP(tensor=I["sm"].tensor, offset=I["sm"][l].offset, ap=[[1, 4], [1, 1]]), w=[bgp])
        TS_("dve", gp[:, 1:2], gp[:, 1:2], -1.0, None, ALU.mult, r=[bgp], w=[bgp])
        MSET("dve", gp[:, 2:3], 1.0, w=[bgp])
        MSET("dve", gp[:, 3:4], math.log(128.0 ** -0.5), w=[bgp])
        DMA(chg[:], bcast_rows(I["c_head_g"].tensor, I["c_head_g"][l].offset, 128, parts=64), w=[bchg])
        cwl = I["c_conv_w"]
        for j in range(4):
            DMA(cw[:, :, j], bass.AP(tensor=cwl.tensor, offset=cwl[l, j].offset, ap=[[1, 128], [128, 8]]), w=[bcw],
                allow_slow_non_contiguous=True)
        cbl = I["c_conv_b"]
        DMA(cb[:], bass.AP(tensor=cbl.tensor, offset=cbl[l].offset, ap=[[1, 128], [128, 8]]), w=[bcb], allow_slow_non_contiguous=True)
        DMA(scv[0:3, :], I["sconv"][l], w=[bscv])
        A1, bA1 = AR.get("A1", [4, NT], F32)
        A2, bA2 = AR.get("A2", [4, NT], F32)
        A3, bA3 = AR.get("A3", [4, NT], F32)
        ONE, bONE = AR.get("ONE", [4, NT], BF16)
        CM, bCM = AR.get("CM", [4, 65], F32)
        MI, bMI = AR.get("MI", [4, 65], F32)
        ME, bME = AR.get("ME", [4, 65], F32)
        DEC, bDEC = AR.get("DEC", [4, 65], F32)
        Wif, bWif = AR.get("Wif", [128, 8, 8], BF16)
        load_w(Wif, bWif, wl, [(5120, 8)], 8)
        MSET("pool", ONE[:], 1.0, w=[bONE])
        for (c0, n) in BLOCKS:
            ps, bp = prj.next()
            for kc in range(8):
                PE(ps[0:4, 0:n], Wif[:, kc, 0:4], HT[:, kc, c0:c0 + n], kc == 0, kc == 7, r=[bWif, bHT], w=[bp])
            ACT(A1[:, c0:c0 + n], ps[0:4, 0:n], AF.Identity, r=[bp, bgp], w=[bA1], bias=gp[:, 0:1], scale=1.0)
            ps, bp = prj.next()
            for kc in range(8):
                PE(ps[0:4, 0:n], Wif[:, kc, 4:8], HT[:, kc, c0:c0 + n], kc == 0, kc == 7, r=[bWif, bHT], w=[bp])
            ACT(A2[:, c0:c0 + n], ps[0:4, 0:n], AF.Exp, r=[bp, bgp], w=[bA2], bias=gp[:, 1:2], scale=-1.0)
        ACT(A2[:], A2[:], AF.Ln, r=[bA2, bgp], w=[bA2], bias=gp[:, 2:3], scale=1.0)
        for (a, b_) in ((0, T), (T, NT)):
            S.op("dve", lambda: nc.vector.tensor_tensor_scan(out=A3[:, a:b_], data0=ONE[:, a:b_], data1=A2[:, a:b_], initial=0.0,
                                                             op0=ALU.mult, op1=ALU.add), reads=[bONE, bA2], writes=[bA3])
        TT("dve", A1[:], A1[:], A3[:], ALU.add, r=[bA1, bA3], w=[bA1])
        S.op("dve", lambda: nc.vector.tensor_reduce(out=CM[:, 0:64], in_=A1[:, 0:T].rearrange("p (c s) -> p c s", s=64), axis=AX.X, op=ALU.max),
             reads=[bA1], writes=[bCM])
        S.op("dve", lambda: nc.vector.tensor_reduce(out=CM[:, 64:65], in_=A1[:, T:NT], axis=AX.X, op=ALU.max), reads=[bA1], writes=[bCM])
        S.op("dve", lambda: nc.vector.tensor_tensor_scan(out=MI[:, 0:64], data0=ONE[:, 0:64], data1=CM[:, 0:64], initial=0.0,
                                                         op0=ALU.mult, op1=ALU.max), reads=[bONE, bCM], writes=[bMI])
        TT("dve", MI[:, 64:65], CM[:, 64:65], gp[:, 4:5], ALU.max, r=[bCM, bgp], w=[bMI])
        MSET("dve", ME[:, 0:1], 0.0, w=[bME])
        CP("dve", ME[:, 1:64], MI[:, 0:63], r=[bMI], w=[bME])
        CP("dve", ME[:, 64:65], gp[:, 4:5], r=[bgp], w=[bME])
        TT("dve", DEC[:], ME[:], MI[:], ALU.subtract, r=[bME, bMI], w=[bDEC])
        ACT(DEC[:], DEC[:], AF.Exp, r=[bDEC], w=[bDEC])
        TT("dve", gp[:, 5:6], MI[:, 63:64], A3[:, T - 1:T], ALU.subtract, r=[bMI, bA3], w=[bgp])
        TT("dve", gp[:, 6:7], MI[:, 64:65], A3[:, NT - 1:NT], ALU.subtract, r=[bMI, bA3], w=[bgp])
        mi_b = MI[:, 0:64].unsqueeze(2).to_broadcast([4, 64, 64])
        a1v = A1[:, 0:T].rearrange("p (c s) -> p c s", s=64)
        a3v = A3[:, 0:T].rearrange("p (c s) -> p c s", s=64)
        TT("dve", a1v, a1v, mi_b, ALU.subtract, r=[bA1, bMI], w=[bA1])
        TS_("dve", A1[:, T:NT], A1[:, T:NT], MI[:, 64:65], None, ALU.subtract, r=[bA1, bMI], w=[bA1])
        ACT(A1[:], A1[:], AF.Exp, r=[bA1, bgp], w=[bA1], bias=gp[:, 3:4], scale=1.0)
        TT("dve", a3v, a3v, mi_b, ALU.subtract, r=[bA3, bMI], w=[bA3])
        TS_("dve", A3[:, T:NT], A3[:, T:NT], MI[:, 64:65], None, ALU.subtract, r=[bA3, bMI], w=[bA3])
        ACT(A3[:], A3[:], AF.Exp, r=[bA3], w=[bA3])
        DMA(bass.AP(tensor=O["p_c_m"].tensor, offset=O["p_c_m"][l].offset, ap=[[1, 4], [1, 1]]), gp[:, 5:6], r=[bgp], w=[OB["p_c_m"]])
        DMA(bass.AP(tensor=O["s_c_m"].tensor, offset=O["s_c_m"][l].offset, ap=[[1, 4], [1, 1]]), gp[:, 6:7], r=[bgp], w=[OB["s_c_m"]])
        for c in range(65):
            L = 64 if c < 64 else TS
            a = c * 64
            if c < 64:
                o1_, o2_, bb = PS[0][:L, c * 8:c * 8 + 4], PS[0][:L, c * 8 + 4:c * 8 + 8], PB[0]
            else:
                o1_, o2_, bb = PS[1][:L, 0:4], PS[1][:L, 4:8], PB[1]
            PE(o1_, A1[:, a:a + L], identf[0:4, 0:4], True, True, r=[bA1, bidf], w=[bb])
            PE(o2_, A3[:, a:a + L], identf[0:4, 0:4], True, True, r=[bA3, bidf], w=[bb])
        CP("dve", WST[:, 0:64, :], PS[0][0:64, :].rearrange("p (c e) -> p c e", e=8), r=[PB[0]], w=[bWST])
        CP("dve", WST[0:TS, 64, :], PS[1][0:TS, 0:8], r=[PB[1]], w=[bWST])
        for h in range(4):
            PE(PS[2][:, h * 65:(h + 1) * 65], sel[0:4, h * 128:(h + 1) * 128], DEC[:, :], True, True, r=[bsel, bDEC], w=[PB[2]])
        CP("dve", DECB[:], PS[2][:, 0:260].rearrange("p (h c) -> p h c", h=4), r=[PB[2]], w=[bDECB])
        barrier()
        for hp2 in range(2):
            AR.off = mark
            QK, bQK = AR.get("QK", [128, 4, NT], BF16)
            Wc, bWc = AR.get("Wc", [128, 8, 128], BF16)
            Wv, bWv = AR.get("WvC", [128, 8, 256], BF16)
            Wo, bWo = AR.get("WoC", [128, 8, 256], BF16)
            CS, bCS = AR.get("CS", [128, 2, 130], F32)
            CSs, bCSs = AR.get("CSs", [128, 2, 130], F32)
            CDB, bCDB = AR.get("CDB", [128, 2, 130], BF16)
            CDBs, bCDBs = AR.get("CDBs", [128, 2, 130], BF16)
            cvr = Rot([AR.get("cvr%d" % i, [4, 128], F32) for i in range(4)])
            VA = Rot([AR.get("VA%d" % i, [64, 2, 130], BF16) for i in range(5)])
            ATr = Rot([AR.get("AT%d" % i, [64, 64], BF16) for i in range(10)])
            KPr = Rot([AR.get("KP%d" % i, [64, 128], BF16) for i in range(10)])
            OCB = Rot([AR.get("OCB%d" % i, [128, 2, 512], BF16) for i in range(2)])
            cf, bcf = AR.get("cf", [128, 2, 128], F32)
            conv_mark = AR.off
            xp, bxp = AR.get("xp", [128, T + 3], F32)
            u, bu = AR.get("u", [128, 2048], F32)
            xps, bxps = AR.get("xps", [128, TS + 3], F32)
            MSET("dve", xp[:, 0:3], 0.0, w=[bxp])
            for slot in range(4):
                cc = (2 * hp2 + slot) if slot < 2 else (4 + 2 * hp2 + slot - 2)
                load_w(Wc, bWc, wl, [(3072 + cc * 128, 128)], 8)
                for (c0, n) in BLOCKS:
                    ps, bp = prj.next()
                    for kc in range(8):
                        PE(ps[:, 0:n], Wc[:, kc, :], HT[:, kc, c0:c0 + n], kc == 0, kc == 7, r=[bWc, bHT], w=[bp])
                    if c0 < T:
                        CP("act", xp[:, 3 + c0:3 + c0 + n], ps[:, 0:n], r=[bp], w=[bxp])
                    else:
                        CP("act", xps[:, 3:3 + n], ps[:, 0:n], r=[bp], w=[bxps])
                for (t0, onm) in ((T - 3, "p_c_conv"), (NT - 3, "s_c_conv")):
                    ps, bp = prj.next()
                    for kc in range(8):
                        PE(ps[0:3, 0:128], HT[:, kc, t0:t0 + 3], Wc[:, kc, :], kc == 0, kc == 7, r=[bWc, bHT], w=[bp])
                    cv_, bcv_ = cvr.next()
                    CP("dve", cv_[0:3, :], ps[0:3, 0:128], r=[bp], w=[bcv_])
                    DMA(O[onm][l, :, cc * 128:(cc + 1) * 128], cv_[0:3, :], r=[bcv_], w=[OB[onm]])
                PE(PS[3][:, 0:3], scv[0:3, cc * 128:(cc + 1) * 128], identf[0:3, 0:3], True, True, r=[bscv, bidf], w=[PB[3]])
                CP("dve", xps[:, 0:3], PS[3][:, 0:3], r=[PB[3]], w=[bxps])
                for (xx, bxx, n, dcol) in ((xp, bxp, T, 0), (xps, bxps, TS, T)):
                    for hs in range(0, n, 2048):
                        hn = min(2048, n - hs)
                        TS_("dve", u[:, 0:hn], xx[:, hs:hs + hn], cw[:, cc, 0:1], cb[:, cc:cc + 1], ALU.mult, ALU.add,
                            r=[bxx, bcw, bcb], w=[bu])
                        for j in range(1, 4):
                            STT(u[:, 0:hn], xx[:, hs + j:hs + j + hn], cw[:, cc, j:j + 1], u[:, 0:hn], ALU.mult, ALU.add,
                                r=[bxx, bcw, bu], w=[bu])
                        ACT(QK[:, slot, dcol + hs:dcol + hs + hn], u[:, 0:hn], AF.Silu, r=[bu], w=[bQK])
            MSET("dve", CS[:], 0.0, w=[bCS])
            MSET("pool", CDB[:], 0.0, w=[bCDB])
            DMA(cf[:], I["sC"][l, 2 * hp2:2 * hp2 + 2].rearrange("h e d -> e h d"), w=[bcf])
            for hh in range(2):
                TRP(PS[3][:, 0:128], cf[:, hh, :], identf[:], r=[bcf, bidf], w=[PB[3]])
                CP("dve", CSs[:, hh, 0:128], PS[3][:, 0:128], r=[PB[3]], w=[bCSs])
            snl = I["sn"]
            DMA(CSs[:, :, 128], bass.AP(tensor=snl.tensor, offset=snl[l, 2 * hp2].offset, ap=[[1, 128], [128, 2]]), w=[bCSs],
                allow_slow_non_contiguous=True)
            for hh in range(2):
                H = 2 * hp2 + hh
                TS_("dve", CDBs[:, hh, 0:129], CSs[:, hh, 0:129], DECB[:, H, 64:65], None, ALU.mult, r=[bCSs, bDECB], w=[bCDBs])
            load_w(Wv, bWv, wl, [(4096 + hp2 * 256, 256)], 8)
            load_w(Wo, bWo, wl, [(4608 + hp2 * 256, 256)], 8)
            for it in VA.items:
                MSET("pool", it[0][:, :, 128:129], 1.0, w=[it[1]])
            barrier()
            AR.off = conv_mark
            HG2 = [AR.get("HG%d" % i, [64, 8, 256], F32) for i in range(2)]
            SQ, bSQ = AR.get("SQ", [64, 8, 256], BF16)
            OGG2 = [AR.get("OGG%d" % i, [64, 8, 256], BF16) for i in range(2)]
            OKG, bOKG = SQ, bSQ
            post = []
            ssr, bssr = AR.get("ssr", [64, 48], F32)
            if debug and hp2 == 0:
                print("C2 arena words used", AR.off, "of 24576; conv_mark", conv_mark)
            prj = Rot([(PS[i], PB[i]) for i in range(7)])
            ocbs = {"cur": OCB.next()}
            ctxs = {}

            def stageX(c):
                smp = c == 64
                L = TS if smp else 64
                a = c * 64
                j8 = c % 8
                OGG, bOGG = OGG2[(c // 8) % 2]
                va, bva = VA.next()
                ps, bp = prj.next()
                for kc in range(8):
                    PE(ps[:L, 0:256], HT[:, kc, a:a + L], Wv[:, kc, :], kc == 0, kc == 7, r=[bWv, bHT], w=[bp])
                CP("act", va[:L, :, 0:128], ps[:L, 0:256].rearrange("p (h e) -> p h e", h=2), r=[bp], w=[bva])
                ps, bp = prj.next()
                for kc in range(8):
                    PE(ps[:L, 0:256], HT[:, kc, a:a + L], Wo[:, kc, :], kc == 0, kc == 7, r=[bWo, bHT], w=[bp])
                ACT(OGG[:L, j8, :], ps[:L, 0:256], AF.Sigmoid, r=[bp], w=[bOGG])
                ats, kps = [], []
                for hh in range(2):
                    H = 2 * hp2 + hh
                    qs = QK[:, hh, a:a + L]
                    ks = QK[:, 2 + hh, a:a + L]
                    ps, bp = prj.next()
                    PE(ps[:L, 0:L], ks, qs, True, True, r=[bQK], w=[bp])
                    at, bat = ATr.next()
                    STT(at[:L, :L], ps[:L, 0:L], WST[:L, c, H:H + 1], cmask[:L, :L], ALU.mult, ALU.mult, r=[bp, bWST, bcm], w=[bat])
                    pk, bpk = prj.next()
                    pkb = pk[:].bitcast(BF16)
                    TRP(pkb[:L, 0:128], ks, identb[:], r=[bQK, bidb], w=[bpk])
                    kp, bkp = KPr.next()
                    ACT(kp[:L, :], pkb[:L, 0:128], AF.Identity, r=[bpk, bWST], w=[bkp], scale=WST[:L, c, H:H + 1])
                    ats.append((at, bat))
                    kps.append((kp, bkp))
                ctxs[c] = dict(va=(va, bva), ats=ats, kps=kps)

            def stageY(c):
                smp = c == 64
                L = TS if smp else 64
                a = c * 64
                j8 = c % 8
                OGG, bOGG = OGG2[(c // 8) % 2]
                ctx = ctxs.pop(c)
                HG, bHG = HG2[(c // 8) % 2]
                va, bva = ctx["va"]
                st_, bst_ = (CSs, bCSs) if smp else (CS, bCS)
                cd_, bcd_ = (CDBs, bCDBs) if smp else (CDB, bCDB)
                pn, bpn = prj.next()
                pnv = pn[:, 0:260].rearrange("p (h e) -> p h e", h=2)
                for hh in range(2):
                    qs = QK[:, hh, a:a + L]
                    at, bat = ctx["ats"][hh]
                    PE(pnv[:L, hh, 0:129], at[:L, :L], va[:L, hh, 0:129], True, False, r=[bat, bva], w=[bpn])
                    PE(pnv[:L, hh, 0:129], qs, cd_[:, hh, 0:129], False, True, r=[bQK, bcd_], w=[bpn])
                for hh in range(2):
                    H = 2 * hp2 + hh
                    kp, bkp = ctx["kps"][hh]
                    pst, bpst = prj.next()
                    PE(pst[:, 0:129], kp[:L, :], va[:L, hh, 0:129], True, True, r=[bkp, bva], w=[bpst])
                    STT(st_[:, hh, 0:129], st_[:, hh, 0:129], DECB[:, H, c:c + 1], pst[:, 0:129], ALU.mult, ALU.add, r=[bst_, bDECB, bpst], w=[bst_])
                if c < 63:
                    TT("pool", cd_[:, :, 0:129], st_[:, :, 0:129],
                       DECB[:, 2 * hp2:2 * hp2 + 2, c + 1:c + 2].to_broadcast([128, 2, 129]), ALU.mult, r=[bst_, bDECB], w=[bcd_])
                cl, bcl = col.next()
                CP("dve", cl[:L, 0:2], pnv[:L, :, 128], r=[bpn], w=[bcl])
                TS_("dve", cl[:L, 2:4], cl[:L, 0:2], -1.0, None, ALU.mult, r=[bcl], w=[bcl])
                TT("dve", cl[:L, 2:4], cl[:L, 2:4], cl[:L, 0:2], ALU.max, r=[bcl], w=[bcl])
                TT("dve", cl[:L, 2:4], cl[:L, 2:4], WST[:L, c, 4 + 2 * hp2:6 + 2 * hp2], ALU.max, r=[bcl, bWST], w=[bcl])
                S.op("dve", lambda: nc.vector.reciprocal(out=cl[:L, 4:6], in_=cl[:L, 2:4]), reads=[bcl], writes=[bcl])
                TT("dve", HG[:L, j8, :].rearrange("p (h e) -> p h e", h=2), pnv[:L, :, 0:128],
                   cl[:L, 4:6].unsqueeze(2).to_broadcast([L, 2, 128]), ALU.mult, r=[bpn, bcl], w=[bHG])
                if j8 == 7 or smp:
                    ng = 1 if smp else 8
                    n2 = ng * 2
                    hgv = HG[:L, 0:ng, :].rearrange("p j (h e) -> p (j h) e", h=2)
                    sqv = SQ[:L, 0:ng, :].rearrange("p j (h e) -> p (j h) e", h=2)
                    TT("pool", SQ[:L, 0:ng, :], HG[:L, 0:ng, :], HG[:L, 0:ng, :], ALU.mult, r=[bHG], w=[bSQ])
                    S.op("dve", lambda: nc.vector.tensor_reduce(out=ssr[:L, 0:n2], in_=sqv, axis=AX.X, op=ALU.add), reads=[bSQ], writes=[bssr])
                    TT("pool", hgv, hgv, chg[:L, :].unsqueeze(1).to_broadcast([L, n2, 128]), ALU.mult, r=[bHG, bchg], w=[bHG])
                    ACT(ssr[:L, 16:16 + n2], ssr[:L, 0:n2], AF.Ln, r=[bssr, bepsc], w=[bssr], bias=epsc[:L, 0:1], scale=1.0 / 128)
                    ACT(ssr[:L, 32:32 + n2], ssr[:L, 16:16 + n2], AF.Exp, r=[bssr], w=[bssr], scale=-0.5)

                    def second(c=c, smp=smp, L=L, ng=ng, n2=n2, HG=HG, bHG=bHG, OGG=OGG, bOGG=bOGG, hgv=hgv):
                        ocb, bocb = ocbs["cur"]
                        TT("dve", hgv, hgv, ssr[:L, 32:32 + n2].unsqueeze(2).to_broadcast([L, n2, 128]), ALU.mult, r=[bHG, bssr], w=[bHG])
                        TT("dve", OKG[:L, 0:ng, :], HG[:L, 0:ng, :], OGG[:L, 0:ng, :], ALU.mult, r=[bHG, bOGG], w=[bOKG])
                        for jj in range(ng):
                            for hh in range(2):
                                TRP(psb7[:, hh * 512 + jj * 64:hh * 512 + jj * 64 + L], OKG[:L, jj, hh * 128:(hh + 1) * 128], identb[:L, :L],
                                    r=[bOKG, bidb], w=[PB[7]])
                        nn = ng * 64 if not smp else TS
                        CP("act", ocb[:, :, 0:nn], psb7.rearrange("p (h t) -> p h t", h=2)[:, :, 0:nn], r=[PB[7]], w=[bocb])
                        c0 = (c // 8) * 512
                        DMA(OCT[2 * hp2:2 * hp2 + 2, :, c0:c0 + nn].rearrange("h p t -> p h t"), ocb[:, :, 0:nn], r=[bocb], w=[bOCT])
                        ocbs["cur"] = OCB.next()
                    post.append(second)
                if c == 63 or smp:
                    oC, oN = ("s_c_C", "s_c_n") if smp else ("p_c_C", "p_c_n")
                    for hh in range(2):
                        pt3, bpt3 = prj.next()
                        TRP(pt3[:, 0:128], st_[:, hh, 0:128], identf[:], r=[bst_, bidf], w=[bpt3])
                        CP("dve", cf[:, hh, :], pt3[:, 0:128], r=[bpt3], w=[bcf])
                    DMA(O[oC][l, 2 * hp2:2 * hp2 + 2].rearrange("h e d -> e h d"), cf[:], r=[bcf], w=[OB[oC]])
                    on = O[oN]
                    DMA(bass.AP(tensor=on.tensor, offset=on[l, 2 * hp2].offset, ap=[[1, 128], [128, 2]]), st_[:, :, 128], r=[bst_],
                        w=[OB[oN]], allow_slow_non_contiguous=True)

            CLOOK = 3
            for c in range(min(CLOOK, 65)):
                stageX(c)
            for c in range(65):
                if post and (c % 8 == 7 or c == 64):
                    post.pop(0)()
                had = len(post)
                stageY(c)
                if c + CLOOK < 65:
                    stageX(c + CLOOK)
                if had:
                    post.pop(0)()
            while post:
                post.pop(0)()
            barrier()
        if debug:
            dbg["OCT"] = OCT

    def phase_MIX(l):
        AR.reset()
        wl = I["w_in"][l]
        Wg = [[AR.get("Wg%d_%d" % (fi, m), [128, 8, 128], BF16) for m in range(3)] for fi in range(4)]
        Wu = [[AR.get("Wu%d_%d" % (fi, m), [128, 4, 128], BF16) for m in range(3)] for fi in range(4)]
        ob3 = Rot([[AR.get("oblk%d_%d" % (i, m), [128, 4, 512], BF16) for m in range(3)] for i in range(2)])
        sg = Rot([AR.get("sg%d" % i, [128, 512], F32) for i in range(3)])
        mx = Rot([AR.get("mx%d" % i, [128, 512], F32) for i in range(2)])
        mo = Rot([AR.get("mo%d" % i, [128, 512], BF16) for i in range(2)])
        ups = [I["w_up_a"][l], I["w_up_b"][l], I["w_up_c"][l]]
        srcs = [(OAT, bOAT), (OBT, bOBT), (OCT, bOCT)]
        prj = Rot([(PS[i], PB[i]) for i in range(7)])
        for fh in range(2):
            for fi in range(4):
                f = fh * 4 + fi
                for m in range(3):
                    load_w(Wg[fi][m][0], Wg[fi][m][1], wl, [(5128 + m * 1024 + f * 128, 128)], 8)
                    load_w(Wu[fi][m][0], Wu[fi][m][1], ups[m], [(f * 128, 128)], 4)
            for (c0, n) in BLOCKS:
                blk = ob3.next()
                for m in range(3):
                    DMA(blk[m][0][:, :, 0:n], srcs[m][0][:, :, c0:c0 + n].rearrange("h p t -> p h t"), r=[srcs[m][1]], w=[blk[m][1]])
                for fi in range(4):
                    f = fh * 4 + fi
                    mx_, bmx = mx.next()
                    for m in range(3):
                        pg, bpg = prj.next()
                        for kc in range(8):
                            PE(pg[:, 0:n], Wg[fi][m][0][:, kc, :], HT[:, kc, c0:c0 + n], kc == 0, kc == 7, r=[Wg[fi][m][1], bHT], w=[bpg])
                        s_, bs_ = sg.next()
                        ACT(s_[:, 0:n], pg[:, 0:n], AF.Sigmoid, r=[bpg], w=[bs_])
                        pu, bpu = prj.next()
                        for kc in range(4):
                            PE(pu[:, 0:n], Wu[fi][m][0][:, kc, :], blk[m][0][:, kc, 0:n], kc == 0, kc == 3, r=[Wu[fi][m][1], blk[m][1]], w=[bpu])
                        if m == 0:
                            TT("dve", mx_[:, 0:n], pu[:, 0:n], s_[:, 0:n], ALU.mult, r=[bpu, bs_], w=[bmx])
                        else:
                            TT("dve", s_[:, 0:n], pu[:, 0:n], s_[:, 0:n], ALU.mult, r=[bpu, bs_], w=[bs_])
                            if m == 1:
                                TT("pool", mx_[:, 0:n], mx_[:, 0:n], s_[:, 0:n], ALU.add, r=[bmx, bs_], w=[bmx])
                            else:
                                mo_, bmo = mo.next()
                                TT("pool", mo_[:, 0:n], mx_[:, 0:n], s_[:, 0:n], ALU.add, r=[bmx, bs_], w=[bmo])
                                put_fm(MIXT, f, c0, n, mo_[:, 0:n], bmo, bMIXT)
        if debug:
            dbg["MIXT"] = MIXT

    def wr_pieces(w_dram, KC, Wr, bWr):
        out = []
        for kc0 in range(0, KC, 4):
            kk = min(4, KC - kc0)
            for half in range(2):
                def piece(kc0=kc0, kk=kk, half=half):
                    st, bst = stg.next()
                    sv = st[:, 0:kk * 512].rearrange("p (k n) -> p k n", k=kk)
                    DMA(sv, w_dram[kc0 * 128:(kc0 + kk) * 128, half * 512:(half + 1) * 512].rearrange("(k p) n -> p k n", p=128), w=[bst])
                    CP("pool", Wr[:, kc0:kc0 + kk, half * 512:(half + 1) * 512], sv, r=[bst], w=[bWr])
                out.append(piece)
        return out

    def phase_resid(l, w_dram, KC, srcT, bsrcT, xin_sel, xout_sel, gamma, final=False, tapname=None, preloaded=False):
        AR.reset()
        Wr, bWr = AR.get("Wr", [128, KC, D], BF16)
        for kc0 in (range(0, KC, 4) if not preloaded else []):
            kk = min(4, KC - kc0)
            for half in range(2):
                st, bst = stg.next()
                sv = st[:, 0:kk * 512].rearrange("p (k n) -> p k n", k=kk)
                DMA(sv, w_dram[kc0 * 128:(kc0 + kk) * 128, half * 512:(half + 1) * 512].rearrange("(k p) n -> p k n", p=128), w=[bst])
                CP("pool", Wr[:, kc0:kc0 + kk, half * 512:(half + 1) * 512], sv, r=[bst], w=[bWr])
        load_gamma(gamma)
        at = Rot([AR.get("at%d" % i, [128, KC, 128], BF16) for i in range(2)])
        xo = Rot([AR.get("xo%d" % i, [128, D], F32) for i in range(4)])
        prj = Rot([(PS[i], PB[i]) for i in range(6)])
        pend_fin = [None]
        for (c0, n) in TILES:
            a_, ba_ = at.next()
            ti_ = c0 // 128
            DMA(a_[:, :, 0:n], srcT[ti_, :, :, 0:n], r=[bsrcT], w=[ba_])
            xt, bxt = xin.next()
            src, bsrc = xrows(xin_sel, c0, n)
            DMA(xt[:n], src, r=[bsrc] if bsrc else [], w=[bxt], q="act")
            xo_, bxo = xo.next()
            for half in range(2):
                ps, bp = prj.next()
                for kc in range(KC):
                    PE(ps[:n, :], a_[:, kc, 0:n], Wr[:, kc, half * 512:(half + 1) * 512], kc == 0, kc == KC - 1, r=[ba_, bWr], w=[bp])
                TT("dve", xo_[:n, half * 512:(half + 1) * 512], ps[:n, :], xt[:n, half * 512:(half + 1) * 512], ALU.add,
                   r=[bp, bxt], w=[bxo])
            if len(pend_fin) > 2:
                pend_fin.pop(1)()
            if not final:
                dst, bdst = xrows(xout_sel, c0, n)
                DMA(dst, xo_[:n], r=[bxo], w=[bdst], q="act")
                pend_fin.append(norm_tile(xo_, bxo, n, c0, defer=True))
            else:
                if c0 < T:
                    norm_tile(xo_, bxo, n, c0, to_out=(O["y_p"][c0:c0 + n, :], OB["y_p"]))
                else:
                    norm_tile(xo_, bxo, n, c0, to_out=(O["y_s"][0:n, :], OB["y_s"]))
        for fn_ in pend_fin[1:]:
            fn_()

    def phase_CROSS(l):
        AR.reset()
        memT, bmemT = AR.get("memT", [128, 8, NMEM], BF16)
        mb, bmb = AR.get("mb", [128, 2, D], BF16)
        mkT = [AR.get("mkT%d" % g, [128, 8, NMEM], BF16) for g in range(2)]
        mvt = [AR.get("mvt%d" % g, [128, 2, D], BF16) for g in range(2)]
        Wm, bWm = AR.get("Wm", [128, 8, 512], BF16)
        qh, bqh = AR.get("qh", [128, 2, NT], BF16)
        oh, boh = AR.get("oh", [128, 2, NT], BF16)
        ptr = Rot([AR.get("ptC%d" % i, [128, 512], BF16) for i in range(4)])
        rcp = Rot([AR.get("rcp%d" % i, [128, 512], F32) for i in range(2)])
        mf = Rot([AR.get("mf%d" % i, [128, 512], F32) for i in range(2)])
        prj = Rot([(PS[i], PB[i]) for i in range(7)])
        for mt in range(2):
            xt, bxt = xin.next()
            DMA(xt[:], I["memp"][mt * 128:(mt + 1) * 128, :], w=[bxt])
            CP("dve", mb[:, mt, :], xt[:], r=[bxt], w=[bmb])
            for kc in range(8):
                TRP(psb7[:, kc * 128:(kc + 1) * 128], mb[:, mt, kc * 128:(kc + 1) * 128], identb[:], r=[bmb, bidb], w=[PB[7]])
            CP("act", memT[:, :, mt * 128:(mt + 1) * 128], psb7.rearrange("p (k t) -> p k t", k=8), r=[PB[7]], w=[bmemT])
        for which, wd, onm in ((0, I["w_mk"][l], "p_mem_k"), (1, I["w_mv"][l], "p_mem_v")):
            for half in range(2):
                load_w(Wm, bWm, wd, [(half * 512, 512)], 8)
                for mt in range(2):
                    ps, bp = prj.next()
                    for kc in range(8):
                        PE(ps[:, :], memT[:, kc, mt * 128:(mt + 1) * 128], Wm[:, kc, :], kc == 0, kc == 7, r=[bmemT, bWm], w=[bp])
                    m_, bm_ = mf.next()
                    CP("act", m_[:], ps[:, :], r=[bp], w=[bm_])
                    DMA(O[onm][l, mt * 128:(mt + 1) * 128, half * 512:(half + 1) * 512], m_[:], r=[bm_], w=[OB[onm]])
                    if which == 1:
                        CP("pool", mvt[0][0][:, mt, half * 512:(half + 1) * 512], m_[:], r=[bm_], w=[mvt[0][1]])
                if which == 0:
                    for j in range(4):
                        ps, bp = prj.next()
                        for kc in range(8):
                            PE(ps[:, 0:NMEM], Wm[:, kc, j * 128:(j + 1) * 128], memT[:, kc, :], kc == 0, kc == 7, r=[bmemT, bWm], w=[bp])
                        CP("dve", mkT[0][0][:, half * 4 + j, :], ps[:, 0:NMEM], r=[bp], w=[mkT[0][1]])
        for mt in range(2):
            xt, bxt = xin.next()
            DMA(xt[:], I["cmk"][l, mt * 128:(mt + 1) * 128, :], w=[bxt])
            CP("dve", mb[:, mt, :], xt[:], r=[bxt], w=[bmb])
            for kc in range(8):
                TRP(psb7[:, kc * 128:(kc + 1) * 128], mb[:, mt, kc * 128:(kc + 1) * 128], identb[:], r=[bmb, bidb], w=[PB[7]])
            CP("act", mkT[1][0][:, :, mt * 128:(mt + 1) * 128], psb7.rearrange("p (k t) -> p k t", k=8), r=[PB[7]], w=[mkT[1][1]])
            xt, bxt = xin.next()
            DMA(xt[:], I["cmv"][l, mt * 128:(mt + 1) * 128, :], w=[bxt])
            CP("dve", mvt[1][0][:, mt, :], xt[:], r=[bxt], w=[mvt[1][1]])
        for h in range(4):
            for half2 in range(1):
                load_w(Wm[:, :, 0:256], bWm, I["w_mq"][l], [(h * 256, 256)], 8)
            for (c0, n) in BLOCKS:
                for dc in range(2):
                    ps, bp = prj.next()
                    for kc in range(8):
                        PE(ps[:, 0:n], Wm[:, kc, dc * 128:(dc + 1) * 128], HT[:, kc, c0:c0 + n], kc == 0, kc == 7, r=[bWm, bHT], w=[bp])
                    CP("act", qh[:, dc, c0:c0 + n], ps[:, 0:n], r=[bp], w=[bqh])
            def c_stage1(c0, n):
                g = 0 if c0 < T else 1
                pts = []
                for mt in range(2):
                    ps, bp = prj.next()
                    for dc in range(2):
                        PE(ps[:, 0:n], mkT[g][0][:, h * 2 + dc, mt * 128:(mt + 1) * 128], qh[:, dc, c0:c0 + n], dc == 0, dc == 1,
                           r=[mkT[g][1], bqh], w=[bp])
                    p_, bp_ = ptr.next()
                    ACT(p_[:, 0:n], ps[:, 0:n], AF.Exp, r=[bp], w=[bp_], scale=1.0 / 16.0)
                    pts.append((p_, bp_))
                return pts

            def c_stage2(c0, n, pts):
                g = 0 if c0 < T else 1
                psu, bpsu = prj.next()
                for mt in range(2):
                    PE(psu[:, 0:n], onesb[:, :], pts[mt][0][:, 0:n], mt == 0, mt == 1, r=[bones, pts[mt][1]], w=[bpsu])
                rc, brc = rcp.next()
                S.op("dve", lambda: nc.vector.reciprocal(out=rc[:, 0:n], in_=psu[:, 0:n]), reads=[bpsu], writes=[brc])
                for ec in range(2):
                    po, bpo = prj.next()
                    for mt in range(2):
                        PE(po[:, 0:n], mvt[g][0][:, mt, h * 256 + ec * 128:h * 256 + (ec + 1) * 128], pts[mt][0][:, 0:n], mt == 0, mt == 1,
                           r=[mvt[g][1], pts[mt][1]], w=[bpo])
                    TT("dve", oh[:, ec, c0:c0 + n], po[:, 0:n], rc[:, 0:n], ALU.mult, r=[bpo, brc], w=[boh])

            nxt = c_stage1(*BLOCKS[0])
            for bi_, (c0, n) in enumerate(BLOCKS):
                cur_pts = nxt
                if bi_ + 1 < len(BLOCKS):
                    nxt = c_stage1(*BLOCKS[bi_ + 1])
                c_stage2(c0, n, cur_pts)
            for ec in range(2):
                put_fm(CROT, 2 * h + ec, 0, T, oh[:, ec, 0:T], boh, bCROT, q="sp")
                put_fm(CROT, 2 * h + ec, T, TS, oh[:, ec, T:NT], boh, bCROT, q="sp")
        if debug:
            dbg["CROT"] = CROT

    def phase_FFNU(l):
        AR.reset()
        Wr, bWr = AR.get("Wr", [128, 22, D], BF16)
        pre = wr_pieces(I["w_ff_d"][l], 22, Wr, bWr)
        Wg, bWg = AR.get("Wfg", [128, 8, 128], BF16)
        Wu, bWu = AR.get("Wfu", [128, 8, 128], BF16)
        sg = Rot([AR.get("fsg%d" % i, [128, 512], F32) for i in range(3)])
        ao = Rot([AR.get("fao%d" % i, [128, 512], BF16) for i in range(3)])
        prj = Rot([(PS[i], PB[i]) for i in range(7)])
        for f in range(22):
            load_w(Wg, bWg, I["w_ff_g"][l], [(f * 128, 128)], 8)
            load_w(Wu, bWu, I["w_ff_u"][l], [(f * 128, 128)], 8)
            if f >= 2 and f % 2 == 0 and pre:
                pre.pop(0)()
                if f >= 12 and pre:
                    pre.pop(0)()
            for (c0, n) in BLOCKS:
                pg, bpg = prj.next()
                for kc in range(8):
                    PE(pg[:, 0:n], Wg[:, kc, :], HT[:, kc, c0:c0 + n], kc == 0, kc == 7, r=[bWg, bHT], w=[bpg])
                s_, bs_ = sg.next()
                ACT(s_[:, 0:n], pg[:, 0:n], AF.Silu, r=[bpg], w=[bs_])
                pu, bpu = prj.next()
                for kc in range(8):
                    PE(pu[:, 0:n], Wu[:, kc, :], HT[:, kc, c0:c0 + n], kc == 0, kc == 7, r=[bWu, bHT], w=[bpu])
                a_, ba_ = ao.next()
                TT("dve", a_[:, 0:n], pu[:, 0:n], s_[:, 0:n], ALU.mult, r=[bpu, bs_], w=[ba_])
                put_fm(ACTT, f, c0, n, a_[:, 0:n], ba_, bACTT)
        while pre:
            pre.pop(0)()

    phases = []
    cur = "in"
    other = {"in": "A", "A": "B", "B": "A"}
    for l in range(n_layers):
        if l == 0:
            phases.append(("norm1", lambda l=l: phase_norm1(l, "in")))
        phases.append(("A%d" % l, lambda l=l: phase_A(l)))
        phases.append(("B%d" % l, lambda l=l: phase_B(l)))
        phases.append(("C%d" % l, lambda l=l: phase_C(l)))
        phases.append(("MIX%d" % l, lambda l=l: phase_MIX(l)))
        xi, xo = cur, other[cur]
        phases.append(("WO%d" % l, lambda l=l, xi=xi, xo=xo: phase_resid(l, I["w_o"][l], 8, MIXT, bMIXT, xi, xo, I["g_cross"][l])))
        cur = xo
        phases.append(("CROSS%d" % l, lambda l=l: phase_CROSS(l)))
        xi, xo = cur, other[cur]
        phases.append(("WMO%d" % l, lambda l=l, xi=xi, xo=xo: phase_resid(l, I["w_mo"][l], 8, CROT, bCROT, xi, xo, I["g_ffn"][l])))
        cur = xo
        phases.append(("FFNU%d" % l, lambda l=l: phase_FFNU(l)))
        xi, xo = cur, other[cur]
        if l < n_layers - 1:
            phases.append(("FFND%d" % l, lambda l=l, xi=xi, xo=xo: phase_resid(l, I["w_ff_d"][l], 22, ACTT, bACTT, xi, xo, I["g_mix"][l + 1], preloaded=True)))
        else:
            phases.append(("FFND%d" % l, lambda l=l, xi=xi, xo=xo: phase_resid(l, I["w_ff_d"][l], 22, ACTT, bACTT, xi, xo, I["g_final"], final=True, preloaded=True)))
        cur = xo
    for name, fn in phases:
        fn()
        barrier()
        if stop_after == name:
            break
    barrier()
    return nc, S, dbg


_PROG = {}


def _prep_inputs(inputs):
    f = lambda a: np.ascontiguousarray(a, dtype=np.float32)
    consts = make_consts()
    maps = []
    shared = {}
    for k in ("g_mix", "w_in", "a_lq1", "a_lk1", "a_lq2", "a_lk2", "a_head_g", "b_rel", "c_conv_w", "c_conv_b", "c_b_i", "c_b_f",
              "c_head_g", "w_up_a", "w_up_b", "w_up_c", "w_o", "g_cross", "w_mq", "w_mk", "w_mv", "w_mo", "g_ffn", "w_ff_g",
              "w_ff_u", "w_ff_d", "g_final"):
        shared[k] = f(inputs[k])
    for k, v in consts.items():
        shared["c_" + k] = f(v)
    for b in range(8):
        m = dict(shared)
        m["x_p"] = f(inputs["x_prompt"][b])
        m["x_s"] = f(inputs["x_sample"][b])
        m["cak"] = f(inputs["cache_a_k"][:, b].reshape(2, PAST, 512))
        m["cav"] = f(inputs["cache_a_v"][:, b].reshape(2, PAST, 512))
        m["cbk"] = f(inputs["cache_b_k"][:, b].reshape(2, NBAND, 512))
        m["cbv"] = f(inputs["cache_b_v"][:, b].reshape(2, NBAND, 512))
        m["sC"] = f(inputs["state_c_C"][:, b])
        m["sn"] = f(inputs["state_c_n"][:, b])
        m["sm"] = f(inputs["state_c_m"][:, b])
        m["sconv"] = f(inputs["state_c_conv"][:, b])
        m["cmk"] = f(inputs["cache_mem_k"][:, b].reshape(2, NMEM, D))
        m["cmv"] = f(inputs["cache_mem_v"][:, b].reshape(2, NMEM, D))
        m["memp"] = f(inputs["mem_prompt"][b])
        maps.append(m)
    return maps


def kernel(**inputs):
    if "nc" not in _PROG:
        _PROG["nc"] = build_program()[0]
    nc = _PROG["nc"]
    maps = _prep_inputs(inputs)
    res = run_bass_kernel_spmd(nc, maps, core_ids=list(range(8)))
    R = res.results
    st = lambda k: np.stack([np.asarray(R[b][k]) for b in range(8)], axis=0)
    st1 = lambda k: np.stack([np.asarray(R[b][k]) for b in range(8)], axis=1)
    out = {}
    out["y_p"] = st("y_p")
    out["y_s"] = st("y_s")
    out["p_a_k"] = st1("p_a_k").reshape(2, 8, T, 4, 128)
    out["p_a_v"] = st1("p_a_v").reshape(2, 8, T, 4, 128)
    out["p_b_k"] = st1("p_b_k").reshape(2, 8, 512, 8, 64)
    out["p_b_v"] = st1("p_b_v").reshape(2, 8, 512, 8, 64)
    out["p_c_C"] = st1("p_c_C")
    out["p_c_n"] = st1("p_c_n")
    out["p_c_m"] = st1("p_c_m")
    out["p_c_conv"] = st1("p_c_conv")
    out["p_mem_k"] = st1("p_mem_k").reshape(2, 8, NMEM, 4, 256)
    out["p_mem_v"] = st1("p_mem_v").reshape(2, 8, NMEM, 4, 256)
    out["s_a_k"] = st1("s_a_k").reshape(2, 8, TS, 4, 128)
    out["s_a_v"] = st1("s_a_v").reshape(2, 8, TS, 4, 128)
    out["s_b_k"] = st1("s_b_k").reshape(2, 8, TS, 8, 64)
    out["s_b_v"] = st1("s_b_v").reshape(2, 8, TS, 8, 64)
    out["s_c_C"] = st1("s_c_C")
    out["s_c_n"] = st1("s_c_n")
    out["s_c_m"] = st1("s_c_m")
    out["s_c_conv"] = st1("s_c_conv")
    return tuple(np.ascontiguousarray(out[k], dtype=np.float32) for k in OUT_ORDER)
```

```python
import math
import numpy as np
import concourse.bass as bass
import concourse.mybir as mybir
from concourse.bass_utils import run_bass_kernel_spmd

F32 = mybir.dt.float32
BF16 = mybir.dt.bfloat16
AF = mybir.ActivationFunctionType
ALU = mybir.AluOpType
AX = mybir.AxisListType

T = 4096
TS = 16
NT = T + TS
D = 1024
NIN = 8200
DFF = 2816
PAST = 1024
NBAND = 512
NMEM = 256
EPS = 1e-6
NEG = -30000.0
BLOCKS = [(i * 512, 512) for i in range(8)] + [(T, TS)]
TILES = [(i * 128, 128) for i in range(32)] + [(T, TS)]
SLOPES = [2.0 ** (-8.0 * (h + 1) / 4) for h in range(4)]


class Buf:
    __slots__ = ("name", "w", "r", "loose")

    def __init__(self, name, loose=False):
        self.name = name
        self.w = None
        self.r = []
        self.loose = loose


class Sched:
    ENG = ("pe", "act", "dve", "pool", "sp")

    def __init__(self, nc, n_dma_sems=16):
        self.nc = nc
        self.e = {"pe": nc.tensor, "act": nc.scalar, "dve": nc.vector, "pool": nc.gpsimd, "sp": nc.sync}
        self.sem = {k: nc.alloc_semaphore(name="sem_" + k) for k in self.ENG}
        self.cnt = {k: 0 for k in self.ENG}
        self.dsem = [nc.alloc_semaphore(name="dsem%d" % i) for i in range(n_dma_sems)]
        self.duse = [0] * n_dma_sems
        self.dnext = 0
        self.seen = {k: {} for k in self.ENG}
        self.ninstr = 0

    def _wait(self, eng, ev):
        if ev is None:
            return
        kind, key, val = ev
        if kind == "c":
            if key == "pe" and eng == "pe":
                return
            k = ("c", key)
            sem = self.sem[key]
        else:
            k = ("d", key)
            sem = self.dsem[key]
        if self.seen[eng].get(k, 0) >= val:
            return
        self.seen[eng][k] = val
        self.e[eng].wait_ge(sem, val)

    def _deps(self, eng, reads, writes):
        for b in reads:
            if not b.loose:
                self._wait(eng, b.w)
        for b in writes:
            if b.loose:
                continue
            self._wait(eng, b.w)
            for ev in b.r:
                self._wait(eng, ev)

    def _commit(self, ev, reads, writes):
        for b in reads:
            if not b.loose:
                b.r.append(ev)
        for b in writes:
            if not b.loose:
                b.w = ev
                b.r = []

    limit = None

    def op(self, eng, fn, reads=(), writes=()):
        if self.limit is not None and self.ninstr >= self.limit:
            self.ninstr += 1
            return None
        self._deps(eng, reads, writes)
        ins = fn()
        self.cnt[eng] += 1
        ins.then_inc(self.sem[eng], 1)
        ev = ("c", eng, self.cnt[eng])
        self._commit(ev, reads, writes)
        self.ninstr += 1
        return ev

    def dma(self, q, out, in_, reads=(), writes=(), **kw):
        if self.limit is not None and self.ninstr >= self.limit:
            self.ninstr += 1
            return None
        slot = self.dnext
        self.dnext = (self.dnext + 1) % len(self.dsem)
        if self.duse[slot] > 0:
            self._wait(q, ("d", slot, 16 * self.duse[slot]))
        self._deps(q, reads, writes)
        ins = self.e[q].dma_start(out=out, in_=in_, **kw)
        self.duse[slot] += 1
        ins.then_inc(self.dsem[slot], 16)
        ev = ("d", slot, 16 * self.duse[slot])
        self._commit(ev, reads, writes)
        self.ninstr += 1
        return ev

    def wait_all(self, bufs, eng="sp"):
        for b in bufs:
            self._wait(eng, b.w)
            for ev in b.r:
                self._wait(eng, ev)


class Rot:
    def __init__(self, items):
        self.items = items
        self.i = 0

    def next(self):
        it = self.items[self.i]
        self.i = (self.i + 1) % len(self.items)
        return it


def make_consts():
    c = {}
    k = np.arange(128)[:, None].astype(np.float64)
    q = np.arange(512)[None, :].astype(np.float64)
    abase = np.zeros((128, 4, 512), np.float32)
    adg = np.zeros((128, 4, 512), np.float32)
    acol = np.zeros((128, 4, 36), np.float32)
    for h in range(4):
        s = SLOPES[h]
        abase[:, h, :] = -s * (q - k)
        adg[:, h, :] = -s * (q - k)
        qq = np.arange(128)[None, :]
        kk = np.arange(128)[:, None]
        dg = np.where((kk // 64) <= (qq // 64), -s * np.abs(qq - kk), NEG)
        adg[:, h, 0:128] = dg
        for Dd in range(-3, 33):
            acol[:, h, Dd + 3] = s * np.arange(128) - s * 128.0 * Dd
    rq = np.zeros((2, 4, 512), np.float32)
    qi = np.arange(512)
    for h in range(4):
        rq[0, h, :] = -8.0 * SLOPES[h] * 256.0 * (qi // 256)
        rq[1, h, :] = -8.0 * SLOPES[h] * (qi % 256)
    c["rq"] = rq.reshape(2, 2048)
    c["abase"] = abase.reshape(128, 2048)
    c["adg"] = adg.reshape(128, 2048)
    c["acol"] = acol.reshape(128, 144)
    kk = np.arange(128)[:, None]
    qq = np.arange(128)[None, :]
    bm = np.zeros((128, 4, 128), np.float32)
    bm[:, 0, :] = np.where((kk < 64) & (qq >= 64), NEG, 0.0)
    bm[:, 1, :] = np.where((kk >= 64) & (qq < 64), NEG, 0.0)
    bm[:, 2, :] = (qq <= kk).astype(np.float32)
    bm[:, 3, :] = 1.0 - bm[:, 2, :]
    c["bmask"] = bm.reshape(128, 512)
    ss = np.arange(64)[:, None]
    tt = np.arange(64)[None, :]
    c["cmask"] = (ss <= tt).astype(np.float32)
    sel = np.zeros((4, 4, 128), np.float32)
    for h in range(4):
        sel[h, h, :] = 1.0
    c["sel"] = sel.reshape(4, 512)
    c["ident"] = np.eye(128, dtype=np.float32)
    return c


CONST_SHAPES = {"rq": [2, 2048], "abase": [128, 2048], "adg": [128, 2048], "acol": [128, 144], "bmask": [128, 512],
                "cmask": [64, 64], "sel": [4, 512], "ident": [128, 128]}

IN_SHAPES = {
    "x_p": [T, D], "x_s": [TS, D], "cak": [2, PAST, 512], "cav": [2, PAST, 512], "cbk": [2, NBAND, 512],
    "cbv": [2, NBAND, 512], "sC": [2, 4, 128, 128], "sn": [2, 4, 128], "sm": [2, 4], "sconv": [2, 3, D],
    "cmk": [2, NMEM, D], "cmv": [2, NMEM, D], "memp": [NMEM, D],
    "g_mix": [2, D], "w_in": [2, D, NIN], "a_lq1": [2, 64], "a_lk1": [2, 64], "a_lq2": [2, 64], "a_lk2": [2, 64],
    "a_head_g": [2, 128], "b_rel": [2, 8, 257], "c_conv_w": [2, 4, D], "c_conv_b": [2, D], "c_b_i": [2, 4],
    "c_b_f": [2, 4], "c_head_g": [2, 128], "w_up_a": [2, 512, D], "w_up_b": [2, 512, D], "w_up_c": [2, 512, D],
    "w_o": [2, D, D], "g_cross": [2, D], "w_mq": [2, D, D], "w_mk": [2, D, D], "w_mv": [2, D, D], "w_mo": [2, D, D],
    "g_ffn": [2, D], "w_ff_g": [2, D, DFF], "w_ff_u": [2, D, DFF], "w_ff_d": [2, DFF, D], "g_final": [D],
}
OUT_SHAPES = {
    "y_p": [T, D], "y_s": [TS, D], "p_a_k": [2, T, 512], "p_a_v": [2, T, 512], "p_b_k": [2, 512, 512],
    "p_b_v": [2, 512, 512], "p_c_C": [2, 4, 128, 128], "p_c_n": [2, 4, 128], "p_c_m": [2, 4], "p_c_conv": [2, 3, D],
    "p_mem_k": [2, NMEM, D], "p_mem_v": [2, NMEM, D], "s_a_k": [2, TS, 512], "s_a_v": [2, TS, 512],
    "s_b_k": [2, TS, 512], "s_b_v": [2, TS, 512], "s_c_C": [2, 4, 128, 128], "s_c_n": [2, 4, 128], "s_c_m": [2, 4],
    "s_c_conv": [2, 3, D],
}
OUT_ORDER = ["y_p", "y_s", "p_a_k", "p_a_v", "p_b_k", "p_b_v", "p_c_C", "p_c_n", "p_c_m", "p_c_conv", "p_mem_k",
             "p_mem_v", "s_a_k", "s_a_v", "s_b_k", "s_b_v", "s_c_C", "s_c_n", "s_c_m", "s_c_conv"]


def build_program(n_layers=2, stop_after=None, debug=False):
    nc = bass.Bass("TRN2", target_bir_lowering=False)
    S = Sched(nc)
    import os as _os
    if _os.environ.get("KLIMIT"):
        S.limit = int(_os.environ["KLIMIT"])
    I = {k: nc.dram_tensor(k, v, F32, kind="ExternalInput").ap() for k, v in IN_SHAPES.items()}
    CI = {k: nc.dram_tensor("c_" + k, v, F32, kind="ExternalInput").ap() for k, v in CONST_SHAPES.items()}
    O = {k: nc.dram_tensor(k, v, F32, kind="ExternalOutput").ap() for k, v in OUT_SHAPES.items()}
    OB = {k: Buf("o_" + k, loose=True) for k in OUT_SHAPES}
    skind = "ExternalOutput" if debug else "Internal"

    def scratch(name, shape, dt, loose=True):
        return nc.dram_tensor(name, shape, dt, kind=skind).ap(), Buf(name, loose=loose)

    XA, bXA = scratch("XA", [NT, D], F32)
    XB, bXB = scratch("XB", [NT, D], F32)
    OAT, bOAT = scratch("OAT", [4, 128, NT], BF16)
    OBT, bOBT = scratch("OBT", [4, 128, NT], BF16)
    OCT, bOCT = scratch("OCT", [4, 128, NT], BF16)
    MIXT, bMIXT = scratch("MIXT", [33, 128, 8, 128], BF16)
    CROT, bCROT = scratch("CROT", [33, 128, 8, 128], BF16)
    ACTT, bACTT = scratch("ACTT", [33, 128, 22, 128], BF16)
    ZB, bZB = scratch("ZB", [8, 128, 384], F32, loose=False)

    def sb(name, shape, dt=F32):
        return nc.alloc_sbuf_tensor(name, shape, dt), Buf(name)

    HT, bHT = sb("HT", [128, 8, NT], BF16)
    identf, bidf = sb("identf", [128, 128], F32)
    identb, bidb = sb("identb", [128, 128], BF16)
    onesb, bones = sb("onesb", [128, 128], BF16)
    GT, bGT = sb("GT", [128, D], F32)
    stg = Rot([sb("stg%d" % i, [128, 2048], F32) for i in range(2)])
    xin = Rot([sb("xin%d" % i, [128, D], F32) for i in range(2)])
    ybf = Rot([sb("ybf%d" % i, [128, D], BF16) for i in range(4)])
    junk, bjunk = sb("junk", [128, D], BF16)
    col = Rot([sb("col%d" % i, [128, 8], F32) for i in range(24)])
    epsc, bepsc = sb("epsc", [128, 1], F32)
    PS = [nc.alloc_psum_tensor("ps%d" % i, [128, 512], F32) for i in range(8)]
    PB = [Buf("ps%d" % i) for i in range(8)]
    ARENA, bAR = sb("arena", [128, 24576], F32)

    class Arena:
        def __init__(self):
            self.off = 0

        def reset(self):
            self.off = 0

        def get(self, name, shape, dt=F32):
            n = int(np.prod(shape[1:]))
            words = n if dt == F32 else (n + 1) // 2
            words = (words + 7) // 8 * 8
            assert self.off + words <= 24576, (name, self.off, words)
            v = ARENA[:, self.off:self.off + words]
            self.off += words
            if dt != F32:
                v = v.bitcast(dt)
            v = v[:, 0:n]
            if len(shape) == 3:
                v = v.rearrange("p (a b) -> p a b", a=shape[1])
            elif len(shape) == 4:
                v = v.rearrange("p (a b c) -> p a b c", a=shape[1], b=shape[2])
            return v[0:shape[0]], Buf(name)

    AR = Arena()
    engs = ("pe", "act", "dve", "pool", "sp")

    def barrier():
        evs = [("c", e, S.cnt[e]) for e in engs if S.cnt[e] > 0]
        evs += [("d", s, 16 * S.duse[s]) for s in range(len(S.dsem)) if S.duse[s] > 0]
        for e in engs:
            for ev in evs:
                S._wait(e, ev)

    def PE(out, lhsT, rhs, start=True, stop=True, r=(), w=()):
        return S.op("pe", lambda: nc.tensor.matmul(out, lhsT=lhsT, rhs=rhs, start=start, stop=stop), reads=r, writes=w)

    def TRP(out, in_, ident, r=(), w=()):
        return S.op("pe", lambda: nc.tensor.transpose(out, in_, ident), reads=r, writes=w)

    def ACT(out, in_, func, r=(), w=(), **kw):
        return S.op("act", lambda: nc.scalar.activation(out=out, in_=in_, func=func, **kw), reads=r, writes=w)

    def TS_(eng, out, in0, s1, s2, op0, op1=None, r=(), w=()):
        e = nc.vector if eng == "dve" else nc.gpsimd
        if op1 is None:
            return S.op(eng, lambda: e.tensor_scalar(out=out, in0=in0, scalar1=s1, scalar2=None, op0=op0), reads=r, writes=w)
        return S.op(eng, lambda: e.tensor_scalar(out=out, in0=in0, scalar1=s1, scalar2=s2, op0=op0, op1=op1), reads=r, writes=w)

    def STT(out, in0, scalar, in1, op0, op1, r=(), w=()):
        return S.op("dve", lambda: nc.vector.scalar_tensor_tensor(out=out, in0=in0, scalar=scalar, in1=in1, op0=op0, op1=op1),
                    reads=r, writes=w)

    def TT(eng, out, in0, in1, op, r=(), w=()):
        e = nc.vector if eng == "dve" else nc.gpsimd
        return S.op(eng, lambda: e.tensor_tensor(out=out, in0=in0, in1=in1, op=op), reads=r, writes=w)

    def CP(eng, out, in_, r=(), w=()):
        if eng == "act":
            return S.op("act", lambda: nc.scalar.copy(out=out, in_=in_), reads=r, writes=w)
        e = nc.vector if eng == "dve" else nc.gpsimd
        return S.op(eng, lambda: e.tensor_copy(out=out, in_=in_), reads=r, writes=w)

    def MSET(eng, ap, val, w=()):
        e = nc.vector if eng == "dve" else nc.gpsimd
        return S.op(eng, lambda: e.memset(ap, val), writes=w)

    dmaq = Rot(["sp", "act"])

    def RSQ(cl, bcl, n, i_src, i_tmp, i_dst, invn):
        ACT(cl[:n, i_tmp:i_tmp + 1], cl[:n, i_src:i_src + 1], AF.Ln, r=[bcl, bepsc], w=[bcl], bias=epsc[:n, 0:1], scale=invn)
        ACT(cl[:n, i_dst:i_dst + 1], cl[:n, i_tmp:i_tmp + 1], AF.Exp, r=[bcl], w=[bcl], scale=-0.5)

    def DMA(out, in_, r=(), w=(), q=None, **kw):
        return S.dma(q or "sp", out, in_, reads=r, writes=w, **kw)


    def put_fm(dst, kc_idx, c0, n, src2d, bsrc, bdst, q="act"):
        if c0 < T:
            t0 = c0 // 128
            nt_ = n // 128
            DMA(dst[t0:t0 + nt_, :, kc_idx, :].rearrange("t p x -> p t x"), src2d.rearrange("p (t x) -> p t x", x=128), r=[bsrc], w=[bdst], q=q)
        else:
            DMA(dst[32, :, kc_idx, 0:n], src2d, r=[bsrc], w=[bdst], q=q)

    def bcast_rows(dram_ap_1d_tensor, offset, n, parts=128):
        return bass.AP(tensor=dram_ap_1d_tensor, offset=offset, ap=[[0, parts], [1, n]])

    def load_w(dst, bdst, wl, pieces, KC, cast_eng="pool"):
        mx = 2048 // KC
        sub = []
        for (c0, n) in pieces:
            for a in range(0, n, mx):
                sub.append((c0 + a, min(mx, n - a)))
        groups, cur, tot = [], [], 0
        for (c0, n) in sub:
            if tot + n > mx:
                groups.append(cur)
                cur, tot = [], 0
            cur.append((c0, n))
            tot += n
        groups.append(cur)
        doff = 0
        for grp in groups:
            ntot = sum(n for _, n in grp)
            st, bst = stg.next()
            sv = st[:, 0:KC * ntot].rearrange("p (k n) -> p k n", k=KC)
            off = 0
            for (c0, n) in grp:
                DMA(sv[:, :, off:off + n], wl[:, c0:c0 + n].rearrange("(k p) n -> p k n", p=128), w=[bst])
                off += n
            CP(cast_eng, dst[:, :, doff:doff + ntot], sv, r=[bst], w=[bdst])
            doff += ntot

    def load_gamma(vec_ap):
        DMA(GT[:], bcast_rows(vec_ap.tensor, vec_ap.offset, D), w=[bGT])

    def xrows(xsel, c0, n):
        if xsel == "in":
            return (I["x_p"][c0:c0 + n, :], None) if c0 < T else (I["x_s"][0:n, :], None)
        if xsel == "A":
            return XA[c0:c0 + n, :], bXA
        return XB[c0:c0 + n, :], bXB

    psb7 = PS[7][:].bitcast(BF16)

    def norm_tile(xt, bxt, n, c0, to_out=None, defer=False):
        cl, bcl = col.next()
        ACT(junk[:n], xt[:n], AF.Square, r=[bxt], w=[bjunk, bcl], accum_out=cl[:n, 0:1])
        RSQ(cl, bcl, n, 0, 1, 2, 1.0 / D)
        if to_out is not None:
            yo, byo = xin.next()
            STT(yo[:n], xt[:n], cl[:n, 2:3], GT[:n], ALU.mult, ALU.mult, r=[bxt, bcl, bGT], w=[byo])
            DMA(to_out[0], yo[:n], r=[byo], w=[to_out[1]])
            return None
        yb, byb = ybf.next()
        STT(yb[:n], xt[:n], cl[:n, 2:3], GT[:n], ALU.mult, ALU.mult, r=[bxt, bcl, bGT], w=[byb])

        def fin():
            for kc in range(8):
                TRP(psb7[:, kc * 128:kc * 128 + n], yb[:n, kc * 128:(kc + 1) * 128], identb[:n, :n], r=[byb, bidb], w=[PB[7]])
            CP("act", HT[:, :, c0:c0 + n], psb7.rearrange("p (k t) -> p k t", k=8)[:, :, 0:n], r=[PB[7]], w=[bHT])
        if defer:
            return fin
        fin()
        return None

    DMA(identf[:], CI["ident"], w=[bidf])
    CP("dve", identb[:], identf[:], r=[bidf], w=[bidb])
    MSET("dve", onesb[:], 1.0, w=[bones])
    MSET("dve", epsc[:], EPS, w=[bepsc])
    cmask, bcm = sb("cmask", [64, 64], F32)
    DMA(cmask[:], CI["cmask"], w=[bcm])
    sel, bsel = sb("sel", [4, 512], F32)
    DMA(sel[:], CI["sel"], w=[bsel])
    lp, blp = sb("lp", [128, 640], F32)

    dbg = {}

    def phase_norm1(l, xsel):
        load_gamma(I["g_mix"][l])
        for (c0, n) in TILES:
            xt, bxt = xin.next()
            src, bsrc = xrows(xsel, c0, n)
            DMA(xt[:n], src, r=[bsrc] if bsrc else [], w=[bxt])
            norm_tile(xt, bxt, n, c0)

    def phase_A(l):
        AR.reset()
        wl = I["w_in"][l]
        qqT, bqq = AR.get("qqT", [128, 2, NT], BF16)
        kkT, bkk = AR.get("kkT", [128, 2, T + PAST + TS], BF16)
        Vaug, bV = AR.get("Vaug", [128, 41, 130], BF16)
        oaT, boa = AR.get("oaT", [128, NT], BF16)
        Wq, bWq = AR.get("Wq", [128, 8, 128], BF16)
        Wkv, bWkv = AR.get("Wkv", [128, 8, 256], BF16)
        kvf = Rot([AR.get("kvf%d" % i, [128, 256], F32) for i in range(2)])
        sbr = Rot([AR.get("sbA%d" % i, [128, 512], F32) for i in range(2)])
        ptr = Rot([AR.get("ptA%d" % i, [128, 512], BF16) for i in range(6)])
        o1, bo1 = AR.get("o1", [128, 4, 128], F32)
        ot = Rot([AR.get("otA%d" % i, [128, 128], F32) for i in range(8)])
        ob_ = Rot([AR.get("obA%d" % i, [128, 128], BF16) for i in range(4)])
        ckb, bckb = AR.get("ckb", [128, 8, 128], BF16)
        ahg, bahg = AR.get("ahg", [128, 128], F32)
        lam, blam = AR.get("lam", [128, 8], F32)
        cst, bcst = AR.get("cstA", [128, 2192], F32)
        DMA(cst[:, 0:2048], CI["adg"], w=[bcst])
        DMA(cst[:, 2048:2192], CI["acol"], w=[bcst])
        adg = cst[:, 0:2048].rearrange("p (h q) -> p h q", h=4)
        abase = adg
        acol = cst[:, 2048:2192].rearrange("p (h d) -> p h d", h=4)
        rqb, brqb = AR.get("rqb", [2, 4, 512], BF16)
        st_rq, bst_rq = stg.next()
        DMA(st_rq[0:2, 0:2048], CI["rq"], w=[bst_rq])
        CP("dve", rqb[:], st_rq[0:2, 0:2048].rearrange("p (h q) -> p h q", h=4), r=[bst_rq], w=[brqb])
        MSET("pool", Vaug[:, :, 128:129], 1.0, w=[bV])
        MSET("pool", kkT[64:66, :, :], 1.0, w=[bkk])
        lam_init = 0.8 - 0.6 * math.exp(-0.3 * l)
        for i, nm in enumerate(["a_lq1", "a_lk1", "a_lq2", "a_lk2"]):
            DMA(lp[:, i * 64:(i + 1) * 64], bcast_rows(I[nm].tensor, I[nm][l].offset, 64), w=[blp])
        TT("dve", lp[:, 256:320], lp[:, 0:64], lp[:, 64:128], ALU.mult, r=[blp], w=[blp])
        TT("dve", lp[:, 320:384], lp[:, 128:192], lp[:, 192:256], ALU.mult, r=[blp], w=[blp])
        ACT(junk[:, 0:64], lp[:, 256:320], AF.Identity, r=[blp], w=[bjunk, blam], accum_out=lam[:, 0:1])
        ACT(junk[:, 0:64], lp[:, 320:384], AF.Identity, r=[blp], w=[bjunk, blam], accum_out=lam[:, 1:2])
        ACT(lam[:, 4:6], lam[:, 0:2], AF.Exp, r=[blam], w=[blam])
        TT("dve", lam[:, 2:3], lam[:, 5:6], lam[:, 4:5], ALU.subtract, r=[blam], w=[blam])
        TS_("dve", lam[:, 3:4], lam[:, 2:3], -lam_init, None, ALU.add, r=[blam], w=[blam])
        DMA(ahg[:], bcast_rows(I["a_head_g"].tensor, I["a_head_g"][l].offset, 128), w=[bahg])
        TS_("dve", ahg[:], ahg[:], 1.0 - lam_init, None, ALU.mult, r=[bahg], w=[bahg])
        prj = Rot([(PS[i], PB[i]) for i in (4, 5, 6)])
        if debug:
            print("A pre-heads ninstr", S.ninstr)
        for h in range(4):
            load_w(Wq, bWq, wl, [(h * 64, 64), (256 + h * 64, 64)], 8)
            load_w(Wkv, bWkv, wl, [(512 + h * 64, 64), (768 + h * 64, 64), (1024 + h * 128, 128)], 8)
            for (c0, n) in BLOCKS:
                kc0 = c0 if c0 < T else T + PAST
                for m in range(2):
                    ps, bp = prj.next()
                    for kc in range(8):
                        PE(ps[0:64, 0:n], Wq[:, kc, m * 64:(m + 1) * 64], HT[:, kc, c0:c0 + n], kc == 0, kc == 7, r=[bWq, bHT], w=[bp])
                    CP("act", qqT[0:64, m, c0:c0 + n], ps[0:64, 0:n], r=[bp], w=[bqq])
                    ps, bp = prj.next()
                    for kc in range(8):
                        PE(ps[0:64, 0:n], Wkv[:, kc, m * 64:(m + 1) * 64], HT[:, kc, c0:c0 + n], kc == 0, kc == 7, r=[bWkv, bHT], w=[bp])
                    CP("dve", kkT[0:64, m, kc0:kc0 + n], ps[0:64, 0:n], r=[bp], w=[bkk])
            for m in range(2):
                for (c0, n) in BLOCKS:
                    DMA(qqT[64:66, m, c0:c0 + n], rqb[0:2, h, 0:n], r=[brqb], w=[bqq])
            if debug:
                print("A head", h, "pre-tokmajor ninstr", S.ninstr)
            for ti, (c0, n) in enumerate(TILES):
                ps, bp = prj.next()
                for kc in range(8):
                    PE(ps[:n, 0:256], HT[:, kc, c0:c0 + n], Wkv[:, kc, :], kc == 0, kc == 7, r=[bWkv, bHT], w=[bp])
                kf, bkf = kvf.next()
                CP("act", kf[:n], ps[:n, 0:256], r=[bp], w=[bkf])
                CP("pool", Vaug[:n, ti, 0:128], kf[:n, 128:256], r=[bkf], w=[bV])
                if c0 < T:
                    DMA(O["p_a_k"][l, c0:c0 + n, h * 128:(h + 1) * 128], kf[:n, 0:128], r=[bkf], w=[OB["p_a_k"]])
                    DMA(O["p_a_v"][l, c0:c0 + n, h * 128:(h + 1) * 128], kf[:n, 128:256], r=[bkf], w=[OB["p_a_v"]])
                else:
                    DMA(O["s_a_k"][l, 0:n, h * 128:(h + 1) * 128], kf[:n, 0:128], r=[bkf], w=[OB["s_a_k"]])
                    DMA(O["s_a_v"][l, 0:n, h * 128:(h + 1) * 128], kf[:n, 128:256], r=[bkf], w=[OB["s_a_v"]])
            st, bst = stg.next()
            sv = st[:, 0:1024].rearrange("p (t c) -> p t c", t=8)
            DMA(sv, I["cak"][l, :, h * 128:(h + 1) * 128].rearrange("(t p) c -> p t c", p=128), w=[bst])
            CP("pool", ckb[:], sv, r=[bst], w=[bckb])
            for m in range(2):
                for t8 in range(8):
                    TRP(psb7[0:64, t8 * 128:(t8 + 1) * 128], ckb[:, t8, m * 64:(m + 1) * 64], identb[:], r=[bckb, bidb], w=[PB[7]])
                CP("act", kkT[0:64, m, T:T + PAST], psb7[0:64, 0:1024], r=[PB[7]], w=[bkk])
            st, bst = stg.next()
            sv = st[:, 0:1024].rearrange("p (t c) -> p t c", t=8)
            DMA(sv, I["cav"][l, :, h * 128:(h + 1) * 128].rearrange("(t p) c -> p t c", p=128), w=[bst])
            CP("pool", Vaug[:, 33:41, 0:128], sv, r=[bst], w=[bV])
            if debug:
                print("A head", h, "pre-attn ninstr", S.ninstr)
            items = []
            prjA = Rot([(PS[i], PB[i]) for i in (2, 3, 4, 5, 6, 7)])
            for qb, (q0, qn) in enumerate(BLOCKS):
                sample = q0 >= T
                nsub = 1 if sample else 4
                if not sample:
                    kbl = []
                    for kb in range(4 * qb + 4):
                        j = kb - 4 * qb
                        if j < 0:
                            kbl.append(dict(kc=kb * 128, nk=128, vt=kb, lo=0, tile=abase, Dd=(q0 - kb * 128) // 128, last_sub=None))
                        else:
                            kbl.append(dict(kc=kb * 128, nk=128, vt=kb, lo=128 * j, tile=adg, Dd=0, last_sub=j))
                else:
                    kbl = [dict(kc=T + kb * 128, nk=128, vt=33 + kb, lo=0, tile=abase, Dd=8 - kb, last_sub=None) for kb in range(8)]
                    kbl.append(dict(kc=T + PAST, nk=TS, vt=32, lo=0, tile=adg, Dd=0, last_sub=0))
                for m in range(2):
                    for bi, kb in enumerate(kbl):
                        it = dict(kb)
                        it.update(q0=q0, qn=qn, m=m, bi=bi, nb=len(kbl), sample=sample, nsub=nsub)
                        items.append(it)

            def stage1(it):
                lo, nk, qn, q0, m = it["lo"], it["nk"], it["qn"], it["q0"], it["m"]
                ps, bp = prjA.next()
                offd = it["last_sub"] is None
                p_, bp_ = ptr.next()
                if offd:
                    PE(ps[:nk, lo:qn], kkT[0:66, m, it["kc"]:it["kc"] + nk], qqT[0:66, m, q0 + lo:q0 + qn], True, True,
                       r=[bkk, bqq], w=[bp])
                    ACT(p_[:nk, lo:qn], ps[:nk, lo:qn], AF.Exp, r=[bp, bcst], w=[bp_],
                        bias=acol[:nk, h, it["Dd"] + 3:it["Dd"] + 4], scale=0.125)
                else:
                    PE(ps[:nk, lo:qn], kkT[0:64, m, it["kc"]:it["kc"] + nk], qqT[0:64, m, q0 + lo:q0 + qn], True, True,
                       r=[bkk, bqq], w=[bp])
                    s_, bs_ = sbr.next()
                    STT(s_[:nk, lo:qn], ps[:nk, lo:qn], 0.125, it["tile"][:nk, h, 0:qn - lo], ALU.mult, ALU.add, r=[bp, bcst], w=[bs_])
                    ACT(p_[:nk, lo:qn], s_[:nk, lo:qn], AF.Exp, r=[bs_], w=[bp_])
                it["p"] = (p_, bp_)

            def stage2(it):
                lo, nk, qn, q0, m = it["lo"], it["nk"], it["qn"], it["q0"], it["m"]
                p_, bp_ = it["p"]
                for sub in range(lo // 128, it["nsub"]):
                    sn_ = min(128, qn - sub * 128)
                    if it["last_sub"] is None:
                        last = it["sample"] and (it["bi"] == it["nb"] - 1)
                    else:
                        last = it["last_sub"] == sub
                    bk_, co_ = sub // 2, (sub % 2) * 256
                    PE(PS[bk_][:sn_, co_:co_ + 129], p_[:nk, sub * 128:sub * 128 + sn_], Vaug[:nk, it["vt"], 0:129],
                       it["bi"] == 0 and sub % 2 == 0, last and (sub % 2 == 1 or it["nsub"] == 1), r=[bp_, bV], w=[PB[bk_]])
                if it["bi"] != it["nb"] - 1:
                    return
                ns_ = it["nsub"]
                snb = min(128, qn)
                clA, bclA = col.next()
                clB, bclB = col.next()
                outs = []
                for sub in range(ns_):
                    sn_ = min(128, qn - sub * 128)
                    cl, bcl = col.next()
                    bk_, co_ = sub // 2, (sub % 2) * 256
                    acc_ = PS[bk_][:sn_, co_:co_ + 129]
                    S.op("dve", lambda: nc.vector.reciprocal(out=cl[:sn_, 0:1], in_=acc_[:, 128:129]), reads=[PB[bk_]], writes=[bcl])
                    if m == 0:
                        TS_("dve", o1[:sn_, sub, :], acc_[:, 0:128], cl[:sn_, 0:1], None, ALU.mult, r=[PB[bk_], bcl], w=[bo1])
                    else:
                        TT("dve", cl[:sn_, 1:2], cl[:sn_, 0:1], lam[:sn_, 3:4], ALU.mult, r=[bcl, blam], w=[bcl])
                        o_, bo_ = ot.next()
                        STT(o_[:sn_], acc_[:, 0:128], cl[:sn_, 1:2], o1[:sn_, sub, :], ALU.mult, ALU.add,
                            r=[PB[bk_], bcl, bo1], w=[bo_])
                        ACT(junk[:sn_, 0:128], o_[:sn_], AF.Square, r=[bo_], w=[bjunk, bclA], accum_out=clA[:sn_, sub:sub + 1])
                        outs.append((sub, sn_, o_, bo_))
                if m == 1:
                    ACT(clB[:snb, 0:ns_], clA[:snb, 0:ns_], AF.Ln, r=[bclA, bepsc], w=[bclB], bias=epsc[:snb, 0:1], scale=1.0 / 128)
                    ACT(clB[:snb, 4:4 + ns_], clB[:snb, 0:ns_], AF.Exp, r=[bclB], w=[bclB], scale=-0.5)
                    it["tr"] = (outs, clB, bclB)

            def stage3(it):
                outs, clB, bclB = it["tr"]
                q0 = it["q0"]
                for (sub, sn_, o_, bo_) in outs:
                    ob1, bob1 = ob_.next()
                    STT(ob1[:sn_], o_[:sn_], clB[:sn_, 4 + sub:5 + sub], ahg[:sn_], ALU.mult, ALU.mult, r=[bo_, bclB, bahg], w=[bob1])
                    pt_, bpt_ = prjA.next()
                    ptb = pt_[:].bitcast(BF16)
                    TRP(ptb[:, 0:sn_], ob1[:sn_, :], identb[:sn_, :sn_], r=[bob1, bidb], w=[bpt_])
                    CP("act", oaT[:, q0 + sub * 128:q0 + sub * 128 + sn_], ptb[:, 0:sn_], r=[bpt_], w=[boa])

            LOOK = 5
            for i in range(min(LOOK, len(items))):
                stage1(items[i])
            pend = None
            for i, it in enumerate(items):
                stage2(it)
                if i + LOOK < len(items):
                    stage1(items[i + LOOK])
                if pend is not None:
                    stage3(pend)
                    pend = None
                if "tr" in it:
                    pend = it
            if pend is not None:
                stage3(pend)
            DMA(OAT[h], oaT[:], r=[boa], w=[bOAT])
        if debug:
            dbg["OAT"] = OAT

    def phase_B(l):
        AR.reset()
        wl = I["w_in"][l]
        qT, bq = AR.get("qTb", [128, NT], BF16)
        kT, bk = AR.get("kTb", [128, T + NBAND + TS], BF16)
        Vb, bV = AR.get("Vb", [128, 37, 2, 66], BF16)
        obT, bobT = AR.get("obT", [128, NT], BF16)
        Wq, bWq = AR.get("WqB", [128, 8, 128], BF16)
        Wkv, bWkv = AR.get("WkvB", [128, 8, 256], BF16)
        kvf = Rot([AR.get("kvfB%d" % i, [128, 256], F32) for i in range(2)])
        sbr = Rot([AR.get("sbB%d" % i, [128, 128], F32) for i in range(10)])
        ptr = Rot([AR.get("ptB%d" % i, [128, 128], BF16) for i in range(11)])
        obt = Rot([AR.get("obtB%d" % i, [128, 128], BF16) for i in range(6)])
        ckb, bckb = AR.get("ckbB", [128, 4, 128], BF16)
        T34, bT34 = AR.get("T34", [128, 8, 2, 128], F32)
        c256, bc256 = AR.get("c256", [128, 8], F32)
        tmp, btmp = AR.get("tmpB", [128, 128], F32)
        zt, bzt = AR.get("ztB", [128, 384], F32)
        cst, bcst = AR.get("cstB", [128, 512], F32)
        DMA(cst[:], CI["bmask"], w=[bcst])
        bmask = cst[:].rearrange("p (m q) -> p m q", m=4)
        MSET("pool", Vb[:, :, :, 64:65], 1.0, w=[bV])
        MSET("dve", zt[:], 0.0, w=[bzt])
        for hh in range(8):
            DMA(ZB[hh], zt[:], r=[bzt], w=[bZB])
        rel = I["b_rel"]
        for hh in range(8):
            DMA(ZB[hh, :, 0:257], bcast_rows(rel.tensor, rel[l, hh].offset, 257), r=[], w=[bZB])
        DMA(c256[:], bass.AP(tensor=rel.tensor, offset=rel[l, 0].offset + 256, ap=[[0, 128], [257, 8]]), w=[bc256],
            allow_slow_non_contiguous=True)
        for hh in range(8):
            zb = ZB[hh]
            DMA(T34[:, hh, 1, :], bass.AP(tensor=zb.tensor, offset=zb.offset + 128, ap=[[383, 128], [1, 128]]), r=[bZB], w=[bT34])
            DMA(T34[:, hh, 0, :], bass.AP(tensor=zb.tensor, offset=zb.offset + 256, ap=[[383, 128], [1, 128]]), r=[bZB], w=[bT34])
            TT("dve", T34[:, hh, 1, :], T34[:, hh, 1, :], bmask[:, 1, :], ALU.add, r=[bT34, bcst], w=[bT34])
            TT("dve", tmp[:], T34[:, hh, 0, :], bmask[:, 2, :], ALU.mult, r=[bT34, bcst], w=[btmp])
            STT(T34[:, hh, 0, :], bmask[:, 3, :], c256[:, hh:hh + 1], tmp[:], ALU.mult, ALU.add, r=[btmp, bcst, bc256], w=[bT34])
        prj = Rot([(PS[i], PB[i]) for i in (4, 5, 6)])
        acc = Rot([(PS[i], PB[i]) for i in (0, 1)])
        for hp in range(4):
            load_w(Wq, bWq, wl, [(1536 + hp * 128, 128)], 8)
            load_w(Wkv, bWkv, wl, [(2048 + hp * 128, 128), (2560 + hp * 128, 128)], 8)
            for (c0, n) in BLOCKS:
                ps, bp = prj.next()
                for kc in range(8):
                    PE(ps[:, 0:n], Wq[:, kc, :], HT[:, kc, c0:c0 + n], kc == 0, kc == 7, r=[bWq, bHT], w=[bp])
                CP("act", qT[:, c0:c0 + n], ps[:, 0:n], r=[bp], w=[bq])
                ps, bp = prj.next()
                for kc in range(8):
                    PE(ps[:, 0:n], Wkv[:, kc, 0:128], HT[:, kc, c0:c0 + n], kc == 0, kc == 7, r=[bWkv, bHT], w=[bp])
                kc0 = c0 if c0 < T else T + NBAND
                CP("dve", kT[:, kc0:kc0 + n], ps[:, 0:n], r=[bp], w=[bk])
            for ti, (c0, n) in enumerate(TILES):
                ps, bp = prj.next()
                for kc in range(8):
                    PE(ps[:n, 0:256], HT[:, kc, c0:c0 + n], Wkv[:, kc, :], kc == 0, kc == 7, r=[bWkv, bHT], w=[bp])
                kf, bkf = kvf.next()
                CP("act", kf[:n], ps[:n, 0:256], r=[bp], w=[bkf])
                CP("pool", Vb[:n, ti, :, 0:64], kf[:n, 128:256].rearrange("p (h d) -> p h d", h=2), r=[bkf], w=[bV])
                if c0 >= T - 512:
                    if c0 < T:
                        r_ = c0 - (T - 512)
                        DMA(O["p_b_k"][l, r_:r_ + n, hp * 128:(hp + 1) * 128], kf[:n, 0:128], r=[bkf], w=[OB["p_b_k"]])
                        DMA(O["p_b_v"][l, r_:r_ + n, hp * 128:(hp + 1) * 128], kf[:n, 128:256], r=[bkf], w=[OB["p_b_v"]])
                    else:
                        DMA(O["s_b_k"][l, 0:n, hp * 128:(hp + 1) * 128], kf[:n, 0:128], r=[bkf], w=[OB["s_b_k"]])
                        DMA(O["s_b_v"][l, 0:n, hp * 128:(hp + 1) * 128], kf[:n, 128:256], r=[bkf], w=[OB["s_b_v"]])
            st, bst = stg.next()
            sv = st[:, 0:512].rearrange("p (t c) -> p t c", t=4)
            DMA(sv, I["cbk"][l, :, hp * 128:(hp + 1) * 128].rearrange("(t p) c -> p t c", p=128), w=[bst])
            CP("pool", ckb[:], sv, r=[bst], w=[bckb])
            for t4 in range(4):
                TRP(psb7[:, t4 * 128:(t4 + 1) * 128], ckb[:, t4, :], identb[:], r=[bckb, bidb], w=[PB[7]])
            CP("act", kT[:, T:T + NBAND], psb7[:, 0:512], r=[PB[7]], w=[bk])
            st, bst = stg.next()
            sv = st[:, 0:512].rearrange("p (t h d) -> p t h d", t=4, h=2)
            DMA(st[:, 0:512].rearrange("p (t c) -> p t c", t=4),
                I["cbv"][l, :, hp * 128:(hp + 1) * 128].rearrange("(t p) c -> p t c", p=128), w=[bst])
            for t4 in range(4):
                CP("pool", Vb[:, 33 + t4, :, 0:64], sv[:, t4], r=[bst], w=[bV])
            items = []
            for qt, (q0, qn) in enumerate(TILES):
                sample = q0 >= T
                for hh in range(2):
                    if not sample:
                        kbl = []
                        for i in range(5):
                            kbi = qt - 4 + i
                            if kbi >= 0:
                                kbl.append(dict(kc=kbi * 128, nk=128, vt=kbi, kind=i))
                    else:
                        kbl = [dict(kc=T + j * 128, nk=128, vt=33 + j, kind=(3 if j == 3 else 1)) for j in range(4)]
                        kbl.append(dict(kc=T + NBAND, nk=TS, vt=32, kind=4))
                    for bi, kb in enumerate(kbl):
                        it = dict(kb)
                        it.update(q0=q0, qn=qn, hh=hh, bi=bi, nb=len(kbl), qt=qt)
                        items.append(it)
            state = {}
            sprj = Rot([(PS[b_], PB[b_]) for b_ in (2, 3, 4, 5, 6)])

            def stage1(it):
                nk, kind, qn, q0, hh = it["nk"], it["kind"], it["qn"], it["q0"], it["hh"]
                H = 2 * hp + hh
                r0 = 64 * hh
                ps, bp = sprj.next()
                PE(ps[:nk, 0:qn], kT[r0:r0 + 64, it["kc"]:it["kc"] + nk], qT[r0:r0 + 64, q0:q0 + qn], True, True, r=[bk, bq], w=[bp])
                p_, bp_ = ptr.next()
                if kind in (1, 2):
                    ACT(p_[:nk, 0:qn], ps[:nk, 0:qn], AF.Exp, r=[bp, bc256], w=[bp_], bias=c256[:nk, H:H + 1], scale=0.125)
                else:
                    s_, bs_ = sbr.next()
                    tile = bmask[:nk, 0, 0:qn] if kind == 0 else (T34[:nk, H, 0, 0:qn] if kind == 3 else T34[:nk, H, 1, 0:qn])
                    STT(s_[:nk, 0:qn], ps[:nk, 0:qn], 0.125, tile, ALU.mult, ALU.add, r=[bp, bcst, bT34], w=[bs_])
                    if kind == 0:
                        ACT(p_[:nk, 0:qn], s_[:nk, 0:qn], AF.Exp, r=[bs_, bc256], w=[bp_], bias=c256[:nk, H:H + 1], scale=1.0)
                    else:
                        ACT(p_[:nk, 0:qn], s_[:nk, 0:qn], AF.Exp, r=[bs_], w=[bp_])
                it["p"] = (p_, bp_)

            def stage2(it):
                nk, qn, hh = it["nk"], it["qn"], it["hh"]
                r0 = 64 * hh
                if it["bi"] == 0:
                    state["ac"] = acc.next()
                    if hh == 0:
                        state["ot"] = obt.next()
                ac, bac = state["ac"]
                o_t, bo_t = state["ot"]
                p_, bp_ = it["p"]
                PE(ac[:qn, 0:65], p_[:nk, 0:qn], Vb[:nk, it["vt"], hh, 0:65], it["bi"] == 0, it["bi"] == it["nb"] - 1, r=[bp_, bV], w=[bac])
                if it["bi"] != it["nb"] - 1:
                    return
                cl, bcl = col.next()
                S.op("dve", lambda: nc.vector.reciprocal(out=cl[:qn, 0:1], in_=ac[:qn, 64:65]), reads=[bac], writes=[bcl])
                TS_("dve", o_t[:qn, r0:r0 + 64], ac[:qn, 0:64], cl[:qn, 0:1], None, ALU.mult, r=[bac, bcl], w=[bo_t])
                if hh == 1:
                    it["tr"] = (o_t, bo_t)

            def stage3(it):
                o_t, bo_t = it["tr"]
                q0, qn = it["q0"], it["qn"]
                TRP(psb7[:, 0:qn], o_t[:qn, :], identb[:qn, :qn], r=[bo_t, bidb], w=[PB[7]])
                CP("act", obT[:, q0:q0 + qn], psb7[:, 0:qn], r=[PB[7]], w=[bobT])

            LOOK = 4
            for i in range(min(LOOK, len(items))):
                stage1(items[i])
            pend = None
            for i, it in enumerate(items):
                stage2(it)
                if i + LOOK < len(items):
                    stage1(items[i + LOOK])
                if pend is not None:
                    stage3(pend)
                    pend = None
                if "tr" in it:
                    pend = it
            if pend is not None:
                stage3(pend)
            DMA(OBT[hp], obT[:], r=[bobT], w=[bOBT])
        if debug:
            dbg["OBT"] = OBT

    def phase_C(l):
        AR.reset()
        wl = I["w_in"][l]
        WST, bWST = AR.get("WST", [64, 65, 8], F32)
        DECB, bDECB = AR.get("DECB", [128, 4, 65], F32)
        chg, bchg = AR.get("chg", [64, 128], F32)
        cw, bcw = AR.get("cw", [128, 8, 4], F32)
        cb, bcb = AR.get("cb", [128, 8], F32)
        gp, bgp = AR.get("gp", [4, 8], F32)
        scv, bscv = AR.get("scv", [4, D], F32)
        mark = AR.off
        prj = Rot([(PS[i], PB[i]) for i in (4, 5, 6)])
        DMA(gp[:, 0:1], bass.AP(tensor=I["c_b_i"].tensor, offset=I["c_b_i"][l].offset, ap=[[1, 4], [1, 1]]), w=[bgp])
        DMA(gp[:, 1:2], bass.AP(tensor=I["c_b_f"].tensor, offset=I["c_b_f"][l].offset, ap=[[1, 4], [1, 1]]), w=[bgp])
        DMA(gp[:, 4:5], bass.AP(tensor=I["sm"].tensor, offset=I["sm"][l].offset, ap=[[1, 4], [1, 1]]), w=[bgp])
        TS_("dve", gp[:, 1:2], gp[:, 1:2], -1.0, None, ALU.mult, r=[bgp], w=[bgp])
        MSET("dve", gp[:, 2:3], 1.0, w=[bgp])
        MSET("dve", gp[:, 3:4], math.log(128.0 ** -0.5), w=[bgp])
        DMA(chg[:], bcast_rows(I["c_head_g"].tensor, I["c_head_g"][l].offset, 128, parts=64), w=[bchg])
        cwl = I["c_conv_w"]
        for j in range(4):
            DMA(cw[:, :, j], bass.AP(tensor=cwl.tensor, offset=cwl[l, j].offset, ap=[[1, 128], [128, 8]]), w=[bcw],
                allow_slow_non_contiguous=True)
        cbl = I["c_conv_b"]
        DMA(cb[:], bass.AP(tensor=cbl.tensor, offset=cbl[l].offset, ap=[[1, 128], [128, 8]]), w=[bcb], allow_slow_non_contiguous=True)
        DMA(scv[0:3, :], I["sconv"][l], w=[bscv])
        A1, bA1 = AR.get("A1", [4, NT], F32)
        A2, bA2 = AR.get("A2", [4, NT], F32)
        A3, bA3 = AR.get("A3", [4, NT], F32)
        ONE, bONE = AR.get("ONE", [4, NT], BF16)
        CM, bCM = AR.get("CM", [4, 65], F32)
        MI, bMI = AR.get("MI", [4, 65], F32)
        ME, bME = AR.get("ME", [4, 65], F32)
        DEC, bDEC = AR.get("DEC", [4, 65], F32)
        Wif, bWif = AR.get("Wif", [128, 8, 8], BF16)
        load_w(Wif, bWif, wl, [(5120, 8)], 8)
        MSET("pool", ONE[:], 1.0, w=[bONE])
        for (c0, n) in BLOCKS:
            ps, bp = prj.next()
            for kc in range(8):
                PE(ps[0:4, 0:n], Wif[:, kc, 0:4], HT[:, kc, c0:c0 + n], kc == 0, kc == 7, r=[bWif, bHT], w=[bp])
            ACT(A1[:, c0:c0 + n], ps[0:4, 0:n], AF.Identity, r=[bp, bgp], w=[bA1], bias=gp[:, 0:1], scale=1.0)
            ps, bp = prj.next()
            for kc in range(8):
                PE(ps[0:4, 0:n], Wif[:, kc, 4:8], HT[:, kc, c0:c0 + n], kc == 0, kc == 7, r=[bWif, bHT], w=[bp])
            ACT(A2[:, c0:c0 + n], ps[0:4, 0:n], AF.Exp, r=[bp, bgp], w=[bA2], bias=gp[:, 1:2], scale=-1.0)
        ACT(A2[:], A2[:], AF.Ln, r=[bA2, bgp], w=[bA2], bias=gp[:, 2:3], scale=1.0)
        for (a, b_) in ((0, T), (T, NT)):
            S.op("dve", lambda: nc.vector.tensor_tensor_scan(out=A3[:, a:b_], data0=ONE[:, a:b_], data1=A2[:, a:b_], initial=0.0,
                                                             op0=ALU.mult, op1=ALU.add), reads=[bONE, bA2], writes=[bA3])
        TT("dve", A1[:], A1[:], A3[:], ALU.add, r=[bA1, bA3], w=[bA1])
        S.op("dve", lambda: nc.vector.tensor_reduce(out=CM[:, 0:64], in_=A1[:, 0:T].rearrange("p (c s) -> p c s", s=64), axis=AX.X, op=ALU.max),
             reads=[bA1], writes=[bCM])
        S.op("dve", lambda: nc.vector.tensor_reduce(out=CM[:, 64:65], in_=A1[:, T:NT], axis=AX.X, op=ALU.max), reads=[bA1], writes=[bCM])
        S.op("dve", lambda: nc.vector.tensor_tensor_scan(out=MI[:, 0:64], data0=ONE[:, 0:64], data1=CM[:, 0:64], initial=0.0,
                                                         op0=ALU.mult, op1=ALU.max), reads=[bONE, bCM], writes=[bMI])
        TT("dve", MI[:, 64:65], CM[:, 64:65], gp[:, 4:5], ALU.max, r=[bCM, bgp], w=[bMI])
        MSET("dve", ME[:, 0:1], 0.0, w=[bME])
        CP("dve", ME[:, 1:64], MI[:, 0:63], r=[bMI], w=[bME])
        CP("dve", ME[:, 64:65], gp[:, 4:5], r=[bgp], w=[bME])
        TT("dve", DEC[:], ME[:], MI[:], ALU.subtract, r=[bME, bMI], w=[bDEC])
        ACT(DEC[:], DEC[:], AF.Exp, r=[bDEC], w=[bDEC])
        TT("dve", gp[:, 5:6], MI[:, 63:64], A3[:, T - 1:T], ALU.subtract, r=[bMI, bA3], w=[bgp])
        TT("dve", gp[:, 6:7], MI[:, 64:65], A3[:, NT - 1:NT], ALU.subtract, r=[bMI, bA3], w=[bgp])
        mi_b = MI[:, 0:64].unsqueeze(2).to_broadcast([4, 64, 64])
        a1v = A1[:, 0:T].rearrange("p (c s) -> p c s", s=64)
        a3v = A3[:, 0:T].rearrange("p (c s) -> p c s", s=64)
        TT("dve", a1v, a1v, mi_b, ALU.subtract, r=[bA1, bMI], w=[bA1])
        TS_("dve", A1[:, T:NT], A1[:, T:NT], MI[:, 64:65], None, ALU.subtract, r=[bA1, bMI], w=[bA1])
        ACT(A1[:], A1[:], AF.Exp, r=[bA1, bgp], w=[bA1], bias=gp[:, 3:4], scale=1.0)
        TT("dve", a3v, a3v, mi_b, ALU.subtract, r=[bA3, bMI], w=[bA3])
        TS_("dve", A3[:, T:NT], A3[:, T:NT], MI[:, 64:65], None, ALU.subtract, r=[bA3, bMI], w=[bA3])
        ACT(A3[:], A3[:], AF.Exp, r=[bA3], w=[bA3])
        DMA(bass.AP(tensor=O["p_c_m"].tensor, offset=O["p_c_m"][l].offset, ap=[[1, 4], [1, 1]]), gp[:, 5:6], r=[bgp], w=[OB["p_c_m"]])
        DMA(bass.AP(tensor=O["s_c_m"].tensor, offset=O["s_c_m"][l].offset, ap=[[1, 4], [1, 1]]), gp[:, 6:7], r=[bgp], w=[OB["s_c_m"]])
        for c in range(65):
            L = 64 if c < 64 else TS
            a = c * 64
            if c < 64:
                o1_, o2_, bb = PS[0][:L, c * 8:c * 8 + 4], PS[0][:L, c * 8 + 4:c * 8 + 8], PB[0]
            else:
                o1_, o2_, bb = PS[1][:L, 0:4], PS[1][:L, 4:8], PB[1]
            PE(o1_, A1[:, a:a + L], identf[0:4, 0:4], True, True, r=[bA1, bidf], w=[bb])
            PE(o2_, A3[:, a:a + L], identf[0:4, 0:4], True, True, r=[bA3, bidf], w=[bb])
        CP("dve", WST[:, 0:64, :], PS[0][0:64, :].rearrange("p (c e) -> p c e", e=8), r=[PB[0]], w=[bWST])
        CP("dve", WST[0:TS, 64, :], PS[1][0:TS, 0:8], r=[PB[1]], w=[bWST])
        for h in range(4):
            PE(PS[2][:, h * 65:(h + 1) * 65], sel[0:4, h * 128:(h + 1) * 128], DEC[:, :], True, True, r=[bsel, bDEC], w=[PB[2]])
        CP("dve", DECB[:], PS[2][:, 0:260].rearrange("p (h c) -> p h c", h=4), r=[PB[2]], w=[bDECB])
        barrier()
        for hp2 in range(2):
            AR.off = mark
            QK, bQK = AR.get("QK", [128, 4, NT], BF16)
            Wc, bWc = AR.get("Wc", [128, 8, 128], BF16)
            Wv, bWv = AR.get("WvC", [128, 8, 256], BF16)
            Wo, bWo = AR.get("WoC", [128, 8, 256], BF16)
            CS, bCS = AR.get("CS", [128, 2, 130], F32)
            CSs, bCSs = AR.get("CSs", [128, 2, 130], F32)
            CDB, bCDB = AR.get("CDB", [128, 2, 130], BF16)
            CDBs, bCDBs = AR.get("CDBs", [128, 2, 130], BF16)
            cvr = Rot([AR.get("cvr%d" % i, [4, 128], F32) for i in range(4)])
            VA = Rot([AR.get("VA%d" % i, [64, 2, 130], BF16) for i in range(5)])
            ATr = Rot([AR.get("AT%d" % i, [64, 64], BF16) for i in range(10)])
            KPr = Rot([AR.get("KP%d" % i, [64, 128], BF16) for i in range(10)])
            OCB = Rot([AR.get("OCB%d" % i, [128, 2, 512], BF16) for i in range(2)])
            cf, bcf = AR.get("cf", [128, 2, 128], F32)
            conv_mark = AR.off
            xp, bxp = AR.get("xp", [128, T + 3], F32)
            u, bu = AR.get("u", [128, 2048], F32)
            xps, bxps = AR.get("xps", [128, TS + 3], F32)
            MSET("dve", xp[:, 0:3], 0.0, w=[bxp])
            for slot in range(4):
                cc = (2 * hp2 + slot) if slot < 2 else (4 + 2 * hp2 + slot - 2)
                load_w(Wc, bWc, wl, [(3072 + cc * 128, 128)], 8)
                for (c0, n) in BLOCKS:
                    ps, bp = prj.next()
                    for kc in range(8):
                        PE(ps[:, 0:n], Wc[:, kc, :], HT[:, kc, c0:c0 + n], kc == 0, kc == 7, r=[bWc, bHT], w=[bp])
                    if c0 < T:
                        CP("act", xp[:, 3 + c0:3 + c0 + n], ps[:, 0:n], r=[bp], w=[bxp])
                    else:
                        CP("act", xps[:, 3:3 + n], ps[:, 0:n], r=[bp], w=[bxps])
                for (t0, onm) in ((T - 3, "p_c_conv"), (NT - 3, "s_c_conv")):
                    ps, bp = prj.next()
                    for kc in range(8):
                        PE(ps[0:3, 0:128], HT[:, kc, t0:t0 + 3], Wc[:, kc, :], kc == 0, kc == 7, r=[bWc, bHT], w=[bp])
                    cv_, bcv_ = cvr.next()
                    CP("dve", cv_[0:3, :], ps[0:3, 0:128], r=[bp], w=[bcv_])
                    DMA(O[onm][l, :, cc * 128:(cc + 1) * 128], cv_[0:3, :], r=[bcv_], w=[OB[onm]])
                PE(PS[3][:, 0:3], scv[0:3, cc * 128:(cc + 1) * 128], identf[0:3, 0:3], True, True, r=[bscv, bidf], w=[PB[3]])
                CP("dve", xps[:, 0:3], PS[3][:, 0:3], r=[PB[3]], w=[bxps])
                for (xx, bxx, n, dcol) in ((xp, bxp, T, 0), (xps, bxps, TS, T)):
                    for hs in range(0, n, 2048):
                        hn = min(2048, n - hs)
                        TS_("dve", u[:, 0:hn], xx[:, hs:hs + hn], cw[:, cc, 0:1], cb[:, cc:cc + 1], ALU.mult, ALU.add,
                            r=[bxx, bcw, bcb], w=[bu])
                        for j in range(1, 4):
                            STT(u[:, 0:hn], xx[:, hs + j:hs + j + hn], cw[:, cc, j:j + 1], u[:, 0:hn], ALU.mult, ALU.add,
                                r=[bxx, bcw, bu], w=[bu])
                        ACT(QK[:, slot, dcol + hs:dcol + hs + hn], u[:, 0:hn], AF.Silu, r=[bu], w=[bQK])
            MSET("dve", CS[:], 0.0, w=[bCS])
            MSET("pool", CDB[:], 0.0, w=[bCDB])
            DMA(cf[:], I["sC"][l, 2 * hp2:2 * hp2 + 2].rearrange("h e d -> e h d"), w=[bcf])
            for hh in range(2):
                TRP(PS[3][:, 0:128], cf[:, hh, :], identf[:], r=[bcf, bidf], w=[PB[3]])
                CP("dve", CSs[:, hh, 0:128], PS[3][:, 0:128], r=[PB[3]], w=[bCSs])
            snl = I["sn"]
            DMA(CSs[:, :, 128], bass.AP(tensor=snl.tensor, offset=snl[l, 2 * hp2].offset, ap=[[1, 128], [128, 2]]), w=[bCSs],
                allow_slow_non_contiguous=True)
            for hh in range(2):
                H = 2 * hp2 + hh
                TS_("dve", CDBs[:, hh, 0:129], CSs[:, hh, 0:129], DECB[:, H, 64:65], None, ALU.mult, r=[bCSs, bDECB], w=[bCDBs])
            load_w(Wv, bWv, wl, [(4096 + hp2 * 256, 256)], 8)
            load_w(Wo, bWo, wl, [(4608 + hp2 * 256, 256)], 8)
            for it in VA.items:
                MSET("pool", it[0][:, :, 128:129], 1.0, w=[it[1]])
            barrier()
            AR.off = conv_mark
            HG2 = [AR.get("HG%d" % i, [64, 8, 256], F32) for i in range(2)]
            SQ, bSQ = AR.get("SQ", [64, 8, 256], BF16)
            OGG2 = [AR.get("OGG%d" % i, [64, 8, 256], BF16) for i in range(2)]
            OKG, bOKG = SQ, bSQ
            post = []
            ssr, bssr = AR.get("ssr", [64, 48], F32)
            if debug and hp2 == 0:
                print("C2 arena words used", AR.off, "of 24576; conv_mark", conv_mark)
            prj = Rot([(PS[i], PB[i]) for i in range(7)])
            ocbs = {"cur": OCB.next()}
            ctxs = {}

            def stageX(c):
                smp = c == 64
                L = TS if smp else 64
                a = c * 64
                j8 = c % 8
                OGG, bOGG = OGG2[(c // 8) % 2]
                va, bva = VA.next()
                ps, bp = prj.next()
                for kc in range(8):
                    PE(ps[:L, 0:256], HT[:, kc, a:a + L], Wv[:, kc, :], kc == 0, kc == 7, r=[bWv, bHT], w=[bp])
                CP("act", va[:L, :, 0:128], ps[:L, 0:256].rearrange("p (h e) -> p h e", h=2), r=[bp], w=[bva])
                ps, bp = prj.next()
                for kc in range(8):
                    PE(ps[:L, 0:256], HT[:, kc, a:a + L], Wo[:, kc, :], kc == 0, kc == 7, r=[bWo, bHT], w=[bp])
                ACT(OGG[:L, j8, :], ps[:L, 0:256], AF.Sigmoid, r=[bp], w=[bOGG])
                ats, kps = [], []
                for hh in range(2):
                    H = 2 * hp2 + hh
                    qs = QK[:, hh, a:a + L]
                    ks = QK[:, 2 + hh, a:a + L]
                    ps, bp = prj.next()
                    PE(ps[:L, 0:L], ks, qs, True, True, r=[bQK], w=[bp])
                    at, bat = ATr.next()
                    STT(at[:L, :L], ps[:L, 0:L], WST[:L, c, H:H + 1], cmask[:L, :L], ALU.mult, ALU.mult, r=[bp, bWST, bcm], w=[bat])
                    pk, bpk = prj.next()
                    pkb = pk[:].bitcast(BF16)
                    TRP(pkb[:L, 0:128], ks, identb[:], r=[bQK, bidb], w=[bpk])
                    kp, bkp = KPr.next()
                    ACT(kp[:L, :], pkb[:L, 0:128], AF.Identity, r=[bpk, bWST], w=[bkp], scale=WST[:L, c, H:H + 1])
                    ats.append((at, bat))
                    kps.append((kp, bkp))
                ctxs[c] = dict(va=(va, bva), ats=ats, kps=kps)

            def stageY(c):
                smp = c == 64
                L = TS if smp else 64
                a = c * 64
                j8 = c % 8
                OGG, bOGG = OGG2[(c // 8) % 2]
                ctx = ctxs.pop(c)
                HG, bHG = HG2[(c // 8) % 2]
                va, bva = ctx["va"]
                st_, bst_ = (CSs, bCSs) if smp else (CS, bCS)
                cd_, bcd_ = (CDBs, bCDBs) if smp else (CDB, bCDB)
                pn, bpn = prj.next()
                pnv = pn[:, 0:260].rearrange("p (h e) -> p h e", h=2)
                for hh in range(2):
                    qs = QK[:, hh, a:a + L]
                    at, bat = ctx["ats"][hh]
                    PE(pnv[:L, hh, 0:129], at[:L, :L], va[:L, hh, 0:129], True, False, r=[bat, bva], w=[bpn])
                    PE(pnv[:L, hh, 0:129], qs, cd_[:, hh, 0:129], False, True, r=[bQK, bcd_], w=[bpn])
                for hh in range(2):
                    H = 2 * hp2 + hh
                    kp, bkp = ctx["kps"][hh]
                    pst, bpst = prj.next()
                    PE(pst[:, 0:129], kp[:L, :], va[:L, hh, 0:129], True, True, r=[bkp, bva], w=[bpst])
                    STT(st_[:, hh, 0:129], st_[:, hh, 0:129], DECB[:, H, c:c + 1], pst[:, 0:129], ALU.mult, ALU.add, r=[bst_, bDECB, bpst], w=[bst_])
                if c < 63:
                    TT("pool", cd_[:, :, 0:129], st_[:, :, 0:129],
                       DECB[:, 2 * hp2:2 * hp2 + 2, c + 1:c + 2].to_broadcast([128, 2, 129]), ALU.mult, r=[bst_, bDECB], w=[bcd_])
                cl, bcl = col.next()
                CP("dve", cl[:L, 0:2], pnv[:L, :, 128], r=[bpn], w=[bcl])
                TS_("dve", cl[:L, 2:4], cl[:L, 0:2], -1.0, None, ALU.mult, r=[bcl], w=[bcl])
                TT("dve", cl[:L, 2:4], cl[:L, 2:4], cl[:L, 0:2], ALU.max, r=[bcl], w=[bcl])
                TT("dve", cl[:L, 2:4], cl[:L, 2:4], WST[:L, c, 4 + 2 * hp2:6 + 2 * hp2], ALU.max, r=[bcl, bWST], w=[bcl])
                S.op("dve", lambda: nc.vector.reciprocal(out=cl[:L, 4:6], in_=cl[:L, 2:4]), reads=[bcl], writes=[bcl])
                TT("dve", HG[:L, j8, :].rearrange("p (h e) -> p h e", h=2), pnv[:L, :, 0:128],
                   cl[:L, 4:6].unsqueeze(2).to_broadcast([L, 2, 128]), ALU.mult, r=[bpn, bcl], w=[bHG])
                if j8 == 7 or smp:
                    ng = 1 if smp else 8
                    n2 = ng * 2
                    hgv = HG[:L, 0:ng, :].rearrange("p j (h e) -> p (j h) e", h=2)
                    sqv = SQ[:L, 0:ng, :].rearrange("p j (h e) -> p (j h) e", h=2)
                    TT("pool", SQ[:L, 0:ng, :], HG[:L, 0:ng, :], HG[:L, 0:ng, :], ALU.mult, r=[bHG], w=[bSQ])
                    S.op("dve", lambda: nc.vector.tensor_reduce(out=ssr[:L, 0:n2], in_=sqv, axis=AX.X, op=ALU.add), reads=[bSQ], writes=[bssr])
                    TT("pool", hgv, hgv, chg[:L, :].unsqueeze(1).to_broadcast([L, n2, 128]), ALU.mult, r=[bHG, bchg], w=[bHG])
                    ACT(ssr[:L, 16:16 + n2], ssr[:L, 0:n2], AF.Ln, r=[bssr, bepsc], w=[bssr], bias=epsc[:L, 0:1], scale=1.0 / 128)
                    ACT(ssr[:L, 32:32 + n2], ssr[:L, 16:16 + n2], AF.Exp, r=[bssr], w=[bssr], scale=-0.5)

                    def second(c=c, smp=smp, L=L, ng=ng, n2=n2, HG=HG, bHG=bHG, OGG=OGG, bOGG=bOGG, hgv=hgv):
                        ocb, bocb = ocbs["cur"]
                        TT("dve", hgv, hgv, ssr[:L, 32:32 + n2].unsqueeze(2).to_broadcast([L, n2, 128]), ALU.mult, r=[bHG, bssr], w=[bHG])
                        TT("dve", OKG[:L, 0:ng, :], HG[:L, 0:ng, :], OGG[:L, 0:ng, :], ALU.mult, r=[bHG, bOGG], w=[bOKG])
                        for jj in range(ng):
                            for hh in range(2):
                                TRP(psb7[:, hh * 512 + jj * 64:hh * 512 + jj * 64 + L], OKG[:L, jj, hh * 128:(hh + 1) * 128], identb[:L, :L],
                                    r=[bOKG, bidb], w=[PB[7]])
                        nn = ng * 64 if not smp else TS
                        CP("act", ocb[:, :, 0:nn], psb7.rearrange("p (h t) -> p h t", h=2)[:, :, 0:nn], r=[PB[7]], w=[bocb])
                        c0 = (c // 8) * 512
                        DMA(OCT[2 * hp2:2 * hp2 + 2, :, c0:c0 + nn].rearrange("h p t -> p h t"), ocb[:, :, 0:nn], r=[bocb], w=[bOCT])
                        ocbs["cur"] = OCB.next()
                    post.append(second)
                if c == 63 or smp:
                    oC, oN = ("s_c_C", "s_c_n") if smp else ("p_c_C", "p_c_n")
                    for hh in range(2):
                        pt3, bpt3 = prj.next()
                        TRP(pt3[:, 0:128], st_[:, hh, 0:128], identf[:], r=[bst_, bidf], w=[bpt3])
                        CP("dve", cf[:, hh, :], pt3[:, 0:128], r=[bpt3], w=[bcf])
                    DMA(O[oC][l, 2 * hp2:2 * hp2 + 2].rearrange("h e d -> e h d"), cf[:], r=[bcf], w=[OB[oC]])
                    on = O[oN]
                    DMA(bass.AP(tensor=on.tensor, offset=on[l, 2 * hp2].offset, ap=[[1, 128], [128, 2]]), st_[:, :, 128], r=[bst_],
                        w=[OB[oN]], allow_slow_non_contiguous=True)

            CLOOK = 3
            for c in range(min(CLOOK, 65)):
                stageX(c)
            for c in range(65):
                if post and (c % 8 == 7 or c == 64):
                    post.pop(0)()
                had = len(post)
                stageY(c)
                if c + CLOOK < 65:
                    stageX(c + CLOOK)
                if had:
                    post.pop(0)()
            while post:
                post.pop(0)()
            barrier()
        if debug:
            dbg["OCT"] = OCT

    def phase_MIX(l):
        AR.reset()
        wl = I["w_in"][l]
        Wr, bWr = AR.get("Wr", [128, 8, D], BF16)
        pre = wr_pieces(I["w_o"][l], 8, Wr, bWr)
        Wg = [[AR.get("Wg%d_%d" % (fi, m), [128, 8, 128], BF16) for m in range(3)] for fi in range(4)]
        Wu = [[AR.get("Wu%d_%d" % (fi, m), [128, 4, 128], BF16) for m in range(3)] for fi in range(4)]
        ob3 = Rot([[AR.get("oblk%d_%d" % (i, m), [128, 4, 512], BF16) for m in range(3)] for i in range(2)])
        sg = Rot([AR.get("sg%d" % i, [128, 512], F32) for i in range(3)])
        mx = Rot([AR.get("mx%d" % i, [128, 512], F32) for i in range(2)])
        mo = Rot([AR.get("mo%d" % i, [128, 512], BF16) for i in range(2)])
        ups = [I["w_up_a"][l], I["w_up_b"][l], I["w_up_c"][l]]
        srcs = [(OAT, bOAT), (OBT, bOBT), (OCT, bOCT)]
        prj = Rot([(PS[i], PB[i]) for i in range(7)])
        for fh in range(2):
            for fi in range(4):
                f = fh * 4 + fi
                for m in range(3):
                    load_w(Wg[fi][m][0], Wg[fi][m][1], wl, [(5128 + m * 1024 + f * 128, 128)], 8)
                    load_w(Wu[fi][m][0], Wu[fi][m][1], ups[m], [(f * 128, 128)], 4)
            for (c0, n) in BLOCKS:
                if fh == 1 and pre:
                    pre.pop(0)()
                blk = ob3.next()
                for m in range(3):
                    DMA(blk[m][0][:, :, 0:n], srcs[m][0][:, :, c0:c0 + n].rearrange("h p t -> p h t"), r=[srcs[m][1]], w=[blk[m][1]])
                for fi in range(4):
                    f = fh * 4 + fi
                    mx_, bmx = mx.next()
                    for m in range(3):
                        pg, bpg = prj.next()
                        for kc in range(8):
                            PE(pg[:, 0:n], Wg[fi][m][0][:, kc, :], HT[:, kc, c0:c0 + n], kc == 0, kc == 7, r=[Wg[fi][m][1], bHT], w=[bpg])
                        s_, bs_ = sg.next()
                        ACT(s_[:, 0:n], pg[:, 0:n], AF.Sigmoid, r=[bpg], w=[bs_])
                        pu, bpu = prj.next()
                        for kc in range(4):
                            PE(pu[:, 0:n], Wu[fi][m][0][:, kc, :], blk[m][0][:, kc, 0:n], kc == 0, kc == 3, r=[Wu[fi][m][1], blk[m][1]], w=[bpu])
                        if m == 0:
                            TT("dve", mx_[:, 0:n], pu[:, 0:n], s_[:, 0:n], ALU.mult, r=[bpu, bs_], w=[bmx])
                        else:
                            TT("dve", s_[:, 0:n], pu[:, 0:n], s_[:, 0:n], ALU.mult, r=[bpu, bs_], w=[bs_])
                            if m == 1:
                                TT("pool", mx_[:, 0:n], mx_[:, 0:n], s_[:, 0:n], ALU.add, r=[bmx, bs_], w=[bmx])
                            else:
                                mo_, bmo = mo.next()
                                TT("pool", mo_[:, 0:n], mx_[:, 0:n], s_[:, 0:n], ALU.add, r=[bmx, bs_], w=[bmo])
                                put_fm(MIXT, f, c0, n, mo_[:, 0:n], bmo, bMIXT)
        while pre:
            pre.pop(0)()
        if debug:
            dbg["MIXT"] = MIXT

    def wr_pieces(w_dram, KC, Wr, bWr):
        out = []
        for kc0 in range(0, KC, 4):
            kk = min(4, KC - kc0)
            for half in range(2):
                def piece(kc0=kc0, kk=kk, half=half):
                    st, bst = stg.next()
                    sv = st[:, 0:kk * 512].rearrange("p (k n) -> p k n", k=kk)
                    DMA(sv, w_dram[kc0 * 128:(kc0 + kk) * 128, half * 512:(half + 1) * 512].rearrange("(k p) n -> p k n", p=128), w=[bst])
                    CP("pool", Wr[:, kc0:kc0 + kk, half * 512:(half + 1) * 512], sv, r=[bst], w=[bWr])
                out.append(piece)
        return out

    def phase_resid(l, w_dram, KC, srcT, bsrcT, xin_sel, xout_sel, gamma, final=False, tapname=None, preloaded=False):
        AR.reset()
        Wr, bWr = AR.get("Wr", [128, KC, D], BF16)
        for kc0 in (range(0, KC, 4) if not preloaded else []):
            kk = min(4, KC - kc0)
            for half in range(2):
                st, bst = stg.next()
                sv = st[:, 0:kk * 512].rearrange("p (k n) -> p k n", k=kk)
                DMA(sv, w_dram[kc0 * 128:(kc0 + kk) * 128, half * 512:(half + 1) * 512].rearrange("(k p) n -> p k n", p=128), w=[bst])
                CP("pool", Wr[:, kc0:kc0 + kk, half * 512:(half + 1) * 512], sv, r=[bst], w=[bWr])
        load_gamma(gamma)
        at = Rot([AR.get("at%d" % i, [128, KC, 128], BF16) for i in range(2)])
        xo = Rot([AR.get("xo%d" % i, [128, D], F32) for i in range(4)])
        prj = Rot([(PS[i], PB[i]) for i in range(6)])
        pend_fin = [None]
        for (c0, n) in TILES:
            a_, ba_ = at.next()
            ti_ = c0 // 128
            DMA(a_[:, :, 0:n], srcT[ti_, :, :, 0:n], r=[bsrcT], w=[ba_])
            xt, bxt = xin.next()
            src, bsrc = xrows(xin_sel, c0, n)
            DMA(xt[:n], src, r=[bsrc] if bsrc else [], w=[bxt], q="act")
            xo_, bxo = xo.next()
            for half in range(2):
                ps, bp = prj.next()
                for kc in range(KC):
                    PE(ps[:n, :], a_[:, kc, 0:n], Wr[:, kc, half * 512:(half + 1) * 512], kc == 0, kc == KC - 1, r=[ba_, bWr], w=[bp])
                TT("dve", xo_[:n, half * 512:(half + 1) * 512], ps[:n, :], xt[:n, half * 512:(half + 1) * 512], ALU.add,
                   r=[bp, bxt], w=[bxo])
            if len(pend_fin) > 2:
                pend_fin.pop(1)()
            if not final:
                dst, bdst = xrows(xout_sel, c0, n)
                DMA(dst, xo_[:n], r=[bxo], w=[bdst], q="act")
                pend_fin.append(norm_tile(xo_, bxo, n, c0, defer=True))
            else:
                if c0 < T:
                    norm_tile(xo_, bxo, n, c0, to_out=(O["y_p"][c0:c0 + n, :], OB["y_p"]))
                else:
                    norm_tile(xo_, bxo, n, c0, to_out=(O["y_s"][0:n, :], OB["y_s"]))
        for fn_ in pend_fin[1:]:
            fn_()

    def phase_CROSS(l):
        AR.reset()
        Wr, bWr = AR.get("Wr", [128, 8, D], BF16)
        pre = wr_pieces(I["w_mo"][l], 8, Wr, bWr)
        memT, bmemT = AR.get("memT", [128, 8, NMEM], BF16)
        mb, bmb = AR.get("mb", [128, 2, D], BF16)
        mkT = [AR.get("mkT%d" % g, [128, 8, NMEM], BF16) for g in range(2)]
        mvt = [AR.get("mvt%d" % g, [128, 2, D], BF16) for g in range(2)]
        Wm, bWm = AR.get("Wm", [128, 8, 512], BF16)
        qh, bqh = AR.get("qh", [128, 2, NT], BF16)
        oh, boh = AR.get("oh", [128, 2, NT], BF16)
        ptr = Rot([AR.get("ptC%d" % i, [128, 512], BF16) for i in range(4)])
        rcp = Rot([AR.get("rcp%d" % i, [128, 512], F32) for i in range(2)])
        mf = Rot([AR.get("mf%d" % i, [128, 512], F32) for i in range(2)])
        prj = Rot([(PS[i], PB[i]) for i in range(7)])
        for mt in range(2):
            xt, bxt = xin.next()
            DMA(xt[:], I["memp"][mt * 128:(mt + 1) * 128, :], w=[bxt])
            CP("dve", mb[:, mt, :], xt[:], r=[bxt], w=[bmb])
            for kc in range(8):
                TRP(psb7[:, kc * 128:(kc + 1) * 128], mb[:, mt, kc * 128:(kc + 1) * 128], identb[:], r=[bmb, bidb], w=[PB[7]])
            CP("act", memT[:, :, mt * 128:(mt + 1) * 128], psb7.rearrange("p (k t) -> p k t", k=8), r=[PB[7]], w=[bmemT])
        for which, wd, onm in ((0, I["w_mk"][l], "p_mem_k"), (1, I["w_mv"][l], "p_mem_v")):
            for half in range(2):
                load_w(Wm, bWm, wd, [(half * 512, 512)], 8)
                for mt in range(2):
                    ps, bp = prj.next()
                    for kc in range(8):
                        PE(ps[:, :], memT[:, kc, mt * 128:(mt + 1) * 128], Wm[:, kc, :], kc == 0, kc == 7, r=[bmemT, bWm], w=[bp])
                    m_, bm_ = mf.next()
                    CP("act", m_[:], ps[:, :], r=[bp], w=[bm_])
                    DMA(O[onm][l, mt * 128:(mt + 1) * 128, half * 512:(half + 1) * 512], m_[:], r=[bm_], w=[OB[onm]])
                    if which == 1:
                        CP("pool", mvt[0][0][:, mt, half * 512:(half + 1) * 512], m_[:], r=[bm_], w=[mvt[0][1]])
                if which == 0:
                    for j in range(4):
                        ps, bp = prj.next()
                        for kc in range(8):
                            PE(ps[:, 0:NMEM], Wm[:, kc, j * 128:(j + 1) * 128], memT[:, kc, :], kc == 0, kc == 7, r=[bmemT, bWm], w=[bp])
                        CP("dve", mkT[0][0][:, half * 4 + j, :], ps[:, 0:NMEM], r=[bp], w=[mkT[0][1]])
        for mt in range(2):
            xt, bxt = xin.next()
            DMA(xt[:], I["cmk"][l, mt * 128:(mt + 1) * 128, :], w=[bxt])
            CP("dve", mb[:, mt, :], xt[:], r=[bxt], w=[bmb])
            for kc in range(8):
                TRP(psb7[:, kc * 128:(kc + 1) * 128], mb[:, mt, kc * 128:(kc + 1) * 128], identb[:], r=[bmb, bidb], w=[PB[7]])
            CP("act", mkT[1][0][:, :, mt * 128:(mt + 1) * 128], psb7.rearrange("p (k t) -> p k t", k=8), r=[PB[7]], w=[mkT[1][1]])
            xt, bxt = xin.next()
            DMA(xt[:], I["cmv"][l, mt * 128:(mt + 1) * 128, :], w=[bxt])
            CP("dve", mvt[1][0][:, mt, :], xt[:], r=[bxt], w=[mvt[1][1]])
        for h in range(4):
            if pre:
                pre.pop(0)()
            for half2 in range(1):
                load_w(Wm[:, :, 0:256], bWm, I["w_mq"][l], [(h * 256, 256)], 8)
            for (c0, n) in BLOCKS:
                for dc in range(2):
                    ps, bp = prj.next()
                    for kc in range(8):
                        PE(ps[:, 0:n], Wm[:, kc, dc * 128:(dc + 1) * 128], HT[:, kc, c0:c0 + n], kc == 0, kc == 7, r=[bWm, bHT], w=[bp])
                    CP("act", qh[:, dc, c0:c0 + n], ps[:, 0:n], r=[bp], w=[bqh])
            def c_stage1(c0, n):
                g = 0 if c0 < T else 1
                pts = []
                for mt in range(2):
                    ps, bp = prj.next()
                    for dc in range(2):
                        PE(ps[:, 0:n], mkT[g][0][:, h * 2 + dc, mt * 128:(mt + 1) * 128], qh[:, dc, c0:c0 + n], dc == 0, dc == 1,
                           r=[mkT[g][1], bqh], w=[bp])
                    p_, bp_ = ptr.next()
                    ACT(p_[:, 0:n], ps[:, 0:n], AF.Exp, r=[bp], w=[bp_], scale=1.0 / 16.0)
                    pts.append((p_, bp_))
                return pts

            def c_stage2(c0, n, pts):
                g = 0 if c0 < T else 1
                psu, bpsu = prj.next()
                for mt in range(2):
                    PE(psu[:, 0:n], onesb[:, :], pts[mt][0][:, 0:n], mt == 0, mt == 1, r=[bones, pts[mt][1]], w=[bpsu])
                rc, brc = rcp.next()
                S.op("dve", lambda: nc.vector.reciprocal(out=rc[:, 0:n], in_=psu[:, 0:n]), reads=[bpsu], writes=[brc])
                for ec in range(2):
                    po, bpo = prj.next()
                    for mt in range(2):
                        PE(po[:, 0:n], mvt[g][0][:, mt, h * 256 + ec * 128:h * 256 + (ec + 1) * 128], pts[mt][0][:, 0:n], mt == 0, mt == 1,
                           r=[mvt[g][1], pts[mt][1]], w=[bpo])
                    TT("dve", oh[:, ec, c0:c0 + n], po[:, 0:n], rc[:, 0:n], ALU.mult, r=[bpo, brc], w=[boh])

            nxt = c_stage1(*BLOCKS[0])
            for bi_, (c0, n) in enumerate(BLOCKS):
                cur_pts = nxt
                if bi_ + 1 < len(BLOCKS):
                    nxt = c_stage1(*BLOCKS[bi_ + 1])
                c_stage2(c0, n, cur_pts)
            for ec in range(2):
                put_fm(CROT, 2 * h + ec, 0, T, oh[:, ec, 0:T], boh, bCROT, q="sp")
                put_fm(CROT, 2 * h + ec, T, TS, oh[:, ec, T:NT], boh, bCROT, q="sp")
        while pre:
            pre.pop(0)()
        if debug:
            dbg["CROT"] = CROT

    def phase_FFNU(l):
        AR.reset()
        Wr, bWr = AR.get("Wr", [128, 22, D], BF16)
        pre = wr_pieces(I["w_ff_d"][l], 22, Wr, bWr)
        Wg, bWg = AR.get("Wfg", [128, 8, 128], BF16)
        Wu, bWu = AR.get("Wfu", [128, 8, 128], BF16)
        sg = Rot([AR.get("fsg%d" % i, [128, 512], F32) for i in range(3)])
        ao = Rot([AR.get("fao%d" % i, [128, 512], BF16) for i in range(3)])
        prj = Rot([(PS[i], PB[i]) for i in range(7)])
        for f in range(22):
            load_w(Wg, bWg, I["w_ff_g"][l], [(f * 128, 128)], 8)
            load_w(Wu, bWu, I["w_ff_u"][l], [(f * 128, 128)], 8)
            if f >= 2 and f % 2 == 0 and pre:
                pre.pop(0)()
                if f >= 12 and pre:
                    pre.pop(0)()
            for (c0, n) in BLOCKS:
                pg, bpg = prj.next()
                for kc in range(8):
                    PE(pg[:, 0:n], Wg[:, kc, :], HT[:, kc, c0:c0 + n], kc == 0, kc == 7, r=[bWg, bHT], w=[bpg])
                s_, bs_ = sg.next()
                ACT(s_[:, 0:n], pg[:, 0:n], AF.Silu, r=[bpg], w=[bs_])
                pu, bpu = prj.next()
                for kc in range(8):
                    PE(pu[:, 0:n], Wu[:, kc, :], HT[:, kc, c0:c0 + n], kc == 0, kc == 7, r=[bWu, bHT], w=[bpu])
                a_, ba_ = ao.next()
                TT("dve", a_[:, 0:n], pu[:, 0:n], s_[:, 0:n], ALU.mult, r=[bpu, bs_], w=[ba_])
                put_fm(ACTT, f, c0, n, a_[:, 0:n], ba_, bACTT)
        while pre:
            pre.pop(0)()

    phases = []
    cur = "in"
    other = {"in": "A", "A": "B", "B": "A"}
    for l in range(n_layers):
        if l == 0:
            phases.append(("norm1", lambda l=l: phase_norm1(l, "in")))
        phases.append(("A%d" % l, lambda l=l: phase_A(l)))
        phases.append(("B%d" % l, lambda l=l: phase_B(l)))
        phases.append(("C%d" % l, lambda l=l: phase_C(l)))
        phases.append(("MIX%d" % l, lambda l=l: phase_MIX(l)))
        xi, xo = cur, other[cur]
        phases.append(("WO%d" % l, lambda l=l, xi=xi, xo=xo: phase_resid(l, I["w_o"][l], 8, MIXT, bMIXT, xi, xo, I["g_cross"][l], preloaded=True)))
        cur = xo
        phases.append(("CROSS%d" % l, lambda l=l: phase_CROSS(l)))
        xi, xo = cur, other[cur]
        phases.append(("WMO%d" % l, lambda l=l, xi=xi, xo=xo: phase_resid(l, I["w_mo"][l], 8, CROT, bCROT, xi, xo, I["g_ffn"][l], preloaded=True)))
        cur = xo
        phases.append(("FFNU%d" % l, lambda l=l: phase_FFNU(l)))
        xi, xo = cur, other[cur]
        if l < n_layers - 1:
            phases.append(("FFND%d" % l, lambda l=l, xi=xi, xo=xo: phase_resid(l, I["w_ff_d"][l], 22, ACTT, bACTT, xi, xo, I["g_mix"][l + 1], preloaded=True)))
        else:
            phases.append(("FFND%d" % l, lambda l=l, xi=xi, xo=xo: phase_resid(l, I["w_ff_d"][l], 22, ACTT, bACTT, xi, xo, I["g_final"], final=True, preloaded=True)))
        cur = xo
    for name, fn in phases:
        fn()
        barrier()
        if stop_after == name:
            break
    barrier()
    return nc, S, dbg


_PROG = {}


def _prep_inputs(inputs):
    f = lambda a: np.ascontiguousarray(a, dtype=np.float32)
    consts = make_consts()
    maps = []
    shared = {}
    for k in ("g_mix", "w_in", "a_lq1", "a_lk1", "a_lq2", "a_lk2", "a_head_g", "b_rel", "c_conv_w", "c_conv_b", "c_b_i", "c_b_f",
              "c_head_g", "w_up_a", "w_up_b", "w_up_c", "w_o", "g_cross", "w_mq", "w_mk", "w_mv", "w_mo", "g_ffn", "w_ff_g",
              "w_ff_u", "w_ff_d", "g_final"):
        shared[k] = f(inputs[k])
    for k, v in consts.items():
        shared["c_" + k] = f(v)
    for b in range(8):
        m = dict(shared)
        m["x_p"] = f(inputs["x_prompt"][b])
        m["x_s"] = f(inputs["x_sample"][b])
        m["cak"] = f(inputs["cache_a_k"][:, b].reshape(2, PAST, 512))
        m["cav"] = f(inputs["cache_a_v"][:, b].reshape(2, PAST, 512))
        m["cbk"] = f(inputs["cache_b_k"][:, b].reshape(2, NBAND, 512))
        m["cbv"] = f(inputs["cache_b_v"][:, b].reshape(2, NBAND, 512))
        m["sC"] = f(inputs["state_c_C"][:, b])
        m["sn"] = f(inputs["state_c_n"][:, b])
        m["sm"] = f(inputs["state_c_m"][:, b])
        m["sconv"] = f(inputs["state_c_conv"][:, b])
        m["cmk"] = f(inputs["cache_mem_k"][:, b].reshape(2, NMEM, D))
        m["cmv"] = f(inputs["cache_mem_v"][:, b].reshape(2, NMEM, D))
        m["memp"] = f(inputs["mem_prompt"][b])
        maps.append(m)
    return maps


def kernel(**inputs):
    if "nc" not in _PROG:
        _PROG["nc"] = build_program()[0]
    nc = _PROG["nc"]
    maps = _prep_inputs(inputs)
    res = run_bass_kernel_spmd(nc, maps, core_ids=list(range(8)))
    R = res.results
    st = lambda k: np.stack([np.asarray(R[b][k]) for b in range(8)], axis=0)
    st1 = lambda k: np.stack([np.asarray(R[b][k]) for b in range(8)], axis=1)
    out = {}
    out["y_p"] = st("y_p")
    out["y_s"] = st("y_s")
    out["p_a_k"] = st1("p_a_k").reshape(2, 8, T, 4, 128)
    out["p_a_v"] = st1("p_a_v").reshape(2, 8, T, 4, 128)
    out["p_b_k"] = st1("p_b_k").reshape(2, 8, 512, 8, 64)
    out["p_b_v"] = st1("p_b_v").reshape(2, 8, 512, 8, 64)
    out["p_c_C"] = st1("p_c_C")
    out["p_c_n"] = st1("p_c_n")
    out["p_c_m"] = st1("p_c_m")
    out["p_c_conv"] = st1("p_c_conv")
    out["p_mem_k"] = st1("p_mem_k").reshape(2, 8, NMEM, 4, 256)
    out["p_mem_v"] = st1("p_mem_v").reshape(2, 8, NMEM, 4, 256)
    out["s_a_k"] = st1("s_a_k").reshape(2, 8, TS, 4, 128)
    out["s_a_v"] = st1("s_a_v").reshape(2, 8, TS, 4, 128)
    out["s_b_k"] = st1("s_b_k").reshape(2, 8, TS, 8, 64)
    out["s_b_v"] = st1("s_b_v").reshape(2, 8, TS, 8, 64)
    out["s_c_C"] = st1("s_c_C")
    out["s_c_n"] = st1("s_c_n")
    out["s_c_m"] = st1("s_c_m")
    out["s_c_conv"] = st1("s_c_conv")
    return tuple(np.ascontiguousarray(out[k], dtype=np.float32) for k in OUT_ORDER)
```
